# Optimizing a Trainium2 kernel written in Bass

```python
import jax, jax.numpy as jnp
from jax import lax
import numpy as np

D_MODEL = 1024
BATCH = 4
SEQ = 8192
DEPTH = 2

GRID_W = 64
CTX_LEN = 256
D_MIX = D_MODEL
SSD_WIDTH = D_MIX // 2
SSD_HEADDIM = 64
SSD_HEADS = SSD_WIDTH // SSD_HEADDIM
SSD_GROUPS = 2
HEADS_PER_GROUP = SSD_HEADS // SSD_GROUPS
SSD_STATE = 128
SSD_CONV = 3
SSD_CHUNK = 128
XBC_WIDTH = SSD_WIDTH + 2 * SSD_GROUPS * SSD_STATE
FNET_WIDTH = D_MIX // 4
FNET_GROUPS = 4
FNET_GDIM = FNET_WIDTH // FNET_GROUPS
POOL_WINDOWS = (2, 4, 8, 16)
POOL_WIDTH = D_MIX // 4
POOL_GDIM = POOL_WIDTH // len(POOL_WINDOWS)
D_FF = 256 * ((8 * D_MODEL // 3 + 255) // 256)
FFN_CONV = 3
OFF_Z = 0
OFF_XBC = OFF_Z + SSD_WIDTH
OFF_DT = OFF_XBC + XBC_WIDTH
OFF_FNET = OFF_DT + 2 * SSD_HEADS
OFF_POOL = OFF_FNET + FNET_WIDTH
N_IN = OFF_POOL + POOL_WIDTH
ALPHA = (2.0 * DEPTH) ** 0.25
BETA = (8.0 * DEPTH) ** -0.25
EPS = 1e-6

kernel_name = "hybrid_ssd_fourier_pool_convffn_prefix"


def layer_norm(x):
    xf = x.astype(jnp.float32)
    mu = jnp.mean(xf, -1, keepdims=True)
    var = jnp.mean(jnp.square(xf - mu), -1, keepdims=True)
    return ((xf - mu) * lax.rsqrt(var + EPS)).astype(x.dtype)


def ada(cvec, w, b):
    m = jax.nn.silu(cvec) @ w + b
    return [mi[:, None, :] for mi in jnp.split(m, 6, axis=-1)]


def modulate(x, shift, scale):
    return layer_norm(x) * (1 + scale) + shift


def post_norm(x, y, gate, g, b):
    return layer_norm(ALPHA * x + gate * y) * g + b


def dwconv1d(x, w, b):
    k = w.shape[0]
    left = (k - 1) // 2
    y = lax.conv_general_dilated(x, w[:, None, :], window_strides=(1,), padding=[(left, k - 1 - left)],
                                 dimension_numbers=("NWC", "WIO", "NWC"), feature_group_count=x.shape[-1])
    return y + b


def dwconv2d_grid(x, w, b):
    bsz, L, C = x.shape
    rows = L // GRID_W
    xg = x.reshape(bsz, rows, GRID_W, C)
    y = lax.conv_general_dilated(xg, w[:, :, None, :], window_strides=(1, 1), padding=[(1, 1), (1, 1)],
                                 dimension_numbers=("NHWC", "HWIO", "NHWC"), feature_group_count=C)
    return y.reshape(bsz, L, C) + b


def segsum(a):
    T = a.shape[-1]
    cs = jnp.cumsum(a, axis=-1)
    seg = cs[..., :, None] - cs[..., None, :]
    mask = jnp.tril(jnp.ones((T, T), dtype=bool))
    return jnp.where(mask, seg, -jnp.inf)


def ssd_chunked(X, dA, B, C, init_state):
    b, L, h, p = X.shape
    n = B.shape[-1]
    nc = L // SSD_CHUNK
    X = X.astype(jnp.float32).reshape(b, nc, SSD_CHUNK, h, p)
    B = B.astype(jnp.float32).reshape(b, nc, SSD_CHUNK, h, n)
    C = C.astype(jnp.float32).reshape(b, nc, SSD_CHUNK, h, n)
    A = dA.astype(jnp.float32).reshape(b, nc, SSD_CHUNK, h).transpose(0, 3, 1, 2)
    A_cs = jnp.cumsum(A, axis=-1)
    Lmat = jnp.exp(segsum(A))
    y_diag = jnp.einsum("bclhn,bcshn,bhcls,bcshp->bclhp", C, B, Lmat, X)
    decay_states = jnp.exp(A_cs[..., -1:] - A_cs)
    states = jnp.einsum("bclhn,bhcl,bclhp->bchpn", B, decay_states, X)
    states = jnp.concatenate([init_state[:, None].astype(jnp.float32), states], axis=1)
    chunk_decay = jnp.exp(segsum(jnp.pad(A_cs[..., -1], ((0, 0), (0, 0), (1, 0)))))
    new_states = jnp.einsum("bhzc,bchpn->bzhpn", chunk_decay, states)
    states, final_state = new_states[:, :-1], new_states[:, -1]
    y_off = jnp.einsum("bclhn,bchpn,bhcl->bclhp", C, states, jnp.exp(A_cs))
    return (y_diag + y_off).reshape(b, L, h, p), final_state


def ssd_final_state(X, dA, B):
    cs = jnp.cumsum(dA, axis=1)
    return jnp.einsum("blh,blhn,blhp->bhpn", jnp.exp(cs[:, -1:] - cs), B.astype(jnp.float32), X)


def rev(a):
    return jnp.flip(a, axis=1)


def ssd_prep(u, conv_w, conv_b, dt_bias, a_log):
    bsz, L, _ = u.shape
    xbc = jax.nn.silu(dwconv1d(u[..., OFF_XBC:OFF_DT], conv_w, conv_b))
    xs = xbc[..., :SSD_WIDTH].reshape(bsz, L, SSD_HEADS, SSD_HEADDIM)
    gn = SSD_GROUPS * SSD_STATE
    bm = xbc[..., SSD_WIDTH:SSD_WIDTH + gn].reshape(bsz, L, SSD_GROUPS, SSD_STATE)
    cm = xbc[..., SSD_WIDTH + gn:].reshape(bsz, L, SSD_GROUPS, SSD_STATE)
    bm = jnp.repeat(bm, HEADS_PER_GROUP, axis=2)
    cm = jnp.repeat(cm, HEADS_PER_GROUP, axis=2)
    dt_raw = u[..., OFF_DT:OFF_FNET].reshape(bsz, L, 2, SSD_HEADS)
    dt = jax.nn.softplus((dt_raw + dt_bias).astype(jnp.float32))
    dA = dt * (-jnp.exp(a_log.astype(jnp.float32)))
    return xs, bm, cm, dt, dA


def ssd_bidir(xs, bm, cm, dt, dA, init_f, init_b):
    x32 = xs.astype(jnp.float32)
    y_f, s_f = ssd_chunked(x32 * dt[:, :, 0, :, None], dA[:, :, 0], bm, cm, init_f)
    y_b, s_b = ssd_chunked(rev(x32 * dt[:, :, 1, :, None]), rev(dA[:, :, 1]), rev(bm), rev(cm), init_b)
    return y_f + rev(y_b), s_f, s_b


def ssd_gate_norm(y, xs, z, d_skip, norm_w):
    bsz, L = z.shape[:2]
    y = (y + xs.astype(jnp.float32) * d_skip.astype(jnp.float32)[:, None]).reshape(bsz, L, SSD_WIDTH)
    y = (y * jax.nn.silu(z.astype(jnp.float32))).reshape(bsz, L, SSD_GROUPS, SSD_WIDTH // SSD_GROUPS)
    y = (y * lax.rsqrt(jnp.mean(y * y, -1, keepdims=True) + EPS)).reshape(bsz, L, SSD_WIDTH)
    return (y * norm_w).astype(z.dtype)


def fourier_mix(f, w):
    bsz, L, _ = f.shape
    fg = f.astype(jnp.float32).reshape(bsz, L, FNET_GROUPS, FNET_GDIM)
    spec = jnp.fft.fft2(fg, axes=(1, 3), norm="ortho").real
    y = jnp.einsum("blgc,gcd->blgd", spec.astype(f.dtype), w)
    return y.reshape(bsz, L, FNET_WIDTH)


def pool_mix(p, w, scale):
    bsz, L, _ = p.shape
    pf = p.astype(jnp.float32)
    cs = jnp.pad(jnp.cumsum(pf, axis=1), ((0, 0), (1, 0), (0, 0)))
    pos = jnp.arange(L)
    outs = []
    for gi, win in enumerate(POOL_WINDOWS):
        left = win // 2
        right = win - 1 - left
        hi = jnp.minimum(pos + right + 1, L)
        lo = jnp.maximum(pos - left, 0)
        sl = slice(gi * POOL_GDIM, (gi + 1) * POOL_GDIM)
        csg = cs[..., sl]
        mean = (csg[:, hi] - csg[:, lo]) / (hi - lo).astype(jnp.float32)[None, :, None]
        outs.append(mean - pf[..., sl])
    pooled = jnp.stack(outs, axis=2).astype(p.dtype)
    y = jnp.einsum("blgc,gcd->blgd", pooled, w).reshape(bsz, L, POOL_WIDTH)
    return y * scale


def token_mixer(u, init_f, init_b, conv_w, conv_b, dt_bias, a_log, d_skip, norm_w, fnet_w, pool_w, pool_scale, w_out):
    xs, bm, cm, dt, dA = ssd_prep(u, conv_w, conv_b, dt_bias, a_log)
    y, s_f, s_b = ssd_bidir(xs, bm, cm, dt, dA, init_f, init_b)
    y_ssd = ssd_gate_norm(y, xs, u[..., OFF_Z:OFF_XBC], d_skip, norm_w)
    y_fnet = fourier_mix(u[..., OFF_FNET:OFF_POOL], fnet_w)
    y_pool = pool_mix(u[..., OFF_POOL:N_IN], pool_w, pool_scale)
    return jnp.concatenate([y_ssd, y_fnet, y_pool], axis=-1) @ w_out, s_f, s_b


def ctx_scan_states(u, conv_w, conv_b, dt_bias, a_log):
    xs, bm, _, dt, dA = ssd_prep(u, conv_w, conv_b, dt_bias, a_log)
    x32 = xs.astype(jnp.float32)
    s_f = ssd_final_state(x32 * dt[:, :, 0, :, None], dA[:, :, 0], bm)
    s_b = ssd_final_state(rev(x32 * dt[:, :, 1, :, None]), rev(dA[:, :, 1]), rev(bm))
    return s_f, s_b


def conv_ffn(h, w_up, conv_w, conv_b, w_down, grid):
    a = h @ w_up
    a = dwconv2d_grid(a, conv_w, conv_b) if grid else dwconv1d(a, conv_w[1], conv_b)
    val, gate = jnp.split(a, 2, axis=-1)
    return (jax.nn.gelu(gate) * val) @ w_down


def setup_inputs(seed: int = 0) -> dict:
    key = jax.random.key(seed)
    ks = jax.random.split(key, 25)
    f32 = jnp.float32

    def nrm(k, shape, s):
        return s * jax.random.normal(k, shape, f32)

    dt0 = jnp.exp(jax.random.uniform(ks[9], (DEPTH, 2, SSD_HEADS), f32, np.log(1e-3), np.log(1e-1)))
    return {
        "x": nrm(ks[0], (BATCH, SEQ, D_MODEL), 1.0),
        "c": nrm(ks[1], (BATCH, D_MODEL), 1.0),
        "ctx": nrm(ks[2], (BATCH, CTX_LEN, D_MODEL), 1.0),
        "c_ctx": nrm(ks[3], (D_MODEL,), 1.0),
        "w_ada": nrm(ks[4], (DEPTH, D_MODEL, 6 * D_MODEL), 0.5 * D_MODEL ** -0.5),
        "b_ada": nrm(ks[5], (DEPTH, 6 * D_MODEL), 0.02),
        "w_in": nrm(ks[6], (DEPTH, D_MODEL, N_IN), D_MODEL ** -0.5),
        "ssd_conv_w": nrm(ks[7], (DEPTH, SSD_CONV, XBC_WIDTH), SSD_CONV ** -0.5),
        "ssd_conv_b": nrm(ks[8], (DEPTH, XBC_WIDTH), 0.02),
        "ssd_dt_bias": dt0 + jnp.log(-jnp.expm1(-dt0)),
        "ssd_a_log": jnp.log(jax.random.uniform(ks[10], (DEPTH, 2, SSD_HEADS), f32, 1.0, 16.0)),
        "ssd_d": 1.0 + nrm(ks[11], (DEPTH, SSD_HEADS), 0.1),
        "ssd_norm_w": 1.0 + nrm(ks[12], (DEPTH, SSD_WIDTH), 0.1),
        "fnet_w": nrm(ks[13], (DEPTH, FNET_GROUPS, FNET_GDIM, FNET_GDIM), FNET_GDIM ** -0.5),
        "pool_w": nrm(ks[14], (DEPTH, len(POOL_WINDOWS), POOL_GDIM, POOL_GDIM), POOL_GDIM ** -0.5),
        "pool_scale": 1.0 + nrm(ks[15], (DEPTH, POOL_WIDTH), 0.1),
        "w_out": nrm(ks[16], (DEPTH, D_MIX, D_MODEL), BETA * D_MIX ** -0.5),
        "ln1_g": 1.0 + nrm(ks[17], (DEPTH, D_MODEL), 0.1),
        "ln1_b": nrm(ks[18], (DEPTH, D_MODEL), 0.02),
        "ffn_w_up": nrm(ks[19], (DEPTH, D_MODEL, 2 * D_FF), D_MODEL ** -0.5),
        "ffn_conv_w": nrm(ks[20], (DEPTH, FFN_CONV, FFN_CONV, 2 * D_FF), 1.0 / FFN_CONV),
        "ffn_conv_b": nrm(ks[21], (DEPTH, 2 * D_FF), 0.02),
        "ffn_w_down": nrm(ks[22], (DEPTH, D_FF, D_MODEL), BETA * D_FF ** -0.5),
        "ln2_g": 1.0 + nrm(ks[23], (DEPTH, D_MODEL), 0.1),
        "ln2_b": nrm(ks[24], (DEPTH, D_MODEL), 0.02),
    }


def reference(x, c, ctx, c_ctx, w_ada, b_ada, w_in, ssd_conv_w, ssd_conv_b, ssd_dt_bias, ssd_a_log, ssd_d,
              ssd_norm_w, fnet_w, pool_w, pool_scale, w_out, ln1_g, ln1_b, ffn_w_up, ffn_conv_w, ffn_conv_b,
              ffn_w_down, ln2_g, ln2_b):
    bsz = x.shape[0]
    zero_state = jnp.zeros((bsz, SSD_HEADS, SSD_HEADDIM, SSD_STATE), jnp.float32)
    for l in range(DEPTH):
        last = l == DEPTH - 1
        sh1, sc1, g1, sh2, sc2, g2 = ada(c, w_ada[l], b_ada[l])
        csh1, csc1, cg1, csh2, csc2, cg2 = ada(c_ctx[None, :], w_ada[l], b_ada[l])
        mixer_p = (ssd_conv_w[l], ssd_conv_b[l], ssd_dt_bias[l], ssd_a_log[l], ssd_d[l], ssd_norm_w[l],
                   fnet_w[l], pool_w[l], pool_scale[l], w_out[l])
        u_ctx = modulate(ctx, csh1, csc1) @ w_in[l]
        u_lat = modulate(x, sh1, sc1) @ w_in[l]
        if last:
            s_f, s_b = ctx_scan_states(u_ctx, ssd_conv_w[l], ssd_conv_b[l], ssd_dt_bias[l], ssd_a_log[l])
        else:
            mix_ctx, s_f, s_b = token_mixer(u_ctx, zero_state, zero_state, *mixer_p)
        mix_lat, _, _ = token_mixer(u_lat, s_f, s_b, *mixer_p)
        x = post_norm(x, mix_lat, g1, ln1_g[l], ln1_b[l])
        ffn_lat = conv_ffn(modulate(x, sh2, sc2), ffn_w_up[l], ffn_conv_w[l], ffn_conv_b[l], ffn_w_down[l], True)
        x = post_norm(x, ffn_lat, g2, ln2_g[l], ln2_b[l])
        if not last:
            ctx = post_norm(ctx, mix_ctx, cg1, ln1_g[l], ln1_b[l])
            ffn_ctx = conv_ffn(modulate(ctx, csh2, csc2), ffn_w_up[l], ffn_conv_w[l], ffn_conv_b[l], ffn_w_down[l], False)
            ctx = post_norm(ctx, ffn_ctx, cg2, ln2_g[l], ln2_b[l])
    return x
```

```python
import contextlib
import math
import numpy as np
import concourse.bass as bass
import concourse.mybir as mybir
from concourse.bass_utils import run_bass_kernel_spmd

F32 = mybir.dt.float32
BF16 = mybir.dt.bfloat16
AF = mybir.ActivationFunctionType
ALU = mybir.AluOpType

D = 1024
SEQ = 8192
CTXL = 256
DEPTH = 2
DFF = 2816
NIN = 2064
OFF_Z, OFF_XBC, OFF_DT, OFF_FNET, OFF_POOL = 0, 512, 1536, 1552, 1808
ALPHA = (2.0 * DEPTH) ** 0.25
EPS = 1e-6
GRID_W = 64
NCORES = 4
WCOLS = 2320
C_DT, C_POOL, C_FN = 1536, 1552, 1808
NEGBIG = -30000.0
EPOCH = 30000


class Buf:
    __slots__ = ("lw", "rd")

    def __init__(self):
        self.lw = None
        self.rd = []


class Sched:
    ENG = ["tensor", "vector", "scalar", "gpsimd", "sync"]

    def __init__(self, nc, stack):
        self.nc = nc
        self.stack = stack
        self.ops = {e: [] for e in self.ENG}
        self.cnt = {e: 0 for e in self.ENG}
        self.sems = {}
        self.waited = {e: {} for e in self.ENG}
        self.same = {"vector", "scalar", "gpsimd"}
        self.dcnt = {}
        self.dma_rr = {e: 0 for e in self.ENG}
        self.NDMA = 8
        self.last_ev = {}

    def sem(self, key):
        if key not in self.sems:
            self.sems[key] = self.stack.enter_context(self.nc.semaphore("s_%s_%s_%s" % key))
        return self.sems[key]

    def _waits(self, eng, reads, writes):
        deps = {}

        def add(ev):
            if ev is None:
                return
            k, v = ev
            if deps.get(k, 0) < v:
                deps[k] = v
        for b in reads:
            add(b.lw)
        for b in writes:
            add(b.lw)
            for r in b.rd:
                add(r)
        out = []
        for k, v in deps.items():
            if k[0] == eng and k[2] == "c" and eng not in self.same:
                continue
            if self.waited[eng].get(k, 0) >= v:
                continue
            self.waited[eng][k] = v
            out.append((self.sem(k), v))
        return out

    def _commit(self, ev, reads, writes):
        for b in reads:
            b.rd.append(ev)
            if len(b.rd) > 24:
                best = {}
                for k, v in b.rd:
                    if best.get(k, 0) < v:
                        best[k] = v
                b.rd = list(best.items())
        for b in writes:
            b.lw = ev
            b.rd = []
        self.last_ev[ev[0]] = ev[1]

    def op(self, eng, fn, reads=(), writes=()):
        waits = self._waits(eng, reads, writes)
        c = self.cnt[eng]
        self.cnt[eng] = c + 1
        key = (eng, c // EPOCH, "c")
        val = c % EPOCH + 1
        s = self.sem(key)

        def run(h, waits=waits, fn=fn, s=s):
            for (ws, wv) in waits:
                h.wait_ge(ws, wv)
            fn(h).then_inc(s, 1)
        self.ops[eng].append(run)
        self._commit((key, val), reads, writes)

    def dma(self, out, in_, reads=(), writes=(), eng="sync"):
        waits = self._waits(eng, reads, writes)
        i = self.dma_rr[eng]
        self.dma_rr[eng] = (i + 1) % self.NDMA
        key = (eng, i, "d")
        n = self.dcnt.get(key, 0) + 1
        self.dcnt[key] = n
        s = self.sem(key)
        prev = (n - 1) * 16
        if self.waited[eng].get(key, 0) < prev:
            self.waited[eng][key] = prev
        else:
            prev = 0

        def run(h, waits=waits, s=s, prev=prev, out=out, in_=in_):
            for (ws, wv) in waits:
                h.wait_ge(ws, wv)
            if prev > 0:
                h.wait_ge(s, prev)
            h.dma_start(out=out, in_=in_).then_inc(s, 16)
        self.ops[eng].append(run)
        self._commit((key, n * 16), reads, writes)

    def barrier(self):
        evs = dict(self.last_ev)
        for eng in self.ENG:
            waits = []
            for k, v in evs.items():
                if k[0] == eng and k[2] == "c":
                    continue
                if self.waited[eng].get(k, 0) >= v:
                    continue
                self.waited[eng][k] = v
                waits.append((self.sem(k), v))

            def run(h, waits=waits):
                for (ws, wv) in waits:
                    h.wait_ge(ws, wv)
            self.ops[eng].append(run)

    def finish(self, block):
        self.barrier()
        ops = self.ops

        @block.tensor
        def _(h):
            for f in ops["tensor"]:
                f(h)

        @block.vector
        def _(h):
            for f in ops["vector"]:
                f(h)

        @block.scalar
        def _(h):
            for f in ops["scalar"]:
                f(h)

        @block.gpsimd
        def _(h):
            for f in ops["gpsimd"]:
                f(h)

        @block.sync
        def _(h):
            for f in ops["sync"]:
                f(h)


class T:
    __slots__ = ("t", "b")

    def __init__(self, t):
        self.t = t
        self.b = Buf()

    def __getitem__(self, k):
        return self.t[k]


def _consts():
    c = {}
    c["ident"] = np.eye(128, dtype=np.float32)
    k = np.arange(128)
    U = (k[:, None] <= k[None, :]).astype(np.float32)
    Us = (k[:, None] < k[None, :]).astype(np.float32)
    c["tri"] = np.stack([U, -U, Us, np.ones((128, 128), np.float32)], 1).astype(np.float32)
    upat = np.zeros((128, 16, 128), np.float32)
    upat[:, 0:8, :] = U[:, None, :]
    upat[:, 8:16, :] = -Us[:, None, :]
    c["upat"] = upat
    neg = np.zeros((128, 16, 128), np.float32)
    neg[:, 0:8, :] = np.where(k[:, None] > k[None, :], NEGBIG, 0.0)[:, None, :]
    neg[:, 8:16, :] = np.where(k[:, None] < k[None, :], NEGBIG, 0.0)[:, None, :]
    c["negm"] = neg
    m = np.arange(64)
    ang = 2 * np.pi * np.outer(m, m) / 64.0
    nrm = 1.0 / math.sqrt(SEQ * 64.0)
    cc = np.cos(ang) * nrm
    sc = -np.sin(ang) * nrm
    c["chdft"] = np.stack([np.concatenate([cc, cc], 1), np.concatenate([sc, sc], 1)], 1).astype(np.float32)
    a128 = 2 * np.pi * np.outer(k, k) / 128.0
    C, S = np.cos(a128), np.sin(a128)
    c["dftA"] = np.stack([np.concatenate([C, -S], 1), np.concatenate([S, C], 1)], 1).astype(np.float32)
    l2 = np.arange(64)
    th = 2 * np.pi * np.outer(l2, k) / 8192.0
    c["twid"] = np.stack([np.cos(th), np.sin(th), -np.sin(th)], 1).astype(np.float32)
    a64 = 2 * np.pi * np.outer(l2, l2) / 64.0
    c["dftB"] = np.stack([np.cos(a64), np.sin(a64)], 1).astype(np.float32)
    kk = np.arange(256)
    a256 = 2 * np.pi * np.outer(kk, kk) / 256.0
    sc256 = math.sqrt(SEQ / CTXL)
    c256 = (np.cos(a256) * sc256).reshape(2, 128, 256).transpose(1, 0, 2)
    s256 = (np.sin(a256) * sc256).reshape(2, 128, 256).transpose(1, 0, 2)
    c["dftC"] = np.stack([c256, s256], 1).astype(np.float32)
    pt = np.zeros((128, 5, 4, 128), np.float32)
    for gi, win in enumerate((2, 4, 8, 16)):
        left = win // 2
        right = win - 1 - left
        for l in range(128):
            for rel, (off, first, last) in enumerate([(-128, False, False), (0, False, False), (128, False, False),
                                                      (0, True, False), (0, False, True)]):
                lo = l - left
                hi = l + right + 1
                if first:
                    lo = max(lo, 0)
                if last:
                    hi = min(hi, 128)
                cnt = hi - lo
                for s in range(lo, hi):
                    sl = s - off
                    if 0 <= sl < 128:
                        pt[sl, rel, gi, l] += 1.0 / cnt
                if off == 0:
                    pt[l, rel, gi, l] -= 1.0
    c["poolm"] = pt
    return c


_CONST = None


def _host_inputs(inp, b):
    global _CONST
    if _CONST is None:
        _CONST = _consts()
    f = np.float32
    A = np.ascontiguousarray
    d = dict(_CONST)
    d["x"] = A(inp["x"][b], dtype=f)
    d["ctx"] = A(inp["ctx"][b], dtype=f)
    cc = np.stack([np.asarray(inp["c"][b], f), np.asarray(inp["c_ctx"], f)], 1)
    d["cc"] = A(cc.reshape(8, 128, 2).transpose(1, 0, 2))
    d["w_ada"] = A(inp["w_ada"], dtype=f)
    d["b_ada"] = A(np.asarray(inp["b_ada"], f).reshape(DEPTH, 1, 6 * D))
    w_in = np.asarray(inp["w_in"], f)
    main = np.concatenate([w_in[:, :, OFF_Z:OFF_DT], w_in[:, :, OFF_DT:OFF_FNET]], 2)
    d["w_in_r"] = A(main.reshape(DEPTH, 8, 128, 1552).transpose(0, 2, 1, 3))
    wf = w_in[:, :, OFF_FNET:OFF_POOL]
    d["w_fT"] = A(wf.transpose(0, 2, 1).reshape(DEPTH, 2, 128, D).transpose(0, 2, 1, 3))
    wp = w_in[:, :, OFF_POOL:NIN]
    d["w_pT"] = A(wp.transpose(0, 2, 1).reshape(DEPTH, 2, 128, D).transpose(0, 2, 1, 3))
    d["fnet_w"] = A(np.asarray(inp["fnet_w"], f).transpose(0, 2, 1, 3))
    d["pool_w"] = A(inp["pool_w"], dtype=f)
    d["ssd_cw"] = A(np.asarray(inp["ssd_conv_w"], f).reshape(DEPTH, 3, 8, 128).transpose(0, 3, 2, 1))
    d["ssd_cb"] = A(np.asarray(inp["ssd_conv_b"], f).reshape(DEPTH, 8, 128).transpose(0, 2, 1))
    d["dt_bias"] = A(np.asarray(inp["ssd_dt_bias"], f).reshape(DEPTH, 1, 16))
    d["a_log"] = A(np.asarray(inp["ssd_a_log"], f).reshape(DEPTH, 1, 16))
    d["d_rep"] = A(np.repeat(np.asarray(inp["ssd_d"], f), 64, axis=1).reshape(DEPTH, 1, 512))
    rs = np.concatenate([np.asarray(inp["ssd_norm_w"], f), np.ones((DEPTH, 256), f),
                         np.asarray(inp["pool_scale"], f)], 1)
    d["rowscale"] = A(rs.reshape(DEPTH, 8, 128).transpose(0, 2, 1))
    d["w_out"] = A(np.asarray(inp["w_out"], f).reshape(DEPTH, 8, 128, D).transpose(0, 2, 1, 3))
    for n in ("ln1_g", "ln1_b", "ln2_g", "ln2_b"):
        d[n] = A(np.asarray(inp[n], f).reshape(DEPTH, 1, D))
    d["w_up"] = A(np.asarray(inp["ffn_w_up"], f).reshape(DEPTH, 8, 128, 2 * DFF).transpose(0, 2, 1, 3))
    d["w_down"] = A(np.asarray(inp["ffn_w_down"], f).reshape(DEPTH, 22, 128, D).transpose(0, 2, 1, 3))
    d["ffn_cw"] = A(np.asarray(inp["ffn_conv_w"], f).reshape(DEPTH, 9, 44, 128).transpose(0, 3, 2, 1))
    d["ffn_cb"] = A(np.asarray(inp["ffn_conv_b"], f).reshape(DEPTH, 44, 128).transpose(0, 2, 1))
    return d


_IN_SHAPES = {
    "x": [SEQ, D], "ctx": [CTXL, D], "cc": [128, 8, 2], "w_ada": [DEPTH, D, 6 * D], "b_ada": [DEPTH, 1, 6 * D],
    "w_in_r": [DEPTH, 128, 8, 1552], "w_fT": [DEPTH, 128, 2, D], "w_pT": [DEPTH, 128, 2, D],
    "fnet_w": [DEPTH, 64, 4, 64], "pool_w": [DEPTH, 4, 64, 64], "ssd_cw": [DEPTH, 128, 8, 3],
    "ssd_cb": [DEPTH, 128, 8], "dt_bias": [DEPTH, 1, 16], "a_log": [DEPTH, 1, 16], "d_rep": [DEPTH, 1, 512],
    "rowscale": [DEPTH, 128, 8], "w_out": [DEPTH, 128, 8, D], "ln1_g": [DEPTH, 1, D], "ln1_b": [DEPTH, 1, D],
    "ln2_g": [DEPTH, 1, D], "ln2_b": [DEPTH, 1, D], "w_up": [DEPTH, 128, 8, 2 * DFF],
    "w_down": [DEPTH, 128, 22, D], "ffn_cw": [DEPTH, 128, 44, 9], "ffn_cb": [DEPTH, 128, 44],
    "ident": [128, 128], "tri": [128, 4, 128], "upat": [128, 16, 128], "negm": [128, 16, 128],
    "chdft": [64, 2, 128], "dftA": [128, 2, 256], "twid": [64, 3, 128], "dftB": [64, 2, 64],
    "dftC": [128, 2, 2, 256], "poolm": [128, 5, 4, 128],
}


class Seq:
    def __init__(self, name, L, row, toff):
        self.name = name
        self.L = L
        self.nt = L // 128
        self.row = row
        self.toff = toff
        self.scr = {}


class Prog:
    def __init__(self, dbg=(), nlayers=DEPTH, stop=None):
        self.dbg = set(dbg)
        self.nlayers = nlayers
        self.stop = stop
        self.nc = bass.Bass("TRN2", target_bir_lowering=False)
        nc = self.nc
        self.din = {n: nc.dram_tensor(n, s, F32, kind="ExternalInput").ap() for n, s in _IN_SHAPES.items()}
        self.out = nc.dram_tensor("y", [SEQ, D], F32, kind="ExternalOutput").ap()
        self.dbg_outs = {}

    def dram(self, name, shape, dt):
        kind = "ExternalOutput" if name in self.dbg else "Internal"
        t = self.nc.dram_tensor(name, shape, dt, kind=kind)
        if name in self.dbg:
            self.dbg_outs[name] = (shape, dt)
        return T(t.ap())

    def sb(self, st, name, shape, dt=F32):
        self._n += 1
        return T(st.enter_context(self.nc.sbuf_tensor("%s_%d" % (name, self._n), shape, dt)))

    def ps(self, st, name, shape, dt=F32):
        self._n += 1
        return T(st.enter_context(self.nc.psum_tensor("%s_%d" % (name, self._n), shape, dt)))

    def V(self, fn, r=(), w=()):
        self.S.op("vector", fn, [t.b for t in r], [t.b for t in w])

    def G(self, fn, r=(), w=()):
        self.S.op("gpsimd", fn, [t.b for t in r], [t.b for t in w])

    def A(self, fn, r=(), w=()):
        self.S.op("scalar", fn, [t.b for t in r], [t.b for t in w])

    def P(self, fn, r=(), w=()):
        self.S.op("tensor", fn, [t.b for t in r], [t.b for t in w])

    def dma(self, out, in_, r=(), w=()):
        self.S.dma(out, in_, [t.b for t in r], [t.b for t in w])

    def mm(self, out_t, out_ap, lhsT, rhs, start, stop, r):
        self.P(lambda h: h.matmul(out_ap, lhsT=lhsT, rhs=rhs, start=start, stop=stop), r, [out_t])

    def cast_load(self, st, name, src_ap, shape, dt=BF16, eng="V"):
        tmp = self.sb(st, name + "_f", shape, F32)
        dst = self.sb(st, name, shape, dt)
        self.dma(tmp[:], src_ap, w=[tmp])
        (self.V if eng == "V" else self.G)(lambda h: h.tensor_copy(out=dst[:], in_=tmp[:]), [tmp], [dst])
        return dst

    def rstd(self, var_ap, out_t, out_ap, r, scale=1.0):
        self.A(lambda h: h.activation(out=out_ap, in_=var_ap, func=AF.Ln, bias=self.epsT[:], scale=scale),
               list(r) + [self.epsT], [out_t])
        self.A(lambda h: h.activation(out=out_ap, in_=out_ap, func=AF.Exp, scale=-0.5), [out_t], [out_t])

    def layernorm(self, st, src_t, src_ap, dst_t, dst_ap, tag, small):
        st6, mv, rs = small
        for hh in range(2):
            self.V(lambda h, hh=hh: h.bn_stats(out=st6[:, hh, :], in_=src_ap[:, hh * 512:(hh + 1) * 512]), [src_t], [st6])
        self.V(lambda h: h.bn_aggr(out=mv[:], in_=st6[:].rearrange("p a b -> p (a b)")), [st6], [mv])
        self.rstd(mv[:, 1:2], rs, rs[:], [mv])
        self.V(lambda h: h.tensor_scalar(out=dst_ap, in0=src_ap, scalar1=mv[:, 0:1], scalar2=rs[:, 0:1],
                                         op0=ALU.subtract, op1=ALU.mult), [src_t, mv, rs], [dst_t])

    def build(self):
        nc = self.nc
        self._n = 0
        with contextlib.ExitStack() as gst:
            self.S = Sched(nc, gst)
            S = self.S
            din = self.din
            lat = Seq("lat", SEQ, 0, 0)
            ctx = Seq("ctx", CTXL, 1, SEQ // 128)
            for sq in (lat, ctx):
                L = sq.L
                n = sq.name
                sq.scr = {
                    "SZ": self.dram("SZ_" + n, [L, 512], BF16),
                    "XBCT": self.dram("XBCT_" + n, [D, L], BF16),
                    "UF": self.dram("UF_" + n, [L, 512], BF16),
                    "MIX": self.dram("MIX_" + n, [L, 1024], BF16),
                    "YP": self.dram("YP_" + n, [L, 512], BF16),
                    "XMID": self.dram("XMID_" + n, [L, D], F32),
                    "H2T": self.dram("H2T_" + n, [D, L], BF16),
                    "X1": self.dram("X1_" + n, [L, D], F32),
                }
            self.MOD = self.dram("MOD", [DEPTH, 2, 6 * D], F32)
            self.WINd = self.dram("WINd", [128, 8, WCOLS], BF16)
            self.WOUTd = self.dram("WOUTd", [128, 8, D], BF16)
            self.WUPd = self.dram("WUPd", [22, 128, 2, 8, 128], BF16)
            self.WDNd = self.dram("WDNd", [128, 22, D], BF16)
            self.identf = self.sb(gst, "identf", [128, 128], F32)
            self.identb = self.sb(gst, "identb", [128, 128], BF16)
            self.dma(self.identf[:], din["ident"][:, :], w=[self.identf])
            self.V(lambda h: h.tensor_copy(out=self.identb[:], in_=self.identf[:]), [self.identf], [self.identb])
            self.epsT = self.sb(gst, "epsT", [128, 1], F32)
            self.V(lambda h: h.memset(self.epsT[:], EPS), [], [self.epsT])
            self.Sf = self.sb(gst, "Sf", [128, 512], F32)
            self.Sb = self.sb(gst, "Sb", [128, 512], F32)
            self.SfB = self.sb(gst, "SfB", [128, 512], BF16)
            self.SbB = self.sb(gst, "SbB", [128, 512], BF16)
            NTT = SEQ // 128 + CTXL // 128
            self.DT = self.sb(gst, "DT", [128, NTT, 16], F32)
            self.dAb = self.sb(gst, "dAb", [128, NTT, 16], BF16)

            for l in range(self.nlayers):
                last = (l == DEPTH - 1)
                xin_lat = T(din["x"]) if l == 0 else lat.scr["X1"]
                xin_ctx = T(din["ctx"]) if l == 0 else ctx.scr["X1"]
                xout_lat = T(self.out) if last else lat.scr["X1"]
                if l == 0:
                    self._xin0 = (xin_lat, xin_ctx)
                self.phase_mod(l)
                if self.stop == "mod":
                    break
                self.phase_weights(l)
                if self.stop == "weights":
                    break
                self.phase_A(l, ctx, xin_ctx)
                self.phase_A(l, lat, xin_lat)
                if "DTd" in self.dbg and l == 0:
                    dtd = self.dram("DTd", [128, SEQ // 128 + CTXL // 128, 16], F32)
                    self.dma(dtd[:, :, :], self.DT[:], r=[self.DT], w=[dtd])
                if self.stop == "A":
                    break
                self.V(lambda h: h.memset(self.Sf[:], 0.0), [], [self.Sf])
                self.V(lambda h: h.memset(self.Sb[:], 0.0), [], [self.Sb])
                self.V(lambda h: h.memset(self.SfB[:], 0.0), [], [self.SfB])
                self.V(lambda h: h.memset(self.SbB[:], 0.0), [], [self.SbB])
                self.phase_sweep(l, ctx, fwd=True)
                self.phase_sweep(l, ctx, fwd=False, full=not last)
                if self.stop == "Sctx":
                    break
                self.phase_sweep(l, lat, fwd=True)
                if self.stop == "Sfwd":
                    break
                if not last:
                    self.phase_F_ctx(l, ctx)
                self.phase_F_lat(l, lat)
                if self.stop == "F":
                    break
                self.phase_sweep(l, lat, fwd=False, full=True)
                if self.stop == "Sbwd":
                    break
                if not last:
                    self.phase_C(l, ctx, xin_ctx)
                self.phase_C(l, lat, xin_lat)
                if self.stop == "C":
                    break
                if not last:
                    self.phase_M(l, ctx, ctx.scr["X1"])
                self.phase_M(l, lat, xout_lat)
            with nc.Block() as block:
                S.finish(block)
        return nc

    def phase_mod(self, l):
        din = self.din
        with contextlib.ExitStack() as st:
            cct = self.sb(st, "cct", [128, 8, 2])
            scs = self.sb(st, "scs", [128, 8, 2])
            self.dma(cct[:], din["cc"][:, :, :], w=[cct])
            self.A(lambda h: h.activation(out=scs[:], in_=cct[:], func=AF.Silu), [cct], [scs])
            brow = self.sb(st, "brow", [1, 6 * D])
            self.dma(brow[:], din["b_ada"][l, :, :], w=[brow])
            ones2 = self.sb(st, "ones2", [1, 2])
            self.V(lambda h: h.memset(ones2[:], 1.0), [], [ones2])
            modsb = self.sb(st, "modsb", [2, 6 * D])
            wa = [self.sb(st, "wa", [128, 8, 512]) for _ in range(2)]
            pm = [self.ps(st, "pm", [2, 512]) for _ in range(2)]
            wsrc = din["w_ada"][l].rearrange("(kc p) n -> p kc n", p=128)
            for cb in range(12):
                w_ = wa[cb % 2]
                p_ = pm[cb % 2]
                self.dma(w_[:], wsrc[:, :, cb * 512:(cb + 1) * 512], w=[w_])
                for kc in range(8):
                    self.mm(p_, p_[:], scs[:, kc, :], w_[:, kc, :], kc == 0, False, [scs, w_])
                self.mm(p_, p_[:], ones2[:], brow[:, cb * 512:(cb + 1) * 512], False, True, [ones2, brow])
                add = 1.0 if cb in (2, 3, 8, 9) else 0.0
                self.V(lambda h, p_=p_, cb=cb, add=add: h.tensor_scalar(
                    out=modsb[:, cb * 512:(cb + 1) * 512], in0=p_[:], scalar1=add, scalar2=None, op0=ALU.add),
                    [p_], [modsb])
            self.dma(self.MOD[l], modsb[:], r=[modsb], w=[self.MOD])
            self.S.barrier()

    def phase_weights(self, l):
        din = self.din
        with contextlib.ExitStack() as st:
            for half in range(2):
                wf32 = self.sb(st, "wi32", [128, 4, 1552])
                wb16 = self.sb(st, "wi16", [128, 4, 1552], BF16)
                self.dma(wf32[:], din["w_in_r"][l, :, half * 4:(half + 1) * 4, :], w=[wf32])
                for kc in range(4):
                    eng = self.V if kc % 2 == 0 else self.G
                    eng(lambda h, kc=kc, wf32=wf32, wb16=wb16: h.tensor_copy(out=wb16[:, kc, :], in_=wf32[:, kc, :]),
                        [wf32], [wb16])
                self.dma(self.WINd[:, half * 4:(half + 1) * 4, 0:1552], wb16[:], r=[wb16], w=[self.WINd])
            chd = self.sb(st, "chd", [64, 2, 128])
            self.dma(chd[:], din["chdft"][:, :, :], w=[chd])
            fnw = self.sb(st, "fnw", [64, 4, 64])
            self.dma(fnw[:], din["fnet_w"][l, :, :, :], w=[fnw])
            pab = self.ps(st, "pab", [128, 4, 128])
            for g in range(4):
                for ri in range(2):
                    self.mm(pab, pab[:, g, ri * 64:(ri + 1) * 64], chd[:, ri, :], fnw[:, g, :], True, True, [chd, fnw])
            wfT = self.sb(st, "wfT", [128, 2, D])
            wpT = self.sb(st, "wpT", [128, 2, D])
            self.dma(wfT[:], din["w_fT"][l, :, :, :], w=[wfT])
            self.dma(wpT[:], din["w_pT"][l, :, :, :], w=[wpT])
            wfold = self.sb(st, "wfold", [128, 8, 768], BF16)
            pf = [self.ps(st, "pfold", [128, 512]) for _ in range(2)]
            bdf = self.sb(st, "bdf", [128, 2, 256])
            bdp = self.sb(st, "bdp", [128, 2, 128])
            self.V(lambda h: h.memset(bdf[:], 0.0), [], [bdf])
            self.V(lambda h: h.memset(bdp[:], 0.0), [], [bdp])
            for j in range(2):
                for gp in range(2):
                    g = 2 * j + gp
                    self.V(lambda h, j=j, gp=gp, g=g: h.tensor_copy(
                        out=bdf[gp * 64:(gp + 1) * 64, j, gp * 128:(gp + 1) * 128],
                        in_=pab[gp * 64:(gp + 1) * 64, g, :]), [pab], [bdf])
                    self.dma(bdp[gp * 64:(gp + 1) * 64, j, gp * 64:(gp + 1) * 64], din["pool_w"][l, g, :, :], w=[bdp])
            i = 0
            for kc in range(8):
                p_ = pf[i % 2]
                i += 1
                for j in range(2):
                    self.mm(p_, p_[:, j * 256:(j + 1) * 256], wfT[:, j, kc * 128:(kc + 1) * 128], bdf[:, j, :],
                            True, True, [wfT, bdf])
                self.A(lambda h, p_=p_, kc=kc: h.copy(out=wfold[:, kc, 256:768], in_=p_[:]), [p_], [wfold])
                p_ = pf[i % 2]
                i += 1
                for j in range(2):
                    self.mm(p_, p_[:, j * 128:(j + 1) * 128], wpT[:, j, kc * 128:(kc + 1) * 128], bdp[:, j, :],
                            True, True, [wpT, bdp])
                self.V(lambda h, p_=p_, kc=kc: h.tensor_copy(out=wfold[:, kc, 0:256], in_=p_[:, 0:256]), [p_], [wfold])
            self.dma(self.WINd[:, :, 1552:WCOLS], wfold[:], r=[wfold], w=[self.WINd])
            rsc = self.sb(st, "rsc", [128, 8])
            self.dma(rsc[:], din["rowscale"][l, :, :], w=[rsc])
            for half in range(2):
                wo32 = self.sb(st, "wo32", [128, 4, D])
                wo16 = self.sb(st, "wo16", [128, 4, D], BF16)
                self.dma(wo32[:], din["w_out"][l, :, half * 4:(half + 1) * 4, :], w=[wo32])
                for kc in range(4):
                    c = half * 4 + kc
                    self.V(lambda h, kc=kc, c=c, wo32=wo32, wo16=wo16: h.tensor_scalar(
                        out=wo16[:, kc, :], in0=wo32[:, kc, :], scalar1=rsc[:, c:c + 1], scalar2=None, op0=ALU.mult),
                        [wo32, rsc], [wo16])
                self.dma(self.WOUTd[:, half * 4:(half + 1) * 4, :], wo16[:], r=[wo16], w=[self.WOUTd])
            self.S.barrier()
        with contextlib.ExitStack() as st:
            u32 = [self.sb(st, "u32", [128, 8, 256]) for _ in range(3)]
            u16 = [self.sb(st, "u16", [128, 2, 8, 128], BF16) for _ in range(3)]
            i = 0
            for vg in range(2):
                for pb in range(11):
                    a, b_ = u32[i % 3], u16[i % 3]
                    c0 = vg * DFF + pb * 256
                    self.dma(a[:], din["w_up"][l, :, :, c0:c0 + 256], w=[a])
                    eng = (self.V, self.G, self.A)[i % 3]
                    if i % 3 == 2:
                        eng(lambda h, a=a, b_=b_: h.copy(out=b_[:], in_=a[:].rearrange("p k (q c) -> p q k c", q=2)), [a], [b_])
                    else:
                        eng(lambda h, a=a, b_=b_: h.tensor_copy(out=b_[:], in_=a[:].rearrange("p k (q c) -> p q k c", q=2)), [a], [b_])
                    for q in range(2):
                        self.dma(self.WUPd[2 * pb + q, :, vg, :, :], b_[:, q, :, :], r=[b_], w=[self.WUPd])
                    i += 1
            d32 = [self.sb(st, "d32", [128, 2, D]) for _ in range(2)]
            d16 = [self.sb(st, "d16", [128, 2, D], BF16) for _ in range(2)]
            for i in range(11):
                a, b_ = d32[i % 2], d16[i % 2]
                self.dma(a[:], din["w_down"][l, :, 2 * i:2 * i + 2, :], w=[a])
                eng = self.V if i % 2 == 0 else self.G
                eng(lambda h, a=a, b_=b_: h.tensor_copy(out=b_[:], in_=a[:]), [a], [b_])
                self.dma(self.WDNd[:, 2 * i:2 * i + 2, :], b_[:], r=[b_], w=[self.WDNd])
            self.S.barrier()

    def phase_A(self, l, sq, xin):
        din = self.din
        scr = sq.scr
        with contextlib.ExitStack() as st:
            WIN = self.sb(st, "WIN", [128, 8, WCOLS], BF16)
            self.dma(WIN[:], self.WINd[:, :, :], r=[self.WINd], w=[WIN])
            scp = self.sb(st, "scp", [128, D])
            sh = self.sb(st, "sh", [128, D])
            self.dma(scp[:], self.MOD[l, sq.row:sq.row + 1, 1024:2048].partition_broadcast(128), r=[self.MOD], w=[scp])
            self.dma(sh[:], self.MOD[l, sq.row:sq.row + 1, 0:1024].partition_broadcast(128), r=[self.MOD], w=[sh])
            dtb = self.sb(st, "dtb", [128, 16])
            nega = self.sb(st, "nega", [128, 16])
            self.dma(dtb[:], din["dt_bias"][l, :, :].partition_broadcast(128), w=[dtb])
            self.dma(nega[:], din["a_log"][l, :, :].partition_broadcast(128), w=[nega])
            self.A(lambda h: h.activation(out=nega[:], in_=nega[:], func=AF.Exp), [nega], [nega])
            self.V(lambda h: h.tensor_scalar(out=nega[:], in0=nega[:], scalar1=-1.0, scalar2=None, op0=ALU.mult), [nega], [nega])
            xt = [self.sb(st, "xt", [128, D]) for _ in range(2)]
            xn = [self.sb(st, "xn", [128, D]) for _ in range(2)]
            hb = [self.sb(st, "hb", [128, D], BF16) for _ in range(2)]
            small = [(self.sb(st, "st6", [128, 2, 6]), self.sb(st, "mv", [128, 2]), self.sb(st, "rs", [128, 1])) for _ in range(2)]
            stw = min(4, sq.nt)
            hT = [self.sb(st, "hT", [128, 8, stw * 128], BF16) for _ in range(2)]
            ptr = [self.ps(st, "ptr", [128, 8, 128], BF16) for _ in range(2)]
            pz = self.ps(st, "pz", [128, 512])
            pfn = self.ps(st, "pfn", [128, 512])
            ppd = self.ps(st, "ppd", [128, 512])
            pxb = [self.ps(st, "pxb", [128, 512]) for _ in range(2)]
            szt = [self.sb(st, "szt", [128, 512], BF16) for _ in range(2)]
            uft = [self.sb(st, "uft", [128, 512], BF16) for _ in range(2)]
            upt = [self.sb(st, "upt", [128, 256], BF16) for _ in range(2)]
            dtr = [self.sb(st, "dtr", [128, 16]) for _ in range(2)]
            xbst = [self.sb(st, "xbst", [128, 8, stw * 128], BF16) for _ in range(2)]
            xbd = scr["XBCT"].t.rearrange("(c p) n -> p c n", p=128)
            for sti in range(sq.nt // stw):
                hT_ = hT[sti % 2]
                for ti in range(stw):
                    t = sti * stw + ti
                    k = t % 2
                    tt = sq.toff + t
                    self.dma(xt[k][:], xin.t[t * 128:(t + 1) * 128, :], r=[xin], w=[xt[k]])
                    self.layernorm(st, xt[k], xt[k][:], xn[k], xn[k][:], "a", small[k])
                    self.G(lambda h, k=k: h.tensor_tensor(out=xn[k][:], in0=xn[k][:], in1=scp[:], op=ALU.mult), [xn[k], scp], [xn[k]])
                    self.G(lambda h, k=k: h.tensor_tensor(out=hb[k][:], in0=xn[k][:], in1=sh[:], op=ALU.add), [xn[k], sh], [hb[k]])
                    for kc in range(8):
                        self.P(lambda h, k=k, kc=kc: h.transpose(out=ptr[k][:, kc, :], in_=hb[k][:, kc * 128:(kc + 1) * 128],
                                                                identity=self.identb[:]), [hb[k], self.identb], [ptr[k]])
                    self.A(lambda h, k=k, ti=ti, hT_=hT_: h.copy(out=hT_[:, :, ti * 128:(ti + 1) * 128], in_=ptr[k][:]), [ptr[k]], [hT_])
                    for kc in range(8):
                        self.mm(pz, pz[:], hT_[:, kc, ti * 128:(ti + 1) * 128], WIN[:, kc, 0:512], kc == 0, kc == 7, [hT_, WIN])
                    for kc in range(8):
                        self.mm(ppd, ppd[:, 0:272], hT_[:, kc, ti * 128:(ti + 1) * 128], WIN[:, kc, C_DT:C_FN], kc == 0, kc == 7, [hT_, WIN])
                    for kc in range(8):
                        self.mm(pfn, pfn[:], hT_[:, kc, ti * 128:(ti + 1) * 128], WIN[:, kc, C_FN:WCOLS], kc == 0, kc == 7, [hT_, WIN])
                    self.V(lambda h, k=k: h.tensor_tensor(out=dtr[k][:], in0=ppd[:, 0:16], in1=dtb[:], op=ALU.add), [ppd, dtb], [dtr[k]])
                    self.A(lambda h, k=k: h.activation(out=dtr[k][:], in_=dtr[k][:], func=AF.Exp), [dtr[k]], [dtr[k]])
                    self.A(lambda h, k=k, tt=tt: h.activation(out=self.DT[:, tt, :], in_=dtr[k][:], func=AF.Ln, bias=1.0), [dtr[k]], [self.DT])
                    self.V(lambda h, tt=tt: h.tensor_tensor(out=self.dAb[:, tt, :], in0=self.DT[:, tt, :], in1=nega[:], op=ALU.mult),
                           [self.DT, nega], [self.dAb])
                    self.A(lambda h, k=k: h.activation(out=szt[k][:], in_=pz[:], func=AF.Silu), [pz], [szt[k]])
                    self.dma(scr["SZ"][t * 128:(t + 1) * 128, :], szt[k][:], r=[szt[k]], w=[scr["SZ"]])
                    self.V(lambda h, k=k: h.tensor_copy(out=uft[k][:], in_=pfn[:]), [pfn], [uft[k]])
                    self.dma(scr["UF"][t * 128:(t + 1) * 128, :], uft[k][:], r=[uft[k]], w=[scr["UF"]])
                    self.V(lambda h, k=k: h.tensor_copy(out=upt[k][:], in_=ppd[:, 16:272]), [ppd], [upt[k]])
                    self.dma(scr["MIX"][t * 128:(t + 1) * 128, 768:1024], upt[k][:], r=[upt[k]], w=[scr["MIX"]])
                xb_ = xbst[sti % 2]
                for ch in range(8):
                    p_ = pxb[ch % 2]
                    for kc in range(8):
                        self.mm(p_, p_[:, 0:stw * 128], WIN[:, kc, 512 + ch * 128:512 + (ch + 1) * 128], hT_[:, kc, :],
                                kc == 0, kc == 7, [hT_, WIN])
                    if ch % 2 == 0:
                        self.A(lambda h, p_=p_, ch=ch, xb_=xb_: h.copy(out=xb_[:, ch, :], in_=p_[:, 0:stw * 128]), [p_], [xb_])
                    else:
                        self.V(lambda h, p_=p_, ch=ch, xb_=xb_: h.tensor_copy(out=xb_[:, ch, :], in_=p_[:, 0:stw * 128]), [p_], [xb_])
                c0 = sti * stw * 128
                self.dma(xbd[:, :, c0:c0 + stw * 128], xb_[:], r=[xb_], w=[scr["XBCT"]])
            self.S.barrier()

    def phase_sweep(self, l, sq, fwd, full=True):
        din = self.din
        scr = sq.scr
        with contextlib.ExitStack() as st:
            cw = self.sb(st, "cw", [128, 8, 3])
            cb = self.sb(st, "cb", [128, 8])
            self.dma(cw[:], din["ssd_cw"][l, :, :, :], w=[cw])
            self.dma(cb[:], din["ssd_cb"][l, :, :], w=[cb])
            DG = self.sb(st, "DG", [128, 8, 3, 128], BF16)
            for ch in range(8):
                for tp in range(3):
                    self.V(lambda h, ch=ch, tp=tp: h.tensor_scalar(out=DG[:, ch, tp, :], in0=self.identf[:],
                                                                   scalar1=cw[:, ch, tp:tp + 1], scalar2=None, op0=ALU.mult),
                           [self.identf, cw], [DG])
            tri = self.cast_load(st, "tri", din["tri"][:, :, :], [128, 4, 128])
            if fwd:
                upat = self.cast_load(st, "upat", din["upat"][:, :, :], [128, 16, 128], eng="G")
                negm = self.cast_load(st, "negm", din["negm"][:, :, :], [128, 16, 128], eng="G")
                dtile = self.sb(st, "dtile", [128, 512])
                self.dma(dtile[:], din["d_rep"][l, :, :].partition_broadcast(128), w=[dtile])
            U, NU, Us, ONES = (tri[:, i, :] for i in range(4))
            Sx = self.Sf if fwd else self.Sb
            SxB = self.SfB if fwd else self.SbB
            xin = [self.sb(st, "xin", [128, 8, 130], BF16) for _ in range(2)]
            xc = [self.sb(st, "xc", [128, 8, 128], BF16) for _ in range(2)]
            XB = [self.sb(st, "XB", [128, 768], BF16) for _ in range(2)]
            pc0 = self.ps(st, "pc0", [128, 4, 128])
            pc1 = self.ps(st, "pc1", [128, 4, 128])
            ptr = self.ps(st, "ptr", [128, 1024], BF16)
            psm = self.ps(st, "psm", [128, 512])
            py = self.ps(st, "py", [128, 512])
            arg = self.sb(st, "arg", [128, 24])
            ex = self.sb(st, "ex", [128, 24])
            wv = self.sb(st, "wv", [128, 8])
            Xw = self.sb(st, "Xw", [128, 512], BF16)
            yo = self.sb(st, "yo", [128, 512])
            if fwd:
                pseg = [self.ps(st, "pseg", [128, 4, 128]) for _ in range(3)]
                rhs1 = self.sb(st, "rhs1", [128, 16, 128], BF16)
                LT = self.sb(st, "LT", [128, 16, 128], BF16)
                MT = self.sb(st, "MT", [128, 16, 128], BF16)
                Xf = self.sb(st, "Xf", [128, 512], BF16)
                Xb = self.sb(st, "Xb", [128, 512], BF16)
                XD = self.sb(st, "XD", [128, 512], BF16)
                ypo = [self.sb(st, "ypo", [128, 512], BF16) for _ in range(2)]
            else:
                ypi = [self.sb(st, "ypi", [128, 512], BF16) for _ in range(2)]
                szi = [self.sb(st, "szi", [128, 512], BF16) for _ in range(2)]
                yz = self.sb(st, "yz", [128, 512])
                sqj = self.sb(st, "sqj", [128, 512])
                ss = self.sb(st, "ss", [128, 2])
                rg = self.sb(st, "rg", [128, 2])
                yn = [self.sb(st, "yn", [128, 512], BF16) for _ in range(2)]
            xbd = scr["XBCT"].t.rearrange("(c p) n -> p c n", p=128)
            order = list(range(sq.nt)) if fwd else list(range(sq.nt - 1, -1, -1))
            for i, t in enumerate(order):
                k = i % 2
                tt = sq.toff + t
                xi, xc_, XB_ = xin[k], xc[k], XB[k]
                lo_, hi_ = max(t * 128 - 1, 0), min(t * 128 + 129, sq.L)
                o_ = lo_ - (t * 128 - 1)
                if o_ > 0:
                    self.G(lambda h, xi=xi: h.memset(xi[:, :, 0:1], 0.0), [], [xi])
                if hi_ < t * 128 + 129:
                    self.G(lambda h, xi=xi: h.memset(xi[:, :, 129:130], 0.0), [], [xi])
                self.dma(xi[:, :, o_:o_ + hi_ - lo_], xbd[:, :, lo_:hi_], r=[scr["XBCT"]], w=[xi])
                if (not fwd) and full:
                    self.dma(ypi[k][:], scr["YP"][t * 128:(t + 1) * 128, :], r=[scr["YP"]], w=[ypi[k]])
                    self.dma(szi[k][:], scr["SZ"][t * 128:(t + 1) * 128, :], r=[scr["SZ"]], w=[szi[k]])
                for ch in range(8):
                    pc = pc0 if ch < 4 else pc1
                    for tp in range(3):
                        self.mm(pc, pc[:, ch % 4, :], DG[:, ch, tp, :], xi[:, ch, tp:tp + 128], tp == 0, tp == 2, [DG, xi])
                for ch in range(8):
                    pc = pc0 if ch < 4 else pc1
                    self.A(lambda h, pc=pc, ch=ch, xc_=xc_: h.activation(out=xc_[:, ch, :], in_=pc[:, ch % 4, :], func=AF.Silu,
                                                                        bias=cb[:, ch:ch + 1]), [pc, cb], [xc_])
                for ch in range(6):
                    self.P(lambda h, ch=ch, xc_=xc_: h.transpose(out=ptr[:, ch * 128:(ch + 1) * 128], in_=xc_[:, ch, :],
                                                                identity=self.identb[:]), [xc_, self.identb], [ptr])
                self.A(lambda h, XB_=XB_: h.copy(out=XB_[:], in_=ptr[:, 0:768]), [ptr], [XB_])
                dA_t = self.dAb[:, tt, :]
                self.mm(psm, psm[:, 0:16], U if fwd else Us, dA_t, True, True, [tri, self.dAb])
                self.mm(psm, psm[:, 16:32], ONES, dA_t, True, True, [tri, self.dAb])
                o = 0 if fwd else 8
                self.V(lambda h, o=o: h.tensor_copy(out=arg[:, 0:8], in_=psm[:, o:o + 8]), [psm], [arg])
                self.V(lambda h, o=o: h.tensor_copy(out=arg[:, 8:16], in_=psm[:, 16 + o:24 + o]), [psm], [arg])
                self.V(lambda h, o=o: h.tensor_tensor(out=arg[:, 16:24], in0=psm[:, 16 + o:24 + o], in1=arg[:, 0:8],
                                                      op=ALU.subtract), [psm, arg], [arg])
                self.A(lambda h: h.activation(out=ex[:], in_=arg[:], func=AF.Exp), [arg], [ex])
                dt_d = self.DT[:, tt, o:o + 8]
                if fwd:
                    self.V(lambda h, dt_d=dt_d: h.tensor_tensor(out=wv[:], in0=ex[:, 16:24], in1=dt_d, op=ALU.mult), [ex, self.DT], [wv])
                    ysc = ex[:, 0:8]
                else:
                    self.V(lambda h, dt_d=dt_d: h.tensor_tensor(out=wv[:], in0=ex[:, 0:8], in1=dt_d, op=ALU.mult), [ex, self.DT], [wv])
                    ysc = ex[:, 16:24]
                X3 = XB_[:, 0:512].rearrange("p (a b) -> p a b", a=8)

                def bc8(ap):
                    return ap.unsqueeze(2).to_broadcast([128, 8, 64])
                self.V(lambda h, X3=X3: h.tensor_tensor(out=Xw[:].rearrange("p (a b) -> p a b", a=8), in0=X3, in1=bc8(wv[:]), op=ALU.mult),
                       [XB_, wv], [Xw])
                if fwd:
                    dtf = self.DT[:, tt, 0:8]
                    dtb_ = self.DT[:, tt, 8:16]
                    self.V(lambda h, X3=X3, dtf=dtf: h.tensor_tensor(out=Xf[:].rearrange("p (a b) -> p a b", a=8), in0=X3, in1=bc8(dtf), op=ALU.mult),
                           [XB_, self.DT], [Xf])
                    self.G(lambda h, X3=X3, dtb_=dtb_: h.tensor_tensor(out=Xb[:].rearrange("p (a b) -> p a b", a=8), in0=X3, in1=bc8(dtb_), op=ALU.mult),
                           [XB_, self.DT], [Xb])
                    self.G(lambda h, XB_=XB_: h.tensor_tensor(out=XD[:], in0=XB_[:, 0:512], in1=dtile[:], op=ALU.mult), [XB_, dtile], [XD])
                    self.G(lambda h, tt=tt: h.tensor_tensor(out=rhs1[:], in0=upat[:], in1=self.dAb[:, tt, :].unsqueeze(2).to_broadcast([128, 16, 128]),
                                                            op=ALU.mult), [upat, self.dAb], [rhs1])
                    for g in range(2):
                        self.mm(psm, psm[:, 128 + g * 128:256 + g * 128], xc_[:, 4 + g, :], xc_[:, 6 + g, :], True, True, [xc_])
                    for q in range(4):
                        pq = pseg[q % 3]
                        lt2 = NU if q < 2 else Us
                        self.mm(pq, pq[:], ONES, rhs1[:, 4 * q:4 * q + 4, :], True, False, [tri, rhs1])
                        self.mm(pq, pq[:], lt2, self.dAb[:, tt, 4 * q:4 * q + 4].unsqueeze(2).to_broadcast([128, 4, 128]), False, False, [tri, self.dAb])
                        self.mm(pq, pq[:], self.identb[:], negm[:, 4 * q:4 * q + 4, :], False, True, [self.identb, negm])
                        self.A(lambda h, pq=pq, q=q: h.activation(out=LT[:, 4 * q:4 * q + 4, :], in_=pq[:], func=AF.Exp), [pq], [LT])
                        g = q % 2
                        self.V(lambda h, q=q, g=g: h.tensor_tensor(
                            out=MT[:, 4 * q:4 * q + 4, :], in0=LT[:, 4 * q:4 * q + 4, :],
                            in1=psm[:, 128 + g * 128:256 + g * 128].unsqueeze(1).to_broadcast([128, 4, 128]), op=ALU.mult),
                            [LT, psm], [MT])
                    self.mm(py, py[:], self.identb[:], XD[:], True, False, [self.identb, XD])
                    for hd in range(8):
                        self.mm(py, py[:, hd * 64:(hd + 1) * 64], MT[:, hd, :], Xf[:, hd * 64:(hd + 1) * 64], False, False, [MT, Xf])
                        self.mm(py, py[:, hd * 64:(hd + 1) * 64], MT[:, 8 + hd, :], Xb[:, hd * 64:(hd + 1) * 64], False, True, [MT, Xb])
                if fwd or full:
                    po = pc0
                    for g in range(2):
                        self.mm(po, po[:].rearrange("p a b -> p (a b)")[:, g * 256:(g + 1) * 256], xc_[:, 6 + g, :],
                                SxB[:, g * 256:(g + 1) * 256], True, True, [xc_, SxB])
                    self.V(lambda h, po=po, ysc=ysc: h.tensor_tensor(out=yo[:].rearrange("p (a b) -> p a b", a=8),
                                                                     in0=po[:].rearrange("p a (c b) -> p (a c) b", b=64),
                                                                     in1=bc8(ysc), op=ALU.mult), [po, ex], [yo])
                if fwd:
                    self.V(lambda h, k=k: h.tensor_tensor(out=ypo[k][:], in0=py[:], in1=yo[:], op=ALU.add), [py, yo], [ypo[k]])
                    self.dma(scr["YP"][t * 128:(t + 1) * 128, :], ypo[k][:], r=[ypo[k]], w=[scr["YP"]])
                elif full:
                    self.V(lambda h, k=k: h.tensor_tensor(out=yz[:], in0=yo[:], in1=ypi[k][:], op=ALU.add), [yo, ypi[k]], [yz])
                    self.G(lambda h, k=k: h.tensor_tensor(out=yz[:], in0=yz[:], in1=szi[k][:], op=ALU.mult), [yz, szi[k]], [yz])
                    for g in range(2):
                        self.A(lambda h, g=g: h.activation(out=sqj[:, g * 256:(g + 1) * 256], in_=yz[:, g * 256:(g + 1) * 256],
                                                           func=AF.Square, accum_out=ss[:, g:g + 1]), [yz], [sqj, ss])
                    self.rstd(ss[:], rg, rg[:], [ss], scale=1.0 / 256.0)
                    for g in range(2):
                        self.V(lambda h, g=g, k=k: h.tensor_scalar(out=yn[k][:, g * 256:(g + 1) * 256], in0=yz[:, g * 256:(g + 1) * 256],
                                                                  scalar1=rg[:, g:g + 1], scalar2=None, op0=ALU.mult), [yz, rg], [yn[k]])
                    self.dma(scr["MIX"][t * 128:(t + 1) * 128, 0:512], yn[k][:], r=[yn[k]], w=[scr["MIX"]])
                pd = pc1
                for g in range(2):
                    self.mm(pd, pd[:].rearrange("p a b -> p (a b)")[:, g * 256:(g + 1) * 256], XB_[:, 512 + g * 128:640 + g * 128],
                            Xw[:, g * 256:(g + 1) * 256], True, True, [XB_, Xw])
                self.V(lambda h: h.tensor_tensor(out=Sx[:].rearrange("p (a b) -> p a b", a=8), in0=Sx[:].rearrange("p (a b) -> p a b", a=8),
                                                 in1=bc8(ex[:, 8:16]), op=ALU.mult), [Sx, ex], [Sx])
                self.V(lambda h, pd=pd: h.tensor_tensor(out=Sx[:], in0=Sx[:], in1=pd[:].rearrange("p a b -> p (a b)"), op=ALU.add), [Sx, pd], [Sx])
                self.G(lambda h: h.tensor_copy(out=SxB[:], in_=Sx[:]), [Sx], [SxB])
            self.S.barrier()

    def phase_F_lat(self, l, sq):
        din = self.din
        scr = sq.scr
        with contextlib.ExitStack() as st:
            dA_ = self.cast_load(st, "dftA", din["dftA"][:, :, :], [128, 2, 256])
            dB_ = self.cast_load(st, "dftB", din["dftB"][:, :, :], [64, 2, 64])
            tw = self.sb(st, "tw", [64, 3, 128])
            self.dma(tw[:], din["twid"][:, :, :], w=[tw])
            Gt = [self.sb(st, "Gt", [128, 64, 128], BF16) for _ in range(2)]
            YF = self.sb(st, "YF", [128, 64, 256], BF16)
            pa = [self.ps(st, "pa", [64, 2, 2, 128]) for _ in range(2)]
            pb = [self.ps(st, "pb", [128, 8, 64]) for _ in range(2)]
            At = [self.sb(st, "At", [64, 2, 2, 128]) for _ in range(2)]
            Bt = [self.sb(st, "Bt", [64, 2, 2, 128]) for _ in range(2)]
            Yp = [self.sb(st, "Yp", [64, 2, 2, 128], BF16) for _ in range(2)]
            ufv = scr["UF"].t.rearrange("(a b) c -> a b c", b=64)
            n = 0
            for g in range(4):
                G_ = Gt[g % 2]
                self.dma(G_[:], ufv[:, :, g * 128:(g + 1) * 128], r=[scr["UF"]], w=[G_])
                for cp in range(32):
                    k = n % 2
                    n += 1
                    pa_, At_, Bt_, Yp_ = pa[k], At[k], Bt[k], Yp[k]
                    for ch in range(2):
                        d_ = 2 * cp + ch
                        self.mm(pa_, pa_[:, ch, :, :].rearrange("p a b -> p (a b)"), G_[:, :, d_], dA_[:, 0, :], True, False, [G_, dA_])
                        self.mm(pa_, pa_[:, ch, :, :].rearrange("p a b -> p (a b)"), G_[:, :, 64 + d_], dA_[:, 1, :], False, True, [G_, dA_])
                    trb = tw[:, 0, :].unsqueeze(1).unsqueeze(1).to_broadcast([64, 2, 2, 128])
                    self.V(lambda h, pa_=pa_, At_=At_, trb=trb: h.tensor_tensor(out=At_[:], in0=pa_[:], in1=trb, op=ALU.mult), [pa_, tw], [At_])
                    self.V(lambda h, pa_=pa_, Bt_=Bt_: h.tensor_tensor(out=Bt_[:, :, 0, :], in0=pa_[:, :, 1, :],
                                                                      in1=tw[:, 1, :].unsqueeze(1).to_broadcast([64, 2, 128]), op=ALU.mult), [pa_, tw], [Bt_])
                    self.V(lambda h, pa_=pa_, Bt_=Bt_: h.tensor_tensor(out=Bt_[:, :, 1, :], in0=pa_[:, :, 0, :],
                                                                      in1=tw[:, 2, :].unsqueeze(1).to_broadcast([64, 2, 128]), op=ALU.mult), [pa_, tw], [Bt_])
                    self.G(lambda h, At_=At_, Bt_=Bt_, Yp_=Yp_: h.tensor_tensor(out=Yp_[:], in0=At_[:], in1=Bt_[:], op=ALU.add), [At_, Bt_], [Yp_])
                    pb_ = pb[(cp // 4) % 2]
                    for ch in range(2):
                        c8 = (cp % 4) * 2 + ch
                        self.mm(pb_, pb_[:, c8, :], Yp_[:, ch, 0, :], dB_[:, 0, :], True, False, [Yp_, dB_])
                        self.mm(pb_, pb_[:, c8, :], Yp_[:, ch, 1, :], dB_[:, 1, :], False, True, [Yp_, dB_])
                    if cp % 4 == 3:
                        c0 = g * 64 + (cp // 4) * 8
                        self.A(lambda h, pb_=pb_, c0=c0: h.copy(out=YF[:, :, c0:c0 + 8].rearrange("p k c -> p c k"), in_=pb_[:]), [pb_], [YF])
            self.dma(scr["MIX"].t.rearrange("(a b) c -> b a c", b=128)[:, :, 512:768], YF[:], r=[YF], w=[scr["MIX"]])
            self.S.barrier()

    def phase_F_ctx(self, l, sq):
        din = self.din
        scr = sq.scr
        with contextlib.ExitStack() as st:
            dC = self.cast_load(st, "dftC", din["dftC"][:, :, :, :], [128, 2, 2, 256])
            gc = self.sb(st, "gc", [128, 2, 512], BF16)
            self.dma(gc[:], scr["UF"].t.rearrange("(c p) n -> p c n", p=128), r=[scr["UF"]], w=[gc])
            pf = [self.ps(st, "pfc", [128, 256]) for _ in range(2)]
            yf = [self.sb(st, "yfc", [128, 256], BF16) for _ in range(2)]
            for kt in range(2):
                for g in range(4):
                    i = 0
                    for lc in range(2):
                        for ri in range(2):
                            self.mm(pf[kt], pf[kt][:, g * 64:(g + 1) * 64], dC[:, ri, lc, kt * 128:(kt + 1) * 128],
                                    gc[:, lc, g * 128 + ri * 64:g * 128 + ri * 64 + 64], i == 0, i == 3, [dC, gc])
                            i += 1
                self.V(lambda h, kt=kt: h.tensor_copy(out=yf[kt][:], in_=pf[kt][:]), [pf[kt]], [yf[kt]])
                self.dma(scr["MIX"][kt * 128:(kt + 1) * 128, 512:768], yf[kt][:], r=[yf[kt]], w=[scr["MIX"]])
            self.S.barrier()

    def phase_C(self, l, sq, xin):
        din = self.din
        scr = sq.scr
        with contextlib.ExitStack() as st:
            WOUT = self.sb(st, "WOUT", [128, 8, D], BF16)
            self.dma(WOUT[:], self.WOUTd[:, :, :], r=[self.WOUTd], w=[WOUT])
            PT = self.cast_load(st, "poolm", din["poolm"][:, :, :, :], [128, 5, 4, 128])

            def bct(name, src):
                t_ = self.sb(st, name, [128, D])
                self.dma(t_[:], src.partition_broadcast(128), r=[self.MOD], w=[t_])
                return t_
            g1t = bct("g1t", self.MOD[l, sq.row:sq.row + 1, 2048:3072])
            sc2 = bct("sc2", self.MOD[l, sq.row:sq.row + 1, 4096:5120])
            sh2 = bct("sh2", self.MOD[l, sq.row:sq.row + 1, 3072:4096])
            lng = bct("lng", din["ln1_g"][l, :, :])
            lnb = bct("lnb", din["ln1_b"][l, :, :])
            xt = [self.sb(st, "xt", [128, D]) for _ in range(2)]
            mx = [self.sb(st, "mx", [128, 768], BF16) for _ in range(2)]
            upw = [self.sb(st, "upw", [128, 3, 256], BF16) for _ in range(2)]
            mixT = [self.sb(st, "mixT", [128, 8, 128], BF16) for _ in range(2)]
            v = [self.sb(st, "v", [128, D]) for _ in range(2)]
            xm = [self.sb(st, "xm", [128, D]) for _ in range(2)]
            hn = [self.sb(st, "hn", [128, D]) for _ in range(2)]
            h2 = [self.sb(st, "h2", [128, D], BF16) for _ in range(2)]
            h2T = [self.sb(st, "h2T", [128, 8, 128], BF16) for _ in range(2)]
            small = [(self.sb(st, "st6", [128, 2, 6]), self.sb(st, "mv", [128, 2]), self.sb(st, "rs", [128, 1])) for _ in range(4)]
            ptm = self.ps(st, "ptm", [128, 8, 128], BF16)
            ppl = self.ps(st, "ppl", [128, 2, 128])
            pout = [self.ps(st, "pout", [128, D]) for _ in range(2)]
            pth = self.ps(st, "pth", [128, 8, 128], BF16)
            h2v = scr["H2T"].t.rearrange("(c p) n -> p c n", p=128)
            for t in range(sq.nt):
                k = t % 2
                self.dma(xt[k][:], xin.t[t * 128:(t + 1) * 128, :], r=[xin], w=[xt[k]])
                self.dma(mx[k][:], scr["MIX"][t * 128:(t + 1) * 128, 0:768], r=[scr["MIX"]], w=[mx[k]])
                rels = []
                for r_, tn in enumerate((t - 1, t, t + 1)):
                    if 0 <= tn < sq.nt:
                        self.dma(upw[k][:, r_, :], scr["MIX"][tn * 128:(tn + 1) * 128, 768:1024], r=[scr["MIX"]], w=[upw[k]])
                        rels.append(r_)
                for c in range(6):
                    self.P(lambda h, k=k, c=c: h.transpose(out=ptm[:, c, :], in_=mx[k][:, c * 128:(c + 1) * 128], identity=self.identb[:]),
                           [mx[k], self.identb], [ptm])
                self.A(lambda h, k=k: h.copy(out=mixT[k][:, 0:6, :], in_=ptm[:, 0:6, :]), [ptm], [mixT[k]])
                for g in range(4):
                    for i, r_ in enumerate(rels):
                        ridx = r_
                        if r_ == 1 and t == 0:
                            ridx = 3
                        if r_ == 1 and t == sq.nt - 1:
                            ridx = 4
                        self.mm(ppl, ppl[(g % 2) * 64:(g % 2) * 64 + 64, g // 2, :], upw[k][:, r_, g * 64:(g + 1) * 64], PT[:, ridx, g, :],
                                i == 0, i == len(rels) - 1, [upw[k], PT])
                self.V(lambda h, k=k: h.tensor_copy(out=mixT[k][:, 6:8, :], in_=ppl[:]), [ppl], [mixT[k]])
                po = pout[k]
                for half in range(2):
                    for c in range(8):
                        self.mm(po, po[:, half * 512:(half + 1) * 512], mixT[k][:, c, :], WOUT[:, c, half * 512:(half + 1) * 512],
                                c == 0, c == 7, [mixT[k], WOUT])
                self.V(lambda h, k=k, po=po: h.tensor_tensor(out=v[k][:], in0=po[:], in1=g1t[:], op=ALU.mult), [po, g1t], [v[k]])
                self.V(lambda h, k=k: h.scalar_tensor_tensor(out=v[k][:], in0=xt[k][:], scalar=ALPHA, in1=v[k][:], op0=ALU.mult, op1=ALU.add),
                       [xt[k], v[k]], [v[k]])
                self.layernorm(st, v[k], v[k][:], xm[k], xm[k][:], "c1", small[k])
                self.G(lambda h, k=k: h.tensor_tensor(out=xm[k][:], in0=xm[k][:], in1=lng[:], op=ALU.mult), [xm[k], lng], [xm[k]])
                self.G(lambda h, k=k: h.tensor_tensor(out=xm[k][:], in0=xm[k][:], in1=lnb[:], op=ALU.add), [xm[k], lnb], [xm[k]])
                self.dma(scr["XMID"][t * 128:(t + 1) * 128, :], xm[k][:], r=[xm[k]], w=[scr["XMID"]])
                self.layernorm(st, xm[k], xm[k][:], hn[k], hn[k][:], "c2", small[2 + k])
                self.G(lambda h, k=k: h.tensor_tensor(out=hn[k][:], in0=hn[k][:], in1=sc2[:], op=ALU.mult), [hn[k], sc2], [hn[k]])
                self.G(lambda h, k=k: h.tensor_tensor(out=h2[k][:], in0=hn[k][:], in1=sh2[:], op=ALU.add), [hn[k], sh2], [h2[k]])
                for kc in range(8):
                    self.P(lambda h, k=k, kc=kc: h.transpose(out=pth[:, kc, :], in_=h2[k][:, kc * 128:(kc + 1) * 128], identity=self.identb[:]),
                           [h2[k], self.identb], [pth])
                self.A(lambda h, k=k: h.copy(out=h2T[k][:], in_=pth[:]), [pth], [h2T[k]])
                self.dma(h2v[:, :, t * 128:(t + 1) * 128], h2T[k][:], r=[h2T[k]], w=[scr["H2T"]])
            self.S.barrier()

    def phase_M(self, l, sq, xout):
        din = self.din
        scr = sq.scr
        is_ctx = sq.name == "ctx"
        if is_ctx:
            rows, W, RB = 1, CTXL, 1
            taps = [(0, dx, 3 + (dx + 1)) for dx in (-1, 0, 1)]
        else:
            rows, W, RB = SEQ // GRID_W, GRID_W, 16
            taps = [(dy, dx, (dy + 1) * 3 + (dx + 1)) for dy in (-1, 0, 1) for dx in (-1, 0, 1)]
        nq = max(1, (RB * W) // 512)
        qrows = RB // nq if not is_ctx else 1
        qn = qrows * W
        ntb = (RB * W) // 128
        with contextlib.ExitStack() as st:
            WDN = self.sb(st, "WDN", [128, 22, D], BF16)
            self.dma(WDN[:], self.WDNd[:, :, :], r=[self.WDNd], w=[WDN])
            fcw = self.sb(st, "fcw", [128, 44, 9])
            fcb = self.sb(st, "fcb", [128, 44])
            self.dma(fcw[:], din["ffn_cw"][l, :, :, :], w=[fcw])
            self.dma(fcb[:], din["ffn_cb"][l, :, :], w=[fcb])

            def bct(name, src):
                t_ = self.sb(st, name, [128, D])
                self.dma(t_[:], src.partition_broadcast(128), r=[self.MOD], w=[t_])
                return t_
            g2t = bct("g2t", self.MOD[l, sq.row:sq.row + 1, 5120:6144])
            lng = bct("lng2", din["ln2_g"][l, :, :])
            lnb = bct("lnb2", din["ln2_b"][l, :, :])
            nhr = RB + 2
            hb = [self.sb(st, "hbk", [128, 8, nhr * W], BF16)]
            Abuf = [self.sb(st, "Abuf", [128, nhr, W + 2], BF16) for _ in range(4)]
            for a_ in Abuf:
                self.G(lambda h, a_=a_: h.memset(a_[:], 0.0), [], [a_])
            actT = self.sb(st, "actT", [128, 22, RB * W], BF16)
            wu = [self.sb(st, "wu", [128, 2, 8, 128], BF16) for _ in range(3)]
            dgt = [self.sb(st, "dgt", [128, len(taps), 128], BF16) for _ in range(3)]
            gl = [self.sb(st, "gl", [128, qn]) for _ in range(2)]
            npu = (nhr * W + 511) // 512
            pu = [self.ps(st, "pu", [128, npu * 512]) for _ in range(2)]
            pcv = [self.ps(st, "pcv", [128, 512]) for _ in range(2)]
            xmt = [self.sb(st, "xmt", [128, D]) for _ in range(2)]
            v = [self.sb(st, "vf", [128, D]) for _ in range(2)]
            xo = xmt
            small = [(self.sb(st, "st6", [128, 2, 6]), self.sb(st, "mv", [128, 2]), self.sb(st, "rs", [128, 1])) for _ in range(2)]
            h2v = scr["H2T"].t.rearrange("(c p) n -> p c n", p=128)
            nblk = rows // RB
            iu = 0
            icv = 0
            idg = 0
            for bi in range(nblk):
                r0, r1 = bi * RB, (bi + 1) * RB
                hr0, hr1 = max(r0 - 1, 0), min(r1 + 1, rows)
                nh = hr1 - hr0
                ar0 = hr0 - (r0 - 1)
                hb_ = hb[0]
                self.dma(hb_[:, :, 0:nh * W], h2v[:, :, hr0 * W:hr1 * W], r=[scr["H2T"]], w=[hb_])
                if bi == nblk - 1 and nblk > 1:
                    for a_ in Abuf:
                        self.G(lambda h, a_=a_: h.memset(a_[:, nhr - 1, :], 0.0), [], [a_])
                for pr in range(22):
                    wu_ = wu[pr % 3]
                    self.dma(wu_[:], self.WUPd[pr, :, :, :, :], r=[self.WUPd], w=[wu_])
                    for vg in (1, 0):
                        cc = vg * 22 + pr
                        pu_ = pu[iu % 2]
                        iu += 1
                        A_ = Abuf[vg * 2 + (pr % 2)]
                        ntok = nh * W
                        for nb in range((ntok + 511) // 512):
                            n0, n1 = nb * 512, min(ntok, (nb + 1) * 512)
                            for kc in range(8):
                                self.mm(pu_, pu_[:, n0:n1], wu_[:, vg, kc, :], hb_[:, kc, n0:n1], kc == 0, kc == 7, [wu_, hb_])
                        self.A(lambda h, pu_=pu_, A_=A_, ntok=ntok, nh=nh, ar0=ar0: h.copy(
                            out=A_[:, ar0:ar0 + nh, 1:W + 1], in_=pu_[:, 0:ntok].rearrange("p (a b) -> p a b", b=W)), [pu_], [A_])
                        dg_ = dgt[idg % 3]
                        idg += 1
                        for ti, (dy, dx, widx) in enumerate(taps):
                            self.G(lambda h, dg_=dg_, ti=ti, cc=cc, widx=widx: h.tensor_scalar(
                                out=dg_[:, ti, :], in0=self.identf[:], scalar1=fcw[:, cc, widx:widx + 1], scalar2=None, op0=ALU.mult),
                                [self.identf, fcw], [dg_])
                        for q in range(nq):
                            pc_ = pcv[icv % 2]
                            icv += 1
                            for ti, (dy, dx, widx) in enumerate(taps):
                                ra = 1 + q * qrows + dy
                                self.mm(pc_, pc_[:, 0:qn], dg_[:, ti, :], A_[:, ra:ra + qrows, 1 + dx:1 + dx + W],
                                        ti == 0, ti == len(taps) - 1, [dg_, A_])
                            if vg == 1:
                                g_ = gl[q % 2]
                                self.A(lambda h, pc_=pc_, g_=g_, cc=cc: h.activation(out=g_[:], in_=pc_[:, 0:qn], func=AF.Gelu_apprx_tanh,
                                                                                  bias=fcb[:, cc:cc + 1]), [pc_, fcb], [g_])
                            else:
                                g_ = gl[q % 2]
                                self.V(lambda h, pc_=pc_, g_=g_, cc=cc, pr=pr, q=q: h.scalar_tensor_tensor(
                                    out=actT[:, pr, q * qn:(q + 1) * qn], in0=pc_[:, 0:qn], scalar=fcb[:, cc:cc + 1], in1=g_[:],
                                    op0=ALU.add, op1=ALU.mult), [pc_, fcb, g_], [actT])
                for tb in range(ntb):
                    t = bi * ntb + tb
                    k = t % 2
                    pd = pu[iu % 2]
                    iu += 1
                    self.dma(xmt[k][:], scr["XMID"][t * 128:(t + 1) * 128, :], r=[scr["XMID"]], w=[xmt[k]])
                    for half in range(2):
                        for fc in range(22):
                            self.mm(pd, pd[:, half * 512:(half + 1) * 512], actT[:, fc, tb * 128:(tb + 1) * 128],
                                    WDN[:, fc, half * 512:(half + 1) * 512], fc == 0, fc == 21, [actT, WDN])
                    self.V(lambda h, k=k, pd=pd: h.tensor_tensor(out=v[k][:], in0=pd[:, 0:D], in1=g2t[:], op=ALU.mult), [pd, g2t], [v[k]])
                    self.V(lambda h, k=k: h.scalar_tensor_tensor(out=v[k][:], in0=xmt[k][:], scalar=ALPHA, in1=v[k][:], op0=ALU.mult, op1=ALU.add),
                           [xmt[k], v[k]], [v[k]])
                    self.layernorm(st, v[k], v[k][:], xo[k], xo[k][:], "m", small[k])
                    self.G(lambda h, k=k: h.tensor_tensor(out=xo[k][:], in0=xo[k][:], in1=lng[:], op=ALU.mult), [xo[k], lng], [xo[k]])
                    self.G(lambda h, k=k: h.tensor_tensor(out=xo[k][:], in0=xo[k][:], in1=lnb[:], op=ALU.add), [xo[k], lnb], [xo[k]])
                    self.dma(xout.t[t * 128:(t + 1) * 128, :], xo[k][:], r=[xo[k]], w=[xout])
            self.S.barrier()


_PROG_CACHE = {}


def _get_prog():
    if "nc" not in _PROG_CACHE:
        p = Prog()
        _PROG_CACHE["nc"] = p.build()
    return _PROG_CACHE["nc"]


def kernel(**inputs):
    nc = _get_prog()
    in_maps = [_host_inputs(inputs, b) for b in range(NCORES)]
    res = run_bass_kernel_spmd(nc, in_maps, core_ids=list(range(NCORES)))
    out = np.stack([np.asarray(res.results[b]["y"], dtype=np.float32) for b in range(NCORES)], 0)
    return out
```

```python
import contextlib
import math
import numpy as np
import concourse.bass as bass
import concourse.mybir as mybir
from concourse.bass_utils import run_bass_kernel_spmd

F32 = mybir.dt.float32
BF16 = mybir.dt.bfloat16
AF = mybir.ActivationFunctionType
ALU = mybir.AluOpType

D = 1024
SEQ = 8192
CTXL = 256
DEPTH = 2
DFF = 2816
NIN = 2064
OFF_Z, OFF_XBC, OFF_DT, OFF_FNET, OFF_POOL = 0, 512, 1536, 1552, 1808
ALPHA = (2.0 * DEPTH) ** 0.25
EPS = 1e-6
GRID_W = 64
NCORES = 4
WCOLS = 2320
C_DT, C_POOL, C_FN = 1536, 1552, 1808
NEGBIG = -30000.0
EPOCH = 30000


class Buf:
    __slots__ = ("lw", "rd")

    def __init__(self):
        self.lw = None
        self.rd = []


class Sched:
    ENG = ["tensor", "vector", "scalar", "gpsimd", "sync"]

    def __init__(self, nc, stack):
        self.nc = nc
        self.stack = stack
        self.ops = {e: [] for e in self.ENG}
        self.cnt = {e: 0 for e in self.ENG}
        self.sems = {}
        self.waited = {e: {} for e in self.ENG}
        self.same = {"vector", "scalar", "gpsimd"}
        self.dcnt = {}
        self.dma_rr = {e: 0 for e in self.ENG}
        self.NDMA = 8
        self.last_ev = {}

    def sem(self, key):
        if key not in self.sems:
            self.sems[key] = self.stack.enter_context(self.nc.semaphore("s_%s_%s_%s" % key))
        return self.sems[key]

    def _waits(self, eng, reads, writes):
        deps = {}

        def add(ev):
            if ev is None:
                return
            k, v = ev
            if deps.get(k, 0) < v:
                deps[k] = v
        for b in reads:
            add(b.lw)
        for b in writes:
            add(b.lw)
            for r in b.rd:
                add(r)
        out = []
        for k, v in deps.items():
            if k[0] == eng and k[2] == "c" and eng not in self.same:
                continue
            if self.waited[eng].get(k, 0) >= v:
                continue
            self.waited[eng][k] = v
            out.append((self.sem(k), v))
        return out

    def _commit(self, ev, reads, writes):
        for b in reads:
            b.rd.append(ev)
            if len(b.rd) > 24:
                best = {}
                for k, v in b.rd:
                    if best.get(k, 0) < v:
                        best[k] = v
                b.rd = list(best.items())
        for b in writes:
            b.lw = ev
            b.rd = []
        self.last_ev[ev[0]] = ev[1]

    def op(self, eng, fn, reads=(), writes=()):
        waits = self._waits(eng, reads, writes)
        c = self.cnt[eng]
        self.cnt[eng] = c + 1
        key = (eng, c // EPOCH, "c")
        val = c % EPOCH + 1
        s = self.sem(key)

        def run(h, waits=waits, fn=fn, s=s):
            for (ws, wv) in waits:
                h.wait_ge(ws, wv)
            fn(h).then_inc(s, 1)
        self.ops[eng].append(run)
        self._commit((key, val), reads, writes)

    def dma(self, out, in_, reads=(), writes=(), eng="sync"):
        waits = self._waits(eng, reads, writes)
        i = self.dma_rr[eng]
        self.dma_rr[eng] = (i + 1) % self.NDMA
        key = (eng, i, "d")
        n = self.dcnt.get(key, 0) + 1
        self.dcnt[key] = n
        s = self.sem(key)
        prev = (n - 1) * 16
        if self.waited[eng].get(key, 0) < prev:
            self.waited[eng][key] = prev
        else:
            prev = 0

        def run(h, waits=waits, s=s, prev=prev, out=out, in_=in_):
            for (ws, wv) in waits:
                h.wait_ge(ws, wv)
            if prev > 0:
                h.wait_ge(s, prev)
            h.dma_start(out=out, in_=in_).then_inc(s, 16)
        self.ops[eng].append(run)
        self._commit((key, n * 16), reads, writes)

    def barrier(self):
        evs = dict(self.last_ev)
        for eng in self.ENG:
            waits = []
            for k, v in evs.items():
                if k[0] == eng and k[2] == "c":
                    continue
                if self.waited[eng].get(k, 0) >= v:
                    continue
                self.waited[eng][k] = v
                waits.append((self.sem(k), v))

            def run(h, waits=waits):
                for (ws, wv) in waits:
                    h.wait_ge(ws, wv)
            self.ops[eng].append(run)

    def finish(self, block):
        self.barrier()
        ops = self.ops

        @block.tensor
        def _(h):
            for f in ops["tensor"]:
                f(h)

        @block.vector
        def _(h):
            for f in ops["vector"]:
                f(h)

        @block.scalar
        def _(h):
            for f in ops["scalar"]:
                f(h)

        @block.gpsimd
        def _(h):
            for f in ops["gpsimd"]:
                f(h)

        @block.sync
        def _(h):
            for f in ops["sync"]:
                f(h)


class T:
    __slots__ = ("t", "b")

    def __init__(self, t):
        self.t = t
        self.b = Buf()

    def __getitem__(self, k):
        return self.t[k]


def _consts():
    c = {}
    c["ident"] = np.eye(128, dtype=np.float32)
    k = np.arange(128)
    U = (k[:, None] <= k[None, :]).astype(np.float32)
    Us = (k[:, None] < k[None, :]).astype(np.float32)
    c["tri"] = np.stack([U, -U, Us, np.ones((128, 128), np.float32)], 1).astype(np.float32)
    upat = np.zeros((128, 16, 128), np.float32)
    upat[:, 0:8, :] = U[:, None, :]
    upat[:, 8:16, :] = -Us[:, None, :]
    c["upat"] = upat
    neg = np.zeros((128, 16, 128), np.float32)
    neg[:, 0:8, :] = np.where(k[:, None] > k[None, :], NEGBIG, 0.0)[:, None, :]
    neg[:, 8:16, :] = np.where(k[:, None] < k[None, :], NEGBIG, 0.0)[:, None, :]
    c["negm"] = neg
    m = np.arange(64)
    ang = 2 * np.pi * np.outer(m, m) / 64.0
    nrm = 1.0 / math.sqrt(SEQ * 64.0)
    cc = np.cos(ang) * nrm
    sc = -np.sin(ang) * nrm
    c["chdft"] = np.stack([np.concatenate([cc, cc], 1), np.concatenate([sc, sc], 1)], 1).astype(np.float32)
    a128 = 2 * np.pi * np.outer(k, k) / 128.0
    C, S = np.cos(a128), np.sin(a128)
    c["dftA"] = np.stack([np.concatenate([C, -S], 1), np.concatenate([S, C], 1)], 1).astype(np.float32)
    l2 = np.arange(64)
    th = 2 * np.pi * np.outer(l2, k) / 8192.0
    c["twid"] = np.stack([np.cos(th), np.sin(th), -np.sin(th)], 1).astype(np.float32)
    a64 = 2 * np.pi * np.outer(l2, l2) / 64.0
    c["dftB"] = np.stack([np.cos(a64), np.sin(a64)], 1).astype(np.float32)
    kk = np.arange(256)
    a256 = 2 * np.pi * np.outer(kk, kk) / 256.0
    sc256 = math.sqrt(SEQ / CTXL)
    c256 = (np.cos(a256) * sc256).reshape(2, 128, 256).transpose(1, 0, 2)
    s256 = (np.sin(a256) * sc256).reshape(2, 128, 256).transpose(1, 0, 2)
    c["dftC"] = np.stack([c256, s256], 1).astype(np.float32)
    pt = np.zeros((128, 5, 4, 128), np.float32)
    for gi, win in enumerate((2, 4, 8, 16)):
        left = win // 2
        right = win - 1 - left
        for l in range(128):
            for rel, (off, first, last) in enumerate([(-128, False, False), (0, False, False), (128, False, False),
                                                      (0, True, False), (0, False, True)]):
                lo = l - left
                hi = l + right + 1
                if first:
                    lo = max(lo, 0)
                if last:
                    hi = min(hi, 128)
                cnt = hi - lo
                for s in range(lo, hi):
                    sl = s - off
                    if 0 <= sl < 128:
                        pt[sl, rel, gi, l] += 1.0 / cnt
                if off == 0:
                    pt[l, rel, gi, l] -= 1.0
    c["poolm"] = pt
    return c


_CONST = None


def _host_inputs(inp, b):
    global _CONST
    if _CONST is None:
        _CONST = _consts()
    f = np.float32
    A = np.ascontiguousarray
    d = dict(_CONST)
    d["x"] = A(inp["x"][b], dtype=f)
    d["ctx"] = A(inp["ctx"][b], dtype=f)
    cc = np.stack([np.asarray(inp["c"][b], f), np.asarray(inp["c_ctx"], f)], 1)
    d["cc"] = A(cc.reshape(8, 128, 2).transpose(1, 0, 2))
    d["w_ada"] = A(inp["w_ada"], dtype=f)
    d["b_ada"] = A(np.asarray(inp["b_ada"], f).reshape(DEPTH, 1, 6 * D))
    w_in = np.asarray(inp["w_in"], f)
    main = np.concatenate([w_in[:, :, OFF_Z:OFF_DT], w_in[:, :, OFF_DT:OFF_FNET]], 2)
    d["w_in_r"] = A(main.reshape(DEPTH, 8, 128, 1552).transpose(0, 2, 1, 3))
    wf = w_in[:, :, OFF_FNET:OFF_POOL]
    d["w_fT"] = A(wf.transpose(0, 2, 1).reshape(DEPTH, 2, 128, D).transpose(0, 2, 1, 3))
    wp = w_in[:, :, OFF_POOL:NIN]
    d["w_pT"] = A(wp.transpose(0, 2, 1).reshape(DEPTH, 2, 128, D).transpose(0, 2, 1, 3))
    d["fnet_w"] = A(np.asarray(inp["fnet_w"], f).transpose(0, 2, 1, 3))
    d["pool_w"] = A(inp["pool_w"], dtype=f)
    d["ssd_cw"] = A(np.asarray(inp["ssd_conv_w"], f).reshape(DEPTH, 3, 8, 128).transpose(0, 3, 2, 1))
    d["ssd_cb"] = A(np.asarray(inp["ssd_conv_b"], f).reshape(DEPTH, 8, 128).transpose(0, 2, 1))
    d["dt_bias"] = A(np.asarray(inp["ssd_dt_bias"], f).reshape(DEPTH, 1, 16))
    d["a_log"] = A(np.asarray(inp["ssd_a_log"], f).reshape(DEPTH, 1, 16))
    d["d_rep"] = A(np.repeat(np.asarray(inp["ssd_d"], f), 64, axis=1).reshape(DEPTH, 1, 512))
    rs = np.concatenate([np.asarray(inp["ssd_norm_w"], f), np.ones((DEPTH, 256), f),
                         np.asarray(inp["pool_scale"], f)], 1)
    d["rowscale"] = A(rs.reshape(DEPTH, 8, 128).transpose(0, 2, 1))
    d["w_out"] = A(np.asarray(inp["w_out"], f).reshape(DEPTH, 8, 128, D).transpose(0, 2, 1, 3))
    for n in ("ln1_g", "ln1_b", "ln2_g", "ln2_b"):
        d[n] = A(np.asarray(inp[n], f).reshape(DEPTH, 1, D))
    d["w_up"] = A(np.asarray(inp["ffn_w_up"], f).reshape(DEPTH, 8, 128, 2 * DFF).transpose(0, 2, 1, 3))
    d["w_down"] = A(np.asarray(inp["ffn_w_down"], f).reshape(DEPTH, 22, 128, D).transpose(0, 2, 1, 3))
    d["ffn_cw"] = A(np.asarray(inp["ffn_conv_w"], f).reshape(DEPTH, 9, 44, 128).transpose(0, 3, 2, 1))
    d["ffn_cb"] = A(np.asarray(inp["ffn_conv_b"], f).reshape(DEPTH, 44, 128).transpose(0, 2, 1))
    return d


_IN_SHAPES = {
    "x": [SEQ, D], "ctx": [CTXL, D], "cc": [128, 8, 2], "w_ada": [DEPTH, D, 6 * D], "b_ada": [DEPTH, 1, 6 * D],
    "w_in_r": [DEPTH, 128, 8, 1552], "w_fT": [DEPTH, 128, 2, D], "w_pT": [DEPTH, 128, 2, D],
    "fnet_w": [DEPTH, 64, 4, 64], "pool_w": [DEPTH, 4, 64, 64], "ssd_cw": [DEPTH, 128, 8, 3],
    "ssd_cb": [DEPTH, 128, 8], "dt_bias": [DEPTH, 1, 16], "a_log": [DEPTH, 1, 16], "d_rep": [DEPTH, 1, 512],
    "rowscale": [DEPTH, 128, 8], "w_out": [DEPTH, 128, 8, D], "ln1_g": [DEPTH, 1, D], "ln1_b": [DEPTH, 1, D],
    "ln2_g": [DEPTH, 1, D], "ln2_b": [DEPTH, 1, D], "w_up": [DEPTH, 128, 8, 2 * DFF],
    "w_down": [DEPTH, 128, 22, D], "ffn_cw": [DEPTH, 128, 44, 9], "ffn_cb": [DEPTH, 128, 44],
    "ident": [128, 128], "tri": [128, 4, 128], "upat": [128, 16, 128], "negm": [128, 16, 128],
    "chdft": [64, 2, 128], "dftA": [128, 2, 256], "twid": [64, 3, 128], "dftB": [64, 2, 64],
    "dftC": [128, 2, 2, 256], "poolm": [128, 5, 4, 128],
}


class Seq:
    def __init__(self, name, L, row, toff):
        self.name = name
        self.L = L
        self.nt = L // 128
        self.row = row
        self.toff = toff
        self.scr = {}


class Prog:
    def __init__(self, dbg=(), nlayers=DEPTH, stop=None):
        self.dbg = set(dbg)
        self.nlayers = nlayers
        self.stop = stop
        self.nc = bass.Bass("TRN2", target_bir_lowering=False)
        nc = self.nc
        self.din = {n: nc.dram_tensor(n, s, F32, kind="ExternalInput").ap() for n, s in _IN_SHAPES.items()}
        self.out = nc.dram_tensor("y", [SEQ, D], F32, kind="ExternalOutput").ap()
        self.dbg_outs = {}

    def dram(self, name, shape, dt):
        kind = "ExternalOutput" if name in self.dbg else "Internal"
        t = self.nc.dram_tensor(name, shape, dt, kind=kind)
        if name in self.dbg:
            self.dbg_outs[name] = (shape, dt)
        return T(t.ap())

    def sb(self, st, name, shape, dt=F32):
        self._n += 1
        return T(st.enter_context(self.nc.sbuf_tensor("%s_%d" % (name, self._n), shape, dt)))

    def ps(self, st, name, shape, dt=F32):
        self._n += 1
        return T(st.enter_context(self.nc.psum_tensor("%s_%d" % (name, self._n), shape, dt)))

    def V(self, fn, r=(), w=()):
        self.S.op("vector", fn, [t.b for t in r], [t.b for t in w])

    def G(self, fn, r=(), w=()):
        self.S.op("gpsimd", fn, [t.b for t in r], [t.b for t in w])

    def A(self, fn, r=(), w=()):
        self.S.op("scalar", fn, [t.b for t in r], [t.b for t in w])

    def P(self, fn, r=(), w=()):
        self.S.op("tensor", fn, [t.b for t in r], [t.b for t in w])

    def dma(self, out, in_, r=(), w=()):
        self.S.dma(out, in_, [t.b for t in r], [t.b for t in w])

    def mm(self, out_t, out_ap, lhsT, rhs, start, stop, r):
        self.P(lambda h: h.matmul(out_ap, lhsT=lhsT, rhs=rhs, start=start, stop=stop), r, [out_t])

    def cast_load(self, st, name, src_ap, shape, dt=BF16, eng="V"):
        tmp = self.sb(st, name + "_f", shape, F32)
        dst = self.sb(st, name, shape, dt)
        self.dma(tmp[:], src_ap, w=[tmp])
        (self.V if eng == "V" else self.G)(lambda h: h.tensor_copy(out=dst[:], in_=tmp[:]), [tmp], [dst])
        return dst

    def rstd(self, var_ap, out_t, out_ap, r, scale=1.0):
        self.A(lambda h: h.activation(out=out_ap, in_=var_ap, func=AF.Ln, bias=self.epsT[:], scale=scale),
               list(r) + [self.epsT], [out_t])
        self.A(lambda h: h.activation(out=out_ap, in_=out_ap, func=AF.Exp, scale=-0.5), [out_t], [out_t])

    def layernorm(self, st, src_t, src_ap, dst_t, dst_ap, tag, small):
        st6, mv, rs = small
        for hh in range(2):
            self.V(lambda h, hh=hh: h.bn_stats(out=st6[:, hh, :], in_=src_ap[:, hh * 512:(hh + 1) * 512]), [src_t], [st6])
        self.V(lambda h: h.bn_aggr(out=mv[:], in_=st6[:].rearrange("p a b -> p (a b)")), [st6], [mv])
        self.rstd(mv[:, 1:2], rs, rs[:], [mv])
        self.V(lambda h: h.tensor_scalar(out=dst_ap, in0=src_ap, scalar1=mv[:, 0:1], scalar2=rs[:, 0:1],
                                         op0=ALU.subtract, op1=ALU.mult), [src_t, mv, rs], [dst_t])

    def build(self):
        nc = self.nc
        self._n = 0
        with contextlib.ExitStack() as gst:
            self.S = Sched(nc, gst)
            S = self.S
            din = self.din
            lat = Seq("lat", SEQ, 0, 0)
            ctx = Seq("ctx", CTXL, 1, SEQ // 128)
            for sq in (lat, ctx):
                L = sq.L
                n = sq.name
                sq.scr = {
                    "SZ": self.dram("SZ_" + n, [L, 512], BF16),
                    "XBCT": self.dram("XBCT_" + n, [D, L], BF16),
                    "UF": self.dram("UF_" + n, [L, 512], BF16),
                    "MIX": self.dram("MIX_" + n, [L, 1024], BF16),
                    "YP": self.dram("YP_" + n, [L, 512], BF16),
                    "XMID": self.dram("XMID_" + n, [L, D], F32),
                    "H2T": self.dram("H2T_" + n, [D, L], BF16),
                    "X1": self.dram("X1_" + n, [L, D], F32),
                }
            self.MOD = self.dram("MOD", [DEPTH, 2, 6 * D], F32)
            self.WINd = self.dram("WINd", [128, 8, WCOLS], BF16)
            self.WOUTd = self.dram("WOUTd", [128, 8, D], BF16)
            self.WUPd = self.dram("WUPd", [22, 128, 2, 8, 128], BF16)
            self.WDNd = self.dram("WDNd", [128, 22, D], BF16)
            self.identf = self.sb(gst, "identf", [128, 128], F32)
            self.identb = self.sb(gst, "identb", [128, 128], BF16)
            self.dma(self.identf[:], din["ident"][:, :], w=[self.identf])
            self.V(lambda h: h.tensor_copy(out=self.identb[:], in_=self.identf[:]), [self.identf], [self.identb])
            self.epsT = self.sb(gst, "epsT", [128, 1], F32)
            self.V(lambda h: h.memset(self.epsT[:], EPS), [], [self.epsT])
            self.Sf = self.sb(gst, "Sf", [128, 512], F32)
            self.Sb = self.sb(gst, "Sb", [128, 512], F32)
            self.SfB = self.sb(gst, "SfB", [128, 512], BF16)
            self.SbB = self.sb(gst, "SbB", [128, 512], BF16)
            NTT = SEQ // 128 + CTXL // 128
            self.DT = self.sb(gst, "DT", [128, NTT, 16], F32)
            self.dAb = self.sb(gst, "dAb", [128, NTT, 16], BF16)

            for l in range(self.nlayers):
                last = (l == DEPTH - 1)
                xin_lat = T(din["x"]) if l == 0 else lat.scr["X1"]
                xin_ctx = T(din["ctx"]) if l == 0 else ctx.scr["X1"]
                xout_lat = T(self.out) if last else lat.scr["X1"]
                if l == 0:
                    self._xin0 = (xin_lat, xin_ctx)
                self.phase_mod(l)
                if self.stop == "mod":
                    break
                self.phase_weights(l)
                if self.stop == "weights":
                    break
                self.phase_A(l, ctx, xin_ctx)
                self.phase_A(l, lat, xin_lat)
                if "DTd" in self.dbg and l == 0:
                    dtd = self.dram("DTd", [128, SEQ // 128 + CTXL // 128, 16], F32)
                    self.dma(dtd[:, :, :], self.DT[:], r=[self.DT], w=[dtd])
                if self.stop == "A":
                    break
                self.V(lambda h: h.memset(self.Sf[:], 0.0), [], [self.Sf])
                self.V(lambda h: h.memset(self.Sb[:], 0.0), [], [self.Sb])
                self.V(lambda h: h.memset(self.SfB[:], 0.0), [], [self.SfB])
                self.V(lambda h: h.memset(self.SbB[:], 0.0), [], [self.SbB])
                self.phase_sweep(l, ctx, fwd=True)
                self.phase_sweep(l, ctx, fwd=False, full=not last)
                if self.stop == "Sctx":
                    break
                self.phase_sweep(l, lat, fwd=True)
                if self.stop == "Sfwd":
                    break
                if not last:
                    self.phase_F_ctx(l, ctx)
                self.phase_F_lat(l, lat)
                if self.stop == "F":
                    break
                self.phase_sweep(l, lat, fwd=False, full=True)
                if self.stop == "Sbwd":
                    break
                if not last:
                    self.phase_C(l, ctx, xin_ctx)
                self.phase_C(l, lat, xin_lat)
                if self.stop == "C":
                    break
                if not last:
                    self.phase_M(l, ctx, ctx.scr["X1"])
                self.phase_M(l, lat, xout_lat)
            with nc.Block() as block:
                S.finish(block)
        return nc

    def phase_mod(self, l):
        din = self.din
        with contextlib.ExitStack() as st:
            cct = self.sb(st, "cct", [128, 8, 2])
            scs = self.sb(st, "scs", [128, 8, 2])
            self.dma(cct[:], din["cc"][:, :, :], w=[cct])
            self.A(lambda h: h.activation(out=scs[:], in_=cct[:], func=AF.Silu), [cct], [scs])
            brow = self.sb(st, "brow", [1, 6 * D])
            self.dma(brow[:], din["b_ada"][l, :, :], w=[brow])
            ones2 = self.sb(st, "ones2", [1, 2])
            self.V(lambda h: h.memset(ones2[:], 1.0), [], [ones2])
            modsb = self.sb(st, "modsb", [2, 6 * D])
            wa = [self.sb(st, "wa", [128, 8, 512]) for _ in range(2)]
            pm = [self.ps(st, "pm", [2, 512]) for _ in range(2)]
            wsrc = din["w_ada"][l].rearrange("(kc p) n -> p kc n", p=128)
            for cb in range(12):
                w_ = wa[cb % 2]
                p_ = pm[cb % 2]
                self.dma(w_[:], wsrc[:, :, cb * 512:(cb + 1) * 512], w=[w_])
                for kc in range(8):
                    self.mm(p_, p_[:], scs[:, kc, :], w_[:, kc, :], kc == 0, False, [scs, w_])
                self.mm(p_, p_[:], ones2[:], brow[:, cb * 512:(cb + 1) * 512], False, True, [ones2, brow])
                add = 1.0 if cb in (2, 3, 8, 9) else 0.0
                self.V(lambda h, p_=p_, cb=cb, add=add: h.tensor_scalar(
                    out=modsb[:, cb * 512:(cb + 1) * 512], in0=p_[:], scalar1=add, scalar2=None, op0=ALU.add),
                    [p_], [modsb])
            self.dma(self.MOD[l], modsb[:], r=[modsb], w=[self.MOD])
            self.S.barrier()

    def phase_weights(self, l):
        din = self.din
        with contextlib.ExitStack() as st:
            for half in range(2):
                wf32 = self.sb(st, "wi32", [128, 4, 1552])
                wb16 = self.sb(st, "wi16", [128, 4, 1552], BF16)
                self.dma(wf32[:], din["w_in_r"][l, :, half * 4:(half + 1) * 4, :], w=[wf32])
                for kc in range(4):
                    eng = self.V if kc % 2 == 0 else self.G
                    eng(lambda h, kc=kc, wf32=wf32, wb16=wb16: h.tensor_copy(out=wb16[:, kc, :], in_=wf32[:, kc, :]),
                        [wf32], [wb16])
                self.dma(self.WINd[:, half * 4:(half + 1) * 4, 0:1552], wb16[:], r=[wb16], w=[self.WINd])
            chd = self.sb(st, "chd", [64, 2, 128])
            self.dma(chd[:], din["chdft"][:, :, :], w=[chd])
            fnw = self.sb(st, "fnw", [64, 4, 64])
            self.dma(fnw[:], din["fnet_w"][l, :, :, :], w=[fnw])
            pab = self.ps(st, "pab", [128, 4, 128])
            for g in range(4):
                for ri in range(2):
                    self.mm(pab, pab[:, g, ri * 64:(ri + 1) * 64], chd[:, ri, :], fnw[:, g, :], True, True, [chd, fnw])
            wfT = self.sb(st, "wfT", [128, 2, D])
            wpT = self.sb(st, "wpT", [128, 2, D])
            self.dma(wfT[:], din["w_fT"][l, :, :, :], w=[wfT])
            self.dma(wpT[:], din["w_pT"][l, :, :, :], w=[wpT])
            wfold = self.sb(st, "wfold", [128, 8, 768], BF16)
            pf = [self.ps(st, "pfold", [128, 512]) for _ in range(2)]
            bdf = self.sb(st, "bdf", [128, 2, 256])
            bdp = self.sb(st, "bdp", [128, 2, 128])
            self.V(lambda h: h.memset(bdf[:], 0.0), [], [bdf])
            self.V(lambda h: h.memset(bdp[:], 0.0), [], [bdp])
            for j in range(2):
                for gp in range(2):
                    g = 2 * j + gp
                    self.V(lambda h, j=j, gp=gp, g=g: h.tensor_copy(
                        out=bdf[gp * 64:(gp + 1) * 64, j, gp * 128:(gp + 1) * 128],
                        in_=pab[gp * 64:(gp + 1) * 64, g, :]), [pab], [bdf])
                    self.dma(bdp[gp * 64:(gp + 1) * 64, j, gp * 64:(gp + 1) * 64], din["pool_w"][l, g, :, :], w=[bdp])
            i = 0
            for kc in range(8):
                p_ = pf[i % 2]
                i += 1
                for j in range(2):
                    self.mm(p_, p_[:, j * 256:(j + 1) * 256], wfT[:, j, kc * 128:(kc + 1) * 128], bdf[:, j, :],
                            True, True, [wfT, bdf])
                self.A(lambda h, p_=p_, kc=kc: h.copy(out=wfold[:, kc, 256:768], in_=p_[:]), [p_], [wfold])
                p_ = pf[i % 2]
                i += 1
                for j in range(2):
                    self.mm(p_, p_[:, j * 128:(j + 1) * 128], wpT[:, j, kc * 128:(kc + 1) * 128], bdp[:, j, :],
                            True, True, [wpT, bdp])
                self.V(lambda h, p_=p_, kc=kc: h.tensor_copy(out=wfold[:, kc, 0:256], in_=p_[:, 0:256]), [p_], [wfold])
            self.dma(self.WINd[:, :, 1552:WCOLS], wfold[:], r=[wfold], w=[self.WINd])
            rsc = self.sb(st, "rsc", [128, 8])
            self.dma(rsc[:], din["rowscale"][l, :, :], w=[rsc])
            for half in range(2):
                wo32 = self.sb(st, "wo32", [128, 4, D])
                wo16 = self.sb(st, "wo16", [128, 4, D], BF16)
                self.dma(wo32[:], din["w_out"][l, :, half * 4:(half + 1) * 4, :], w=[wo32])
                for kc in range(4):
                    c = half * 4 + kc
                    self.V(lambda h, kc=kc, c=c, wo32=wo32, wo16=wo16: h.tensor_scalar(
                        out=wo16[:, kc, :], in0=wo32[:, kc, :], scalar1=rsc[:, c:c + 1], scalar2=None, op0=ALU.mult),
                        [wo32, rsc], [wo16])
                self.dma(self.WOUTd[:, half * 4:(half + 1) * 4, :], wo16[:], r=[wo16], w=[self.WOUTd])
            self.S.barrier()
        with contextlib.ExitStack() as st:
            u32 = [self.sb(st, "u32", [128, 8, 256]) for _ in range(3)]
            u16 = [self.sb(st, "u16", [128, 2, 8, 128], BF16) for _ in range(3)]
            i = 0
            for vg in range(2):
                for pb in range(11):
                    a, b_ = u32[i % 3], u16[i % 3]
                    c0 = vg * DFF + pb * 256
                    self.dma(a[:], din["w_up"][l, :, :, c0:c0 + 256], w=[a])
                    eng = (self.V, self.G, self.A)[i % 3]
                    if i % 3 == 2:
                        eng(lambda h, a=a, b_=b_: h.copy(out=b_[:], in_=a[:].rearrange("p k (q c) -> p q k c", q=2)), [a], [b_])
                    else:
                        eng(lambda h, a=a, b_=b_: h.tensor_copy(out=b_[:], in_=a[:].rearrange("p k (q c) -> p q k c", q=2)), [a], [b_])
                    for q in range(2):
                        self.dma(self.WUPd[2 * pb + q, :, vg, :, :], b_[:, q, :, :], r=[b_], w=[self.WUPd])
                    i += 1
            d32 = [self.sb(st, "d32", [128, 2, D]) for _ in range(2)]
            d16 = [self.sb(st, "d16", [128, 2, D], BF16) for _ in range(2)]
            for i in range(11):
                a, b_ = d32[i % 2], d16[i % 2]
                self.dma(a[:], din["w_down"][l, :, 2 * i:2 * i + 2, :], w=[a])
                eng = self.V if i % 2 == 0 else self.G
                eng(lambda h, a=a, b_=b_: h.tensor_copy(out=b_[:], in_=a[:]), [a], [b_])
                self.dma(self.WDNd[:, 2 * i:2 * i + 2, :], b_[:], r=[b_], w=[self.WDNd])
            self.S.barrier()

    def phase_A(self, l, sq, xin):
        din = self.din
        scr = sq.scr
        with contextlib.ExitStack() as st:
            WIN = self.sb(st, "WIN", [128, 8, WCOLS], BF16)
            self.dma(WIN[:], self.WINd[:, :, :], r=[self.WINd], w=[WIN])
            scp = self.sb(st, "scp", [128, D])
            sh = self.sb(st, "sh", [128, D])
            self.dma(scp[:], self.MOD[l, sq.row:sq.row + 1, 1024:2048].partition_broadcast(128), r=[self.MOD], w=[scp])
            self.dma(sh[:], self.MOD[l, sq.row:sq.row + 1, 0:1024].partition_broadcast(128), r=[self.MOD], w=[sh])
            dtb = self.sb(st, "dtb", [128, 16])
            nega = self.sb(st, "nega", [128, 16])
            self.dma(dtb[:], din["dt_bias"][l, :, :].partition_broadcast(128), w=[dtb])
            self.dma(nega[:], din["a_log"][l, :, :].partition_broadcast(128), w=[nega])
            self.A(lambda h: h.activation(out=nega[:], in_=nega[:], func=AF.Exp), [nega], [nega])
            self.V(lambda h: h.tensor_scalar(out=nega[:], in0=nega[:], scalar1=-1.0, scalar2=None, op0=ALU.mult), [nega], [nega])
            xt = [self.sb(st, "xt", [128, D]) for _ in range(2)]
            xn = [self.sb(st, "xn", [128, D]) for _ in range(2)]
            hb = [self.sb(st, "hb", [128, D], BF16) for _ in range(2)]
            small = [(self.sb(st, "st6", [128, 2, 6]), self.sb(st, "mv", [128, 2]), self.sb(st, "rs", [128, 1])) for _ in range(2)]
            stw = min(4, sq.nt)
            hT = [self.sb(st, "hT", [128, 8, stw * 128], BF16) for _ in range(2)]
            ptr = [self.ps(st, "ptr", [128, 8, 128], BF16) for _ in range(2)]
            pz = self.ps(st, "pz", [128, 512])
            pfn = self.ps(st, "pfn", [128, 512])
            ppd = self.ps(st, "ppd", [128, 512])
            pxb = [self.ps(st, "pxb", [128, 512]) for _ in range(2)]
            szt = [self.sb(st, "szt", [128, 512], BF16) for _ in range(2)]
            uft = [self.sb(st, "uft", [128, 512], BF16) for _ in range(2)]
            upt = [self.sb(st, "upt", [128, 256], BF16) for _ in range(2)]
            dtr = [self.sb(st, "dtr", [128, 16]) for _ in range(2)]
            xbst = [self.sb(st, "xbst", [128, 8, stw * 128], BF16) for _ in range(2)]
            xbd = scr["XBCT"].t.rearrange("(c p) n -> p c n", p=128)
            for sti in range(sq.nt // stw):
                hT_ = hT[sti % 2]
                for ti in range(stw):
                    t = sti * stw + ti
                    k = t % 2
                    tt = sq.toff + t
                    self.dma(xt[k][:], xin.t[t * 128:(t + 1) * 128, :], r=[xin], w=[xt[k]])
                    self.layernorm(st, xt[k], xt[k][:], xn[k], xn[k][:], "a", small[k])
                    self.G(lambda h, k=k: h.tensor_tensor(out=xn[k][:], in0=xn[k][:], in1=scp[:], op=ALU.mult), [xn[k], scp], [xn[k]])
                    self.G(lambda h, k=k: h.tensor_tensor(out=hb[k][:], in0=xn[k][:], in1=sh[:], op=ALU.add), [xn[k], sh], [hb[k]])
                    for kc in range(8):
                        self.P(lambda h, k=k, kc=kc: h.transpose(out=ptr[k][:, kc, :], in_=hb[k][:, kc * 128:(kc + 1) * 128],
                                                                identity=self.identb[:]), [hb[k], self.identb], [ptr[k]])
                    self.A(lambda h, k=k, ti=ti, hT_=hT_: h.copy(out=hT_[:, :, ti * 128:(ti + 1) * 128], in_=ptr[k][:]), [ptr[k]], [hT_])
                    for kc in range(8):
                        self.mm(pz, pz[:], hT_[:, kc, ti * 128:(ti + 1) * 128], WIN[:, kc, 0:512], kc == 0, kc == 7, [hT_, WIN])
                    for kc in range(8):
                        self.mm(ppd, ppd[:, 0:272], hT_[:, kc, ti * 128:(ti + 1) * 128], WIN[:, kc, C_DT:C_FN], kc == 0, kc == 7, [hT_, WIN])
                    for kc in range(8):
                        self.mm(pfn, pfn[:], hT_[:, kc, ti * 128:(ti + 1) * 128], WIN[:, kc, C_FN:WCOLS], kc == 0, kc == 7, [hT_, WIN])
                    self.V(lambda h, k=k: h.tensor_tensor(out=dtr[k][:], in0=ppd[:, 0:16], in1=dtb[:], op=ALU.add), [ppd, dtb], [dtr[k]])
                    self.A(lambda h, k=k: h.activation(out=dtr[k][:], in_=dtr[k][:], func=AF.Exp), [dtr[k]], [dtr[k]])
                    self.A(lambda h, k=k, tt=tt: h.activation(out=self.DT[:, tt, :], in_=dtr[k][:], func=AF.Ln, bias=1.0), [dtr[k]], [self.DT])
                    self.V(lambda h, tt=tt: h.tensor_tensor(out=self.dAb[:, tt, :], in0=self.DT[:, tt, :], in1=nega[:], op=ALU.mult),
                           [self.DT, nega], [self.dAb])
                    self.A(lambda h, k=k: h.activation(out=szt[k][:], in_=pz[:], func=AF.Silu), [pz], [szt[k]])
                    self.dma(scr["SZ"][t * 128:(t + 1) * 128, :], szt[k][:], r=[szt[k]], w=[scr["SZ"]])
                    self.V(lambda h, k=k: h.tensor_copy(out=uft[k][:], in_=pfn[:]), [pfn], [uft[k]])
                    self.dma(scr["UF"][t * 128:(t + 1) * 128, :], uft[k][:], r=[uft[k]], w=[scr["UF"]])
                    self.V(lambda h, k=k: h.tensor_copy(out=upt[k][:], in_=ppd[:, 16:272]), [ppd], [upt[k]])
                    self.dma(scr["MIX"][t * 128:(t + 1) * 128, 768:1024], upt[k][:], r=[upt[k]], w=[scr["MIX"]])
                xb_ = xbst[sti % 2]
                for ch in range(8):
                    p_ = pxb[ch % 2]
                    for kc in range(8):
                        self.mm(p_, p_[:, 0:stw * 128], WIN[:, kc, 512 + ch * 128:512 + (ch + 1) * 128], hT_[:, kc, :],
                                kc == 0, kc == 7, [hT_, WIN])
                    if ch % 2 == 0:
                        self.A(lambda h, p_=p_, ch=ch, xb_=xb_: h.copy(out=xb_[:, ch, :], in_=p_[:, 0:stw * 128]), [p_], [xb_])
                    else:
                        self.V(lambda h, p_=p_, ch=ch, xb_=xb_: h.tensor_copy(out=xb_[:, ch, :], in_=p_[:, 0:stw * 128]), [p_], [xb_])
                c0 = sti * stw * 128
                self.dma(xbd[:, :, c0:c0 + stw * 128], xb_[:], r=[xb_], w=[scr["XBCT"]])
            self.S.barrier()

    def phase_sweep(self, l, sq, fwd, full=True):
        din = self.din
        scr = sq.scr
        with contextlib.ExitStack() as st:
            cw = self.sb(st, "cw", [128, 8, 3])
            cb = self.sb(st, "cb", [128, 8])
            self.dma(cw[:], din["ssd_cw"][l, :, :, :], w=[cw])
            self.dma(cb[:], din["ssd_cb"][l, :, :], w=[cb])
            DG = self.sb(st, "DG", [128, 8, 3, 128], BF16)
            for ch in range(8):
                for tp in range(3):
                    self.V(lambda h, ch=ch, tp=tp: h.tensor_scalar(out=DG[:, ch, tp, :], in0=self.identf[:],
                                                                   scalar1=cw[:, ch, tp:tp + 1], scalar2=None, op0=ALU.mult),
                           [self.identf, cw], [DG])
            tri = self.cast_load(st, "tri", din["tri"][:, :, :], [128, 4, 128])
            if fwd:
                upat = self.cast_load(st, "upat", din["upat"][:, :, :], [128, 16, 128], eng="G")
                negm = self.cast_load(st, "negm", din["negm"][:, :, :], [128, 16, 128], eng="G")
                dtile = self.sb(st, "dtile", [128, 512])
                self.dma(dtile[:], din["d_rep"][l, :, :].partition_broadcast(128), w=[dtile])
            U, NU, Us, ONES = (tri[:, i, :] for i in range(4))
            Sx = self.Sf if fwd else self.Sb
            SxB = self.SfB if fwd else self.SbB
            xin = [self.sb(st, "xin", [128, 8, 130], BF16) for _ in range(2)]
            xc = [self.sb(st, "xc", [128, 8, 128], BF16) for _ in range(2)]
            XB = [self.sb(st, "XB", [128, 768], BF16) for _ in range(2)]
            pc0 = self.ps(st, "pc0", [128, 4, 128])
            pc1 = self.ps(st, "pc1", [128, 4, 128])
            ptr = self.ps(st, "ptr", [128, 1024], BF16)
            psm = self.ps(st, "psm", [128, 512])
            py = self.ps(st, "py", [128, 512])
            arg = self.sb(st, "arg", [128, 24])
            ex = self.sb(st, "ex", [128, 24])
            wv = self.sb(st, "wv", [128, 8])
            Xw = self.sb(st, "Xw", [128, 512], BF16)
            yo = self.sb(st, "yo", [128, 512])
            if fwd:
                pseg = [self.ps(st, "pseg", [128, 4, 128]) for _ in range(3)]
                rhs1 = self.sb(st, "rhs1", [128, 16, 128], BF16)
                LT = self.sb(st, "LT", [128, 16, 128], BF16)
                MT = self.sb(st, "MT", [128, 16, 128], BF16)
                Xf = self.sb(st, "Xf", [128, 512], BF16)
                Xb = self.sb(st, "Xb", [128, 512], BF16)
                XD = self.sb(st, "XD", [128, 512], BF16)
                ypo = [self.sb(st, "ypo", [128, 512], BF16) for _ in range(2)]
            else:
                ypi = [self.sb(st, "ypi", [128, 512], BF16) for _ in range(2)]
                szi = [self.sb(st, "szi", [128, 512], BF16) for _ in range(2)]
                yz = self.sb(st, "yz", [128, 512])
                sqj = self.sb(st, "sqj", [128, 512])
                ss = self.sb(st, "ss", [128, 2])
                rg = self.sb(st, "rg", [128, 2])
                yn = [self.sb(st, "yn", [128, 512], BF16) for _ in range(2)]
            xbd = scr["XBCT"].t.rearrange("(c p) n -> p c n", p=128)
            order = list(range(sq.nt)) if fwd else list(range(sq.nt - 1, -1, -1))
            for i, t in enumerate(order):
                k = i % 2
                tt = sq.toff + t
                xi, xc_, XB_ = xin[k], xc[k], XB[k]
                lo_, hi_ = max(t * 128 - 1, 0), min(t * 128 + 129, sq.L)
                o_ = lo_ - (t * 128 - 1)
                if o_ > 0:
                    self.G(lambda h, xi=xi: h.memset(xi[:, :, 0:1], 0.0), [], [xi])
                if hi_ < t * 128 + 129:
                    self.G(lambda h, xi=xi: h.memset(xi[:, :, 129:130], 0.0), [], [xi])
                self.dma(xi[:, :, o_:o_ + hi_ - lo_], xbd[:, :, lo_:hi_], r=[scr["XBCT"]], w=[xi])
                if (not fwd) and full:
                    self.dma(ypi[k][:], scr["YP"][t * 128:(t + 1) * 128, :], r=[scr["YP"]], w=[ypi[k]])
                    self.dma(szi[k][:], scr["SZ"][t * 128:(t + 1) * 128, :], r=[scr["SZ"]], w=[szi[k]])
                for ch in range(8):
                    pc = pc0 if ch < 4 else pc1
                    for tp in range(3):
                        self.mm(pc, pc[:, ch % 4, :], DG[:, ch, tp, :], xi[:, ch, tp:tp + 128], tp == 0, tp == 2, [DG, xi])
                for ch in range(8):
                    pc = pc0 if ch < 4 else pc1
                    self.A(lambda h, pc=pc, ch=ch, xc_=xc_: h.activation(out=xc_[:, ch, :], in_=pc[:, ch % 4, :], func=AF.Silu,
                                                                        bias=cb[:, ch:ch + 1]), [pc, cb], [xc_])
                for ch in range(6):
                    self.P(lambda h, ch=ch, xc_=xc_: h.transpose(out=ptr[:, ch * 128:(ch + 1) * 128], in_=xc_[:, ch, :],
                                                                identity=self.identb[:]), [xc_, self.identb], [ptr])
                self.A(lambda h, XB_=XB_: h.copy(out=XB_[:], in_=ptr[:, 0:768]), [ptr], [XB_])
                dA_t = self.dAb[:, tt, :]
                self.mm(psm, psm[:, 0:16], U if fwd else Us, dA_t, True, True, [tri, self.dAb])
                self.mm(psm, psm[:, 16:32], ONES, dA_t, True, True, [tri, self.dAb])
                o = 0 if fwd else 8
                self.V(lambda h, o=o: h.tensor_copy(out=arg[:, 0:8], in_=psm[:, o:o + 8]), [psm], [arg])
                self.V(lambda h, o=o: h.tensor_copy(out=arg[:, 8:16], in_=psm[:, 16 + o:24 + o]), [psm], [arg])
                self.V(lambda h, o=o: h.tensor_tensor(out=arg[:, 16:24], in0=psm[:, 16 + o:24 + o], in1=arg[:, 0:8],
                                                      op=ALU.subtract), [psm, arg], [arg])
                self.A(lambda h: h.activation(out=ex[:], in_=arg[:], func=AF.Exp), [arg], [ex])
                dt_d = self.DT[:, tt, o:o + 8]
                if fwd:
                    self.V(lambda h, dt_d=dt_d: h.tensor_tensor(out=wv[:], in0=ex[:, 16:24], in1=dt_d, op=ALU.mult), [ex, self.DT], [wv])
                    ysc = ex[:, 0:8]
                else:
                    self.V(lambda h, dt_d=dt_d: h.tensor_tensor(out=wv[:], in0=ex[:, 0:8], in1=dt_d, op=ALU.mult), [ex, self.DT], [wv])
                    ysc = ex[:, 16:24]
                X3 = XB_[:, 0:512].rearrange("p (a b) -> p a b", a=8)

                def bc8(ap):
                    return ap.unsqueeze(2).to_broadcast([128, 8, 64])
                self.V(lambda h, X3=X3: h.tensor_tensor(out=Xw[:].rearrange("p (a b) -> p a b", a=8), in0=X3, in1=bc8(wv[:]), op=ALU.mult),
                       [XB_, wv], [Xw])
                if fwd:
                    dtf = self.DT[:, tt, 0:8]
                    dtb_ = self.DT[:, tt, 8:16]
                    self.V(lambda h, X3=X3, dtf=dtf: h.tensor_tensor(out=Xf[:].rearrange("p (a b) -> p a b", a=8), in0=X3, in1=bc8(dtf), op=ALU.mult),
                           [XB_, self.DT], [Xf])
                    self.G(lambda h, X3=X3, dtb_=dtb_: h.tensor_tensor(out=Xb[:].rearrange("p (a b) -> p a b", a=8), in0=X3, in1=bc8(dtb_), op=ALU.mult),
                           [XB_, self.DT], [Xb])
                    self.G(lambda h, XB_=XB_: h.tensor_tensor(out=XD[:], in0=XB_[:, 0:512], in1=dtile[:], op=ALU.mult), [XB_, dtile], [XD])
                    self.G(lambda h, tt=tt: h.tensor_tensor(out=rhs1[:], in0=upat[:], in1=self.dAb[:, tt, :].unsqueeze(2).to_broadcast([128, 16, 128]),
                                                            op=ALU.mult), [upat, self.dAb], [rhs1])
                    for g in range(2):
                        self.mm(psm, psm[:, 128 + g * 128:256 + g * 128], xc_[:, 4 + g, :], xc_[:, 6 + g, :], True, True, [xc_])
                    for q in range(4):
                        pq = pseg[q % 3]
                        lt2 = NU if q < 2 else Us
                        self.mm(pq, pq[:], ONES, rhs1[:, 4 * q:4 * q + 4, :], True, False, [tri, rhs1])
                        self.mm(pq, pq[:], lt2, self.dAb[:, tt, 4 * q:4 * q + 4].unsqueeze(2).to_broadcast([128, 4, 128]), False, False, [tri, self.dAb])
                        self.mm(pq, pq[:], self.identb[:], negm[:, 4 * q:4 * q + 4, :], False, True, [self.identb, negm])
                        self.A(lambda h, pq=pq, q=q: h.activation(out=LT[:, 4 * q:4 * q + 4, :], in_=pq[:], func=AF.Exp), [pq], [LT])
                        g = q % 2
                        self.V(lambda h, q=q, g=g: h.tensor_tensor(
                            out=MT[:, 4 * q:4 * q + 4, :], in0=LT[:, 4 * q:4 * q + 4, :],
                            in1=psm[:, 128 + g * 128:256 + g * 128].unsqueeze(1).to_broadcast([128, 4, 128]), op=ALU.mult),
                            [LT, psm], [MT])
                    self.mm(py, py[:], self.identb[:], XD[:], True, False, [self.identb, XD])
                    for hd in range(8):
                        self.mm(py, py[:, hd * 64:(hd + 1) * 64], MT[:, hd, :], Xf[:, hd * 64:(hd + 1) * 64], False, False, [MT, Xf])
                        self.mm(py, py[:, hd * 64:(hd + 1) * 64], MT[:, 8 + hd, :], Xb[:, hd * 64:(hd + 1) * 64], False, True, [MT, Xb])
                if fwd or full:
                    po = pc0
                    for g in range(2):
                        self.mm(po, po[:].rearrange("p a b -> p (a b)")[:, g * 256:(g + 1) * 256], xc_[:, 6 + g, :],
                                SxB[:, g * 256:(g + 1) * 256], True, True, [xc_, SxB])
                    self.V(lambda h, po=po, ysc=ysc: h.tensor_tensor(out=yo[:].rearrange("p (a b) -> p a b", a=8),
                                                                     in0=po[:].rearrange("p a (c b) -> p (a c) b", b=64),
                                                                     in1=bc8(ysc), op=ALU.mult), [po, ex], [yo])
                if fwd:
                    self.V(lambda h, k=k: h.tensor_tensor(out=ypo[k][:], in0=py[:], in1=yo[:], op=ALU.add), [py, yo], [ypo[k]])
                    self.dma(scr["YP"][t * 128:(t + 1) * 128, :], ypo[k][:], r=[ypo[k]], w=[scr["YP"]])
                elif full:
                    self.V(lambda h, k=k: h.tensor_tensor(out=yz[:], in0=yo[:], in1=ypi[k][:], op=ALU.add), [yo, ypi[k]], [yz])
                    self.G(lambda h, k=k: h.tensor_tensor(out=yz[:], in0=yz[:], in1=szi[k][:], op=ALU.mult), [yz, szi[k]], [yz])
                    for g in range(2):
                        self.A(lambda h, g=g: h.activation(out=sqj[:, g * 256:(g + 1) * 256], in_=yz[:, g * 256:(g + 1) * 256],
                                                           func=AF.Square, accum_out=ss[:, g:g + 1]), [yz], [sqj, ss])
                    self.rstd(ss[:], rg, rg[:], [ss], scale=1.0 / 256.0)
                    for g in range(2):
                        self.V(lambda h, g=g, k=k: h.tensor_scalar(out=yn[k][:, g * 256:(g + 1) * 256], in0=yz[:, g * 256:(g + 1) * 256],
                                                                  scalar1=rg[:, g:g + 1], scalar2=None, op0=ALU.mult), [yz, rg], [yn[k]])
                    self.dma(scr["MIX"][t * 128:(t + 1) * 128, 0:512], yn[k][:], r=[yn[k]], w=[scr["MIX"]])
                pd = pc1
                for g in range(2):
                    self.mm(pd, pd[:].rearrange("p a b -> p (a b)")[:, g * 256:(g + 1) * 256], XB_[:, 512 + g * 128:640 + g * 128],
                            Xw[:, g * 256:(g + 1) * 256], True, True, [XB_, Xw])
                self.V(lambda h: h.tensor_tensor(out=Sx[:].rearrange("p (a b) -> p a b", a=8), in0=Sx[:].rearrange("p (a b) -> p a b", a=8),
                                                 in1=bc8(ex[:, 8:16]), op=ALU.mult), [Sx, ex], [Sx])
                self.V(lambda h, pd=pd: h.tensor_tensor(out=Sx[:], in0=Sx[:], in1=pd[:].rearrange("p a b -> p (a b)"), op=ALU.add), [Sx, pd], [Sx])
                self.G(lambda h: h.tensor_copy(out=SxB[:], in_=Sx[:]), [Sx], [SxB])
            self.S.barrier()

    def phase_F_lat(self, l, sq):
        din = self.din
        scr = sq.scr
        with contextlib.ExitStack() as st:
            dA_ = self.cast_load(st, "dftA", din["dftA"][:, :, :], [128, 2, 256])
            dB_ = self.cast_load(st, "dftB", din["dftB"][:, :, :], [64, 2, 64])
            tw = self.sb(st, "tw", [64, 3, 128])
            self.dma(tw[:], din["twid"][:, :, :], w=[tw])
            Gt = [self.sb(st, "Gt", [128, 64, 128], BF16) for _ in range(2)]
            YF = self.sb(st, "YF", [128, 64, 256], BF16)
            pa = [self.ps(st, "pa", [64, 2, 2, 128]) for _ in range(2)]
            pb = [self.ps(st, "pb", [128, 8, 64]) for _ in range(2)]
            At = [self.sb(st, "At", [64, 2, 2, 128]) for _ in range(2)]
            Bt = [self.sb(st, "Bt", [64, 2, 2, 128]) for _ in range(2)]
            Yp = [self.sb(st, "Yp", [64, 2, 2, 128], BF16) for _ in range(2)]
            ufv = scr["UF"].t.rearrange("(a b) c -> a b c", b=64)
            n = 0
            for g in range(4):
                G_ = Gt[g % 2]
                self.dma(G_[:], ufv[:, :, g * 128:(g + 1) * 128], r=[scr["UF"]], w=[G_])
                for cp in range(32):
                    k = n % 2
                    n += 1
                    pa_, At_, Bt_, Yp_ = pa[k], At[k], Bt[k], Yp[k]
                    for ch in range(2):
                        d_ = 2 * cp + ch
                        self.mm(pa_, pa_[:, ch, :, :].rearrange("p a b -> p (a b)"), G_[:, :, d_], dA_[:, 0, :], True, False, [G_, dA_])
                        self.mm(pa_, pa_[:, ch, :, :].rearrange("p a b -> p (a b)"), G_[:, :, 64 + d_], dA_[:, 1, :], False, True, [G_, dA_])
                    trb = tw[:, 0, :].unsqueeze(1).unsqueeze(1).to_broadcast([64, 2, 2, 128])
                    self.V(lambda h, pa_=pa_, At_=At_, trb=trb: h.tensor_tensor(out=At_[:], in0=pa_[:], in1=trb, op=ALU.mult), [pa_, tw], [At_])
                    self.V(lambda h, pa_=pa_, Bt_=Bt_: h.tensor_tensor(out=Bt_[:, :, 0, :], in0=pa_[:, :, 1, :],
                                                                      in1=tw[:, 1, :].unsqueeze(1).to_broadcast([64, 2, 128]), op=ALU.mult), [pa_, tw], [Bt_])
                    self.V(lambda h, pa_=pa_, Bt_=Bt_: h.tensor_tensor(out=Bt_[:, :, 1, :], in0=pa_[:, :, 0, :],
                                                                      in1=tw[:, 2, :].unsqueeze(1).to_broadcast([64, 2, 128]), op=ALU.mult), [pa_, tw], [Bt_])
                    self.G(lambda h, At_=At_, Bt_=Bt_, Yp_=Yp_: h.tensor_tensor(out=Yp_[:], in0=At_[:], in1=Bt_[:], op=ALU.add), [At_, Bt_], [Yp_])
                    pb_ = pb[(cp // 4) % 2]
                    for ch in range(2):
                        c8 = (cp % 4) * 2 + ch
                        self.mm(pb_, pb_[:, c8, :], Yp_[:, ch, 0, :], dB_[:, 0, :], True, False, [Yp_, dB_])
                        self.mm(pb_, pb_[:, c8, :], Yp_[:, ch, 1, :], dB_[:, 1, :], False, True, [Yp_, dB_])
                    if cp % 4 == 3:
                        c0 = g * 64 + (cp // 4) * 8
                        self.A(lambda h, pb_=pb_, c0=c0: h.copy(out=YF[:, :, c0:c0 + 8].rearrange("p k c -> p c k"), in_=pb_[:]), [pb_], [YF])
            self.dma(scr["MIX"].t.rearrange("(a b) c -> b a c", b=128)[:, :, 512:768], YF[:], r=[YF], w=[scr["MIX"]])
            self.S.barrier()

    def phase_F_ctx(self, l, sq):
        din = self.din
        scr = sq.scr
        with contextlib.ExitStack() as st:
            dC = self.cast_load(st, "dftC", din["dftC"][:, :, :, :], [128, 2, 2, 256])
            gc = self.sb(st, "gc", [128, 2, 512], BF16)
            self.dma(gc[:], scr["UF"].t.rearrange("(c p) n -> p c n", p=128), r=[scr["UF"]], w=[gc])
            pf = [self.ps(st, "pfc", [128, 256]) for _ in range(2)]
            yf = [self.sb(st, "yfc", [128, 256], BF16) for _ in range(2)]
            for kt in range(2):
                for g in range(4):
                    i = 0
                    for lc in range(2):
                        for ri in range(2):
                            self.mm(pf[kt], pf[kt][:, g * 64:(g + 1) * 64], dC[:, ri, lc, kt * 128:(kt + 1) * 128],
                                    gc[:, lc, g * 128 + ri * 64:g * 128 + ri * 64 + 64], i == 0, i == 3, [dC, gc])
                            i += 1
                self.V(lambda h, kt=kt: h.tensor_copy(out=yf[kt][:], in_=pf[kt][:]), [pf[kt]], [yf[kt]])
                self.dma(scr["MIX"][kt * 128:(kt + 1) * 128, 512:768], yf[kt][:], r=[yf[kt]], w=[scr["MIX"]])
            self.S.barrier()

    def phase_C(self, l, sq, xin):
        din = self.din
        scr = sq.scr
        with contextlib.ExitStack() as st:
            WOUT = self.sb(st, "WOUT", [128, 8, D], BF16)
            self.dma(WOUT[:], self.WOUTd[:, :, :], r=[self.WOUTd], w=[WOUT])
            PT = self.cast_load(st, "poolm", din["poolm"][:, :, :, :], [128, 5, 4, 128])

            def bct(name, src):
                t_ = self.sb(st, name, [128, D])
                self.dma(t_[:], src.partition_broadcast(128), r=[self.MOD], w=[t_])
                return t_
            g1t = bct("g1t", self.MOD[l, sq.row:sq.row + 1, 2048:3072])
            sc2 = bct("sc2", self.MOD[l, sq.row:sq.row + 1, 4096:5120])
            sh2 = bct("sh2", self.MOD[l, sq.row:sq.row + 1, 3072:4096])
            lng = bct("lng", din["ln1_g"][l, :, :])
            lnb = bct("lnb", din["ln1_b"][l, :, :])
            xt = [self.sb(st, "xt", [128, D]) for _ in range(2)]
            mx = [self.sb(st, "mx", [128, 768], BF16) for _ in range(2)]
            upw = [self.sb(st, "upw", [128, 3, 256], BF16) for _ in range(2)]
            mixT = [self.sb(st, "mixT", [128, 8, 128], BF16) for _ in range(2)]
            v = [self.sb(st, "v", [128, D]) for _ in range(2)]
            xm = [self.sb(st, "xm", [128, D]) for _ in range(2)]
            hn = [self.sb(st, "hn", [128, D]) for _ in range(2)]
            h2 = [self.sb(st, "h2", [128, D], BF16) for _ in range(2)]
            h2T = [self.sb(st, "h2T", [128, 8, 128], BF16) for _ in range(2)]
            small = [(self.sb(st, "st6", [128, 2, 6]), self.sb(st, "mv", [128, 2]), self.sb(st, "rs", [128, 1])) for _ in range(4)]
            ptm = self.ps(st, "ptm", [128, 8, 128], BF16)
            ppl = self.ps(st, "ppl", [128, 2, 128])
            pout = [self.ps(st, "pout", [128, D]) for _ in range(2)]
            pth = self.ps(st, "pth", [128, 8, 128], BF16)
            h2v = scr["H2T"].t.rearrange("(c p) n -> p c n", p=128)
            for t in range(sq.nt):
                k = t % 2
                self.dma(xt[k][:], xin.t[t * 128:(t + 1) * 128, :], r=[xin], w=[xt[k]])
                self.dma(mx[k][:], scr["MIX"][t * 128:(t + 1) * 128, 0:768], r=[scr["MIX"]], w=[mx[k]])
                rels = []
                for r_, tn in enumerate((t - 1, t, t + 1)):
                    if 0 <= tn < sq.nt:
                        self.dma(upw[k][:, r_, :], scr["MIX"][tn * 128:(tn + 1) * 128, 768:1024], r=[scr["MIX"]], w=[upw[k]])
                        rels.append(r_)
                for c in range(6):
                    self.P(lambda h, k=k, c=c: h.transpose(out=ptm[:, c, :], in_=mx[k][:, c * 128:(c + 1) * 128], identity=self.identb[:]),
                           [mx[k], self.identb], [ptm])
                self.A(lambda h, k=k: h.copy(out=mixT[k][:, 0:6, :], in_=ptm[:, 0:6, :]), [ptm], [mixT[k]])
                for g in range(4):
                    for i, r_ in enumerate(rels):
                        ridx = r_
                        if r_ == 1 and t == 0:
                            ridx = 3
                        if r_ == 1 and t == sq.nt - 1:
                            ridx = 4
                        self.mm(ppl, ppl[(g % 2) * 64:(g % 2) * 64 + 64, g // 2, :], upw[k][:, r_, g * 64:(g + 1) * 64], PT[:, ridx, g, :],
                                i == 0, i == len(rels) - 1, [upw[k], PT])
                self.V(lambda h, k=k: h.tensor_copy(out=mixT[k][:, 6:8, :], in_=ppl[:]), [ppl], [mixT[k]])
                po = pout[k]
                for half in range(2):
                    for c in range(8):
                        self.mm(po, po[:, half * 512:(half + 1) * 512], mixT[k][:, c, :], WOUT[:, c, half * 512:(half + 1) * 512],
                                c == 0, c == 7, [mixT[k], WOUT])
                self.V(lambda h, k=k, po=po: h.tensor_tensor(out=v[k][:], in0=po[:], in1=g1t[:], op=ALU.mult), [po, g1t], [v[k]])
                self.V(lambda h, k=k: h.scalar_tensor_tensor(out=v[k][:], in0=xt[k][:], scalar=ALPHA, in1=v[k][:], op0=ALU.mult, op1=ALU.add),
                       [xt[k], v[k]], [v[k]])
                self.layernorm(st, v[k], v[k][:], xm[k], xm[k][:], "c1", small[k])
                self.G(lambda h, k=k: h.tensor_tensor(out=xm[k][:], in0=xm[k][:], in1=lng[:], op=ALU.mult), [xm[k], lng], [xm[k]])
                self.G(lambda h, k=k: h.tensor_tensor(out=xm[k][:], in0=xm[k][:], in1=lnb[:], op=ALU.add), [xm[k], lnb], [xm[k]])
                self.dma(scr["XMID"][t * 128:(t + 1) * 128, :], xm[k][:], r=[xm[k]], w=[scr["XMID"]])
                self.layernorm(st, xm[k], xm[k][:], hn[k], hn[k][:], "c2", small[2 + k])
                self.G(lambda h, k=k: h.tensor_tensor(out=hn[k][:], in0=hn[k][:], in1=sc2[:], op=ALU.mult), [hn[k], sc2], [hn[k]])
                self.G(lambda h, k=k: h.tensor_tensor(out=h2[k][:], in0=hn[k][:], in1=sh2[:], op=ALU.add), [hn[k], sh2], [h2[k]])
                for kc in range(8):
                    self.P(lambda h, k=k, kc=kc: h.transpose(out=pth[:, kc, :], in_=h2[k][:, kc * 128:(kc + 1) * 128], identity=self.identb[:]),
                           [h2[k], self.identb], [pth])
                self.A(lambda h, k=k: h.copy(out=h2T[k][:], in_=pth[:]), [pth], [h2T[k]])
                self.dma(h2v[:, :, t * 128:(t + 1) * 128], h2T[k][:], r=[h2T[k]], w=[scr["H2T"]])
            self.S.barrier()

    def phase_M(self, l, sq, xout):
        din = self.din
        scr = sq.scr
        is_ctx = sq.name == "ctx"
        if is_ctx:
            rows, W, RB = 1, CTXL, 1
            taps = [(0, dx, 3 + (dx + 1)) for dx in (-1, 0, 1)]
        else:
            rows, W, RB = SEQ // GRID_W, GRID_W, 16
            taps = [(dy, dx, (dy + 1) * 3 + (dx + 1)) for dy in (-1, 0, 1) for dx in (-1, 0, 1)]
        nq = max(1, (RB * W) // 512)
        qrows = RB // nq if not is_ctx else 1
        qn = qrows * W
        ntb = (RB * W) // 128
        with contextlib.ExitStack() as st:
            WDN = self.sb(st, "WDN", [128, 22, D], BF16)
            self.dma(WDN[:], self.WDNd[:, :, :], r=[self.WDNd], w=[WDN])
            fcw = self.sb(st, "fcw", [128, 44, 9])
            fcb = self.sb(st, "fcb", [128, 44])
            self.dma(fcw[:], din["ffn_cw"][l, :, :, :], w=[fcw])
            self.dma(fcb[:], din["ffn_cb"][l, :, :], w=[fcb])

            def bct(name, src):
                t_ = self.sb(st, name, [128, D])
                self.dma(t_[:], src.partition_broadcast(128), r=[self.MOD], w=[t_])
                return t_
            g2t = bct("g2t", self.MOD[l, sq.row:sq.row + 1, 5120:6144])
            lng = bct("lng2", din["ln2_g"][l, :, :])
            lnb = bct("lnb2", din["ln2_b"][l, :, :])
            nhr = RB + 2
            hb = [self.sb(st, "hbk", [128, 8, nhr * W], BF16)]
            Abuf = [self.sb(st, "Abuf", [128, nhr, W + 2], BF16) for _ in range(4)]
            for a_ in Abuf:
                self.G(lambda h, a_=a_: h.memset(a_[:], 0.0), [], [a_])
            actT = self.sb(st, "actT", [128, 22, RB * W], BF16)
            wu = [self.sb(st, "wu", [128, 2, 8, 128], BF16) for _ in range(3)]
            dgt = [self.sb(st, "dgt", [128, len(taps), 128], BF16) for _ in range(3)]
            gl = [self.sb(st, "gl", [128, qn]) for _ in range(2)]
            npu = (nhr * W + 511) // 512
            pu = [self.ps(st, "pu", [128, npu * 512]) for _ in range(2)]
            pcv = [self.ps(st, "pcv", [128, 512]) for _ in range(2)]
            xmt = [self.sb(st, "xmt", [128, D]) for _ in range(2)]
            v = [self.sb(st, "vf", [128, D]) for _ in range(2)]
            xo = xmt
            small = [(self.sb(st, "st6", [128, 2, 6]), self.sb(st, "mv", [128, 2]), self.sb(st, "rs", [128, 1])) for _ in range(2)]
            h2v = scr["H2T"].t.rearrange("(c p) n -> p c n", p=128)
            nblk = rows // RB
            iu = 0
            icv = 0
            idg = 0
            for bi in range(nblk):
                r0, r1 = bi * RB, (bi + 1) * RB
                hr0, hr1 = max(r0 - 1, 0), min(r1 + 1, rows)
                nh = hr1 - hr0
                ar0 = hr0 - (r0 - 1)
                hb_ = hb[0]
                self.dma(hb_[:, :, 0:nh * W], h2v[:, :, hr0 * W:hr1 * W], r=[scr["H2T"]], w=[hb_])
                if bi == nblk - 1 and nblk > 1:
                    for a_ in Abuf:
                        self.G(lambda h, a_=a_: h.memset(a_[:, nhr - 1, :], 0.0), [], [a_])
                units = []
                for pr in range(22):
                    for vg in (1, 0):
                        units.append((pr, vg))
                state = {}

                def up_stage(pr, vg, hb_=hb_, nh=nh, ar0=ar0):
                    nonlocal iu, idg
                    wu_ = wu[pr % 3]
                    if vg == 1:
                        self.dma(wu_[:], self.WUPd[pr, :, :, :, :], r=[self.WUPd], w=[wu_])
                    cc = vg * 22 + pr
                    pu_ = pu[iu % 2]
                    iu += 1
                    A_ = Abuf[vg * 2 + (pr % 2)]
                    ntok = nh * W
                    for nb in range((ntok + 511) // 512):
                        n0, n1 = nb * 512, min(ntok, (nb + 1) * 512)
                        for kc in range(8):
                            self.mm(pu_, pu_[:, n0:n1], wu_[:, vg, kc, :], hb_[:, kc, n0:n1], kc == 0, kc == 7, [wu_, hb_])
                    self.A(lambda h, pu_=pu_, A_=A_, ntok=ntok, nh=nh, ar0=ar0: h.copy(
                        out=A_[:, ar0:ar0 + nh, 1:W + 1], in_=pu_[:, 0:ntok].rearrange("p (a b) -> p a b", b=W)), [pu_], [A_])
                    dg_ = dgt[idg % 3]
                    idg += 1
                    nt_ = len(taps)
                    w0 = taps[0][2]
                    self.V(lambda h, dg_=dg_, cc=cc, nt_=nt_, w0=w0: h.tensor_tensor(
                        out=dg_[:], in0=self.identf[:].unsqueeze(1).to_broadcast([128, nt_, 128]),
                        in1=fcw[:, cc, w0:w0 + nt_].unsqueeze(2).to_broadcast([128, nt_, 128]), op=ALU.mult),
                        [self.identf, fcw], [dg_])
                    state[(pr, vg)] = (A_, dg_, cc)

                def conv_stage(pr, vg):
                    nonlocal icv
                    A_, dg_, cc = state[(pr, vg)]
                    for q in range(nq):
                        pc_ = pcv[icv % 2]
                        icv += 1
                        for ti, (dy, dx, widx) in enumerate(taps):
                            ra = 1 + q * qrows + dy
                            self.mm(pc_, pc_[:, 0:qn], dg_[:, ti, :], A_[:, ra:ra + qrows, 1 + dx:1 + dx + W],
                                    ti == 0, ti == len(taps) - 1, [dg_, A_])
                        g_ = gl[q % 2]
                        if vg == 1:
                            self.A(lambda h, pc_=pc_, g_=g_, cc=cc: h.activation(out=g_[:], in_=pc_[:, 0:qn], func=AF.Gelu_apprx_tanh,
                                                                              bias=fcb[:, cc:cc + 1]), [pc_, fcb], [g_])
                        else:
                            self.V(lambda h, pc_=pc_, g_=g_, cc=cc, pr=pr, q=q: h.scalar_tensor_tensor(
                                out=actT[:, pr, q * qn:(q + 1) * qn], in0=pc_[:, 0:qn], scalar=fcb[:, cc:cc + 1], in1=g_[:],
                                op0=ALU.add, op1=ALU.mult), [pc_, fcb, g_], [actT])
                for ui in range(len(units) + 1):
                    if ui < len(units):
                        up_stage(*units[ui])
                    if ui >= 1:
                        conv_stage(*units[ui - 1])
                for tb in range(ntb):
                    t = bi * ntb + tb
                    k = t % 2
                    pd = pu[iu % 2]
                    iu += 1
                    self.dma(xmt[k][:], scr["XMID"][t * 128:(t + 1) * 128, :], r=[scr["XMID"]], w=[xmt[k]])
                    for half in range(2):
                        for fc in range(22):
                            self.mm(pd, pd[:, half * 512:(half + 1) * 512], actT[:, fc, tb * 128:(tb + 1) * 128],
                                    WDN[:, fc, half * 512:(half + 1) * 512], fc == 0, fc == 21, [actT, WDN])
                    self.V(lambda h, k=k, pd=pd: h.tensor_tensor(out=v[k][:], in0=pd[:, 0:D], in1=g2t[:], op=ALU.mult), [pd, g2t], [v[k]])
                    self.V(lambda h, k=k: h.scalar_tensor_tensor(out=v[k][:], in0=xmt[k][:], scalar=ALPHA, in1=v[k][:], op0=ALU.mult, op1=ALU.add),
                           [xmt[k], v[k]], [v[k]])
                    self.layernorm(st, v[k], v[k][:], xo[k], xo[k][:], "m", small[k])
                    self.G(lambda h, k=k: h.tensor_tensor(out=xo[k][:], in0=xo[k][:], in1=lng[:], op=ALU.mult), [xo[k], lng], [xo[k]])
                    self.G(lambda h, k=k: h.tensor_tensor(out=xo[k][:], in0=xo[k][:], in1=lnb[:], op=ALU.add), [xo[k], lnb], [xo[k]])
                    self.dma(xout.t[t * 128:(t + 1) * 128, :], xo[k][:], r=[xo[k]], w=[xout])
            self.S.barrier()


_PROG_CACHE = {}


def _get_prog():
    if "nc" not in _PROG_CACHE:
        p = Prog()
        _PROG_CACHE["nc"] = p.build()
    return _PROG_CACHE["nc"]


def kernel(**inputs):
    nc = _get_prog()
    in_maps = [_host_inputs(inputs, b) for b in range(NCORES)]
    res = run_bass_kernel_spmd(nc, in_maps, core_ids=list(range(NCORES)))
    out = np.stack([np.asarray(res.results[b]["y"], dtype=np.float32) for b in range(NCORES)], 0)
    return out
```

```python
import contextlib
import math
import numpy as np
import concourse.bass as bass
import concourse.mybir as mybir
from concourse.bass_utils import run_bass_kernel_spmd

F32 = mybir.dt.float32
BF16 = mybir.dt.bfloat16
AF = mybir.ActivationFunctionType
ALU = mybir.AluOpType

D = 1024
SEQ = 8192
CTXL = 256
DEPTH = 2
DFF = 2816
NIN = 2064
OFF_Z, OFF_XBC, OFF_DT, OFF_FNET, OFF_POOL = 0, 512, 1536, 1552, 1808
ALPHA = (2.0 * DEPTH) ** 0.25
EPS = 1e-6
GRID_W = 64
NCORES = 4
WCOLS = 2320
C_DT, C_POOL, C_FN = 1536, 1552, 1808
NEGBIG = -30000.0
EPOCH = 30000


class Buf:
    __slots__ = ("lw", "rd")

    def __init__(self):
        self.lw = None
        self.rd = []


class Sched:
    ENG = ["tensor", "vector", "scalar", "gpsimd", "sync"]

    def __init__(self, nc, stack):
        self.nc = nc
        self.stack = stack
        self.ops = {e: [] for e in self.ENG}
        self.cnt = {e: 0 for e in self.ENG}
        self.sems = {}
        self.waited = {e: {} for e in self.ENG}
        self.same = {"vector", "scalar", "gpsimd"}
        self.dcnt = {}
        self.dma_rr = {e: 0 for e in self.ENG}
        self.NDMA = 8
        self.last_ev = {}

    def sem(self, key):
        if key not in self.sems:
            self.sems[key] = self.stack.enter_context(self.nc.semaphore("s_%s_%s_%s" % key))
        return self.sems[key]

    def _waits(self, eng, reads, writes):
        deps = {}

        def add(ev):
            if ev is None:
                return
            k, v = ev
            if deps.get(k, 0) < v:
                deps[k] = v
        for b in reads:
            add(b.lw)
        for b in writes:
            add(b.lw)
            for r in b.rd:
                add(r)
        out = []
        for k, v in deps.items():
            if k[0] == eng and k[2] == "c" and eng not in self.same:
                continue
            if self.waited[eng].get(k, 0) >= v:
                continue
            self.waited[eng][k] = v
            out.append((self.sem(k), v))
        return out

    def _commit(self, ev, reads, writes):
        for b in reads:
            b.rd.append(ev)
            if len(b.rd) > 24:
                best = {}
                for k, v in b.rd:
                    if best.get(k, 0) < v:
                        best[k] = v
                b.rd = list(best.items())
        for b in writes:
            b.lw = ev
            b.rd = []
        self.last_ev[ev[0]] = ev[1]

    def op(self, eng, fn, reads=(), writes=()):
        waits = self._waits(eng, reads, writes)
        c = self.cnt[eng]
        self.cnt[eng] = c + 1
        key = (eng, c // EPOCH, "c")
        val = c % EPOCH + 1
        s = self.sem(key)

        def run(h, waits=waits, fn=fn, s=s):
            for (ws, wv) in waits:
                h.wait_ge(ws, wv)
            fn(h).then_inc(s, 1)
        self.ops[eng].append(run)
        self._commit((key, val), reads, writes)

    def dma(self, out, in_, reads=(), writes=(), eng="sync"):
        waits = self._waits(eng, reads, writes)
        i = self.dma_rr[eng]
        self.dma_rr[eng] = (i + 1) % self.NDMA
        key = (eng, i, "d")
        n = self.dcnt.get(key, 0) + 1
        self.dcnt[key] = n
        s = self.sem(key)
        prev = (n - 1) * 16
        if self.waited[eng].get(key, 0) < prev:
            self.waited[eng][key] = prev
        else:
            prev = 0

        def run(h, waits=waits, s=s, prev=prev, out=out, in_=in_):
            for (ws, wv) in waits:
                h.wait_ge(ws, wv)
            if prev > 0:
                h.wait_ge(s, prev)
            h.dma_start(out=out, in_=in_).then_inc(s, 16)
        self.ops[eng].append(run)
        self._commit((key, n * 16), reads, writes)

    def barrier(self):
        evs = dict(self.last_ev)
        for eng in self.ENG:
            waits = []
            for k, v in evs.items():
                if k[0] == eng and k[2] == "c":
                    continue
                if self.waited[eng].get(k, 0) >= v:
                    continue
                self.waited[eng][k] = v
                waits.append((self.sem(k), v))

            def run(h, waits=waits):
                for (ws, wv) in waits:
                    h.wait_ge(ws, wv)
            self.ops[eng].append(run)

    def finish(self, block):
        self.barrier()
        ops = self.ops

        @block.tensor
        def _(h):
            for f in ops["tensor"]:
                f(h)

        @block.vector
        def _(h):
            for f in ops["vector"]:
                f(h)

        @block.scalar
        def _(h):
            for f in ops["scalar"]:
                f(h)

        @block.gpsimd
        def _(h):
            for f in ops["gpsimd"]:
                f(h)

        @block.sync
        def _(h):
            for f in ops["sync"]:
                f(h)


class T:
    __slots__ = ("t", "b")

    def __init__(self, t):
        self.t = t
        self.b = Buf()

    def __getitem__(self, k):
        return self.t[k]


def _consts():
    c = {}
    c["ident"] = np.eye(128, dtype=np.float32)
    k = np.arange(128)
    U = (k[:, None] <= k[None, :]).astype(np.float32)
    Us = (k[:, None] < k[None, :]).astype(np.float32)
    c["tri"] = np.stack([U, -U, Us, np.ones((128, 128), np.float32)], 1).astype(np.float32)
    upat = np.zeros((128, 16, 128), np.float32)
    upat[:, 0:8, :] = U[:, None, :]
    upat[:, 8:16, :] = -Us[:, None, :]
    c["upat"] = upat
    neg = np.zeros((128, 16, 128), np.float32)
    neg[:, 0:8, :] = np.where(k[:, None] > k[None, :], NEGBIG, 0.0)[:, None, :]
    neg[:, 8:16, :] = np.where(k[:, None] < k[None, :], NEGBIG, 0.0)[:, None, :]
    c["negm"] = neg
    m = np.arange(64)
    ang = 2 * np.pi * np.outer(m, m) / 64.0
    nrm = 1.0 / math.sqrt(SEQ * 64.0)
    cc = np.cos(ang) * nrm
    sc = -np.sin(ang) * nrm
    c["chdft"] = np.stack([np.concatenate([cc, cc], 1), np.concatenate([sc, sc], 1)], 1).astype(np.float32)
    a128 = 2 * np.pi * np.outer(k, k) / 128.0
    C, S = np.cos(a128), np.sin(a128)
    c["dftA"] = np.stack([np.concatenate([C, -S], 1), np.concatenate([S, C], 1)], 1).astype(np.float32)
    l2 = np.arange(64)
    th = 2 * np.pi * np.outer(l2, k) / 8192.0
    c["twid"] = np.stack([np.cos(th), np.sin(th), -np.sin(th)], 1).astype(np.float32)
    a64 = 2 * np.pi * np.outer(l2, l2) / 64.0
    c["dftB"] = np.stack([np.cos(a64), np.sin(a64)], 1).astype(np.float32)
    kk = np.arange(256)
    a256 = 2 * np.pi * np.outer(kk, kk) / 256.0
    sc256 = math.sqrt(SEQ / CTXL)
    c256 = (np.cos(a256) * sc256).reshape(2, 128, 256).transpose(1, 0, 2)
    s256 = (np.sin(a256) * sc256).reshape(2, 128, 256).transpose(1, 0, 2)
    c["dftC"] = np.stack([c256, s256], 1).astype(np.float32)
    pt = np.zeros((128, 5, 4, 128), np.float32)
    for gi, win in enumerate((2, 4, 8, 16)):
        left = win // 2
        right = win - 1 - left
        for l in range(128):
            for rel, (off, first, last) in enumerate([(-128, False, False), (0, False, False), (128, False, False),
                                                      (0, True, False), (0, False, True)]):
                lo = l - left
                hi = l + right + 1
                if first:
                    lo = max(lo, 0)
                if last:
                    hi = min(hi, 128)
                cnt = hi - lo
                for s in range(lo, hi):
                    sl = s - off
                    if 0 <= sl < 128:
                        pt[sl, rel, gi, l] += 1.0 / cnt
                if off == 0:
                    pt[l, rel, gi, l] -= 1.0
    c["poolm"] = pt
    return c


_CONST = None


def _host_inputs(inp, b):
    global _CONST
    if _CONST is None:
        _CONST = _consts()
    f = np.float32
    A = np.ascontiguousarray
    d = dict(_CONST)
    d["x"] = A(inp["x"][b], dtype=f)
    d["ctx"] = A(inp["ctx"][b], dtype=f)
    cc = np.stack([np.asarray(inp["c"][b], f), np.asarray(inp["c_ctx"], f)], 1)
    d["cc"] = A(cc.reshape(8, 128, 2).transpose(1, 0, 2))
    d["w_ada"] = A(inp["w_ada"], dtype=f)
    d["b_ada"] = A(np.asarray(inp["b_ada"], f).reshape(DEPTH, 1, 6 * D))
    w_in = np.asarray(inp["w_in"], f)
    main = np.concatenate([w_in[:, :, OFF_Z:OFF_DT], w_in[:, :, OFF_DT:OFF_FNET]], 2)
    d["w_in_r"] = A(main.reshape(DEPTH, 8, 128, 1552).transpose(0, 2, 1, 3))
    wf = w_in[:, :, OFF_FNET:OFF_POOL]
    d["w_fT"] = A(wf.transpose(0, 2, 1).reshape(DEPTH, 2, 128, D).transpose(0, 2, 1, 3))
    wp = w_in[:, :, OFF_POOL:NIN]
    d["w_pT"] = A(wp.transpose(0, 2, 1).reshape(DEPTH, 2, 128, D).transpose(0, 2, 1, 3))
    d["fnet_w"] = A(np.asarray(inp["fnet_w"], f).transpose(0, 2, 1, 3))
    d["pool_w"] = A(inp["pool_w"], dtype=f)
    d["ssd_cw"] = A(np.asarray(inp["ssd_conv_w"], f).reshape(DEPTH, 3, 8, 128).transpose(0, 3, 2, 1))
    d["ssd_cb"] = A(np.asarray(inp["ssd_conv_b"], f).reshape(DEPTH, 8, 128).transpose(0, 2, 1))
    d["dt_bias"] = A(np.asarray(inp["ssd_dt_bias"], f).reshape(DEPTH, 1, 16))
    d["a_log"] = A(np.asarray(inp["ssd_a_log"], f).reshape(DEPTH, 1, 16))
    d["d_rep"] = A(np.repeat(np.asarray(inp["ssd_d"], f), 64, axis=1).reshape(DEPTH, 1, 512))
    rs = np.concatenate([np.asarray(inp["ssd_norm_w"], f), np.ones((DEPTH, 256), f),
                         np.asarray(inp["pool_scale"], f)], 1)
    d["rowscale"] = A(rs.reshape(DEPTH, 8, 128).transpose(0, 2, 1))
    d["w_out"] = A(np.asarray(inp["w_out"], f).reshape(DEPTH, 8, 128, D).transpose(0, 2, 1, 3))
    for n in ("ln1_g", "ln1_b", "ln2_g", "ln2_b"):
        d[n] = A(np.asarray(inp[n], f).reshape(DEPTH, 1, D))
    d["w_up"] = A(np.asarray(inp["ffn_w_up"], f).reshape(DEPTH, 8, 128, 2 * DFF).transpose(0, 2, 1, 3))
    d["w_down"] = A(np.asarray(inp["ffn_w_down"], f).reshape(DEPTH, 22, 128, D).transpose(0, 2, 1, 3))
    d["ffn_cw"] = A(np.asarray(inp["ffn_conv_w"], f).reshape(DEPTH, 9, 44, 128).transpose(0, 3, 2, 1))
    d["ffn_cb"] = A(np.asarray(inp["ffn_conv_b"], f).reshape(DEPTH, 44, 128).transpose(0, 2, 1))
    return d


_IN_SHAPES = {
    "x": [SEQ, D], "ctx": [CTXL, D], "cc": [128, 8, 2], "w_ada": [DEPTH, D, 6 * D], "b_ada": [DEPTH, 1, 6 * D],
    "w_in_r": [DEPTH, 128, 8, 1552], "w_fT": [DEPTH, 128, 2, D], "w_pT": [DEPTH, 128, 2, D],
    "fnet_w": [DEPTH, 64, 4, 64], "pool_w": [DEPTH, 4, 64, 64], "ssd_cw": [DEPTH, 128, 8, 3],
    "ssd_cb": [DEPTH, 128, 8], "dt_bias": [DEPTH, 1, 16], "a_log": [DEPTH, 1, 16], "d_rep": [DEPTH, 1, 512],
    "rowscale": [DEPTH, 128, 8], "w_out": [DEPTH, 128, 8, D], "ln1_g": [DEPTH, 1, D], "ln1_b": [DEPTH, 1, D],
    "ln2_g": [DEPTH, 1, D], "ln2_b": [DEPTH, 1, D], "w_up": [DEPTH, 128, 8, 2 * DFF],
    "w_down": [DEPTH, 128, 22, D], "ffn_cw": [DEPTH, 128, 44, 9], "ffn_cb": [DEPTH, 128, 44],
    "ident": [128, 128], "tri": [128, 4, 128], "upat": [128, 16, 128], "negm": [128, 16, 128],
    "chdft": [64, 2, 128], "dftA": [128, 2, 256], "twid": [64, 3, 128], "dftB": [64, 2, 64],
    "dftC": [128, 2, 2, 256], "poolm": [128, 5, 4, 128],
}


class Seq:
    def __init__(self, name, L, row, toff):
        self.name = name
        self.L = L
        self.nt = L // 128
        self.row = row
        self.toff = toff
        self.scr = {}


class Prog:
    def __init__(self, dbg=(), nlayers=DEPTH, stop=None):
        self.dbg = set(dbg)
        self.nlayers = nlayers
        self.stop = stop
        self.nc = bass.Bass("TRN2", target_bir_lowering=False)
        nc = self.nc
        self.din = {n: nc.dram_tensor(n, s, F32, kind="ExternalInput").ap() for n, s in _IN_SHAPES.items()}
        self.out = nc.dram_tensor("y", [SEQ, D], F32, kind="ExternalOutput").ap()
        self.dbg_outs = {}

    def dram(self, name, shape, dt):
        kind = "ExternalOutput" if name in self.dbg else "Internal"
        t = self.nc.dram_tensor(name, shape, dt, kind=kind)
        if name in self.dbg:
            self.dbg_outs[name] = (shape, dt)
        return T(t.ap())

    def sb(self, st, name, shape, dt=F32):
        self._n += 1
        return T(st.enter_context(self.nc.sbuf_tensor("%s_%d" % (name, self._n), shape, dt)))

    def ps(self, st, name, shape, dt=F32):
        self._n += 1
        return T(st.enter_context(self.nc.psum_tensor("%s_%d" % (name, self._n), shape, dt)))

    def V(self, fn, r=(), w=()):
        self.S.op("vector", fn, [t.b for t in r], [t.b for t in w])

    def G(self, fn, r=(), w=()):
        self.S.op("gpsimd", fn, [t.b for t in r], [t.b for t in w])

    def A(self, fn, r=(), w=()):
        self.S.op("scalar", fn, [t.b for t in r], [t.b for t in w])

    def P(self, fn, r=(), w=()):
        self.S.op("tensor", fn, [t.b for t in r], [t.b for t in w])

    def dma(self, out, in_, r=(), w=()):
        self.S.dma(out, in_, [t.b for t in r], [t.b for t in w])

    def mm(self, out_t, out_ap, lhsT, rhs, start, stop, r):
        self.P(lambda h: h.matmul(out_ap, lhsT=lhsT, rhs=rhs, start=start, stop=stop), r, [out_t])

    def cast_load(self, st, name, src_ap, shape, dt=BF16, eng="V"):
        tmp = self.sb(st, name + "_f", shape, F32)
        dst = self.sb(st, name, shape, dt)
        self.dma(tmp[:], src_ap, w=[tmp])
        (self.V if eng == "V" else self.G)(lambda h: h.tensor_copy(out=dst[:], in_=tmp[:]), [tmp], [dst])
        return dst

    def rstd(self, var_ap, out_t, out_ap, r, scale=1.0):
        self.A(lambda h: h.activation(out=out_ap, in_=var_ap, func=AF.Ln, bias=self.epsT[:], scale=scale),
               list(r) + [self.epsT], [out_t])
        self.A(lambda h: h.activation(out=out_ap, in_=out_ap, func=AF.Exp, scale=-0.5), [out_t], [out_t])

    def layernorm(self, st, src_t, src_ap, dst_t, dst_ap, tag, small):
        st6, mv, rs = small
        for hh in range(2):
            self.V(lambda h, hh=hh: h.bn_stats(out=st6[:, hh, :], in_=src_ap[:, hh * 512:(hh + 1) * 512]), [src_t], [st6])
        self.V(lambda h: h.bn_aggr(out=mv[:], in_=st6[:].rearrange("p a b -> p (a b)")), [st6], [mv])
        self.rstd(mv[:, 1:2], rs, rs[:], [mv])
        self.V(lambda h: h.tensor_scalar(out=dst_ap, in0=src_ap, scalar1=mv[:, 0:1], scalar2=rs[:, 0:1],
                                         op0=ALU.subtract, op1=ALU.mult), [src_t, mv, rs], [dst_t])

    def build(self):
        nc = self.nc
        self._n = 0
        with contextlib.ExitStack() as gst:
            self.S = Sched(nc, gst)
            S = self.S
            din = self.din
            lat = Seq("lat", SEQ, 0, 0)
            ctx = Seq("ctx", CTXL, 1, SEQ // 128)
            for sq in (lat, ctx):
                L = sq.L
                n = sq.name
                sq.scr = {
                    "SZ": self.dram("SZ_" + n, [L, 512], BF16),
                    "XBCT": self.dram("XBCT_" + n, [D, L], BF16),
                    "UF": self.dram("UF_" + n, [L, 512], BF16),
                    "MIX": self.dram("MIX_" + n, [L, 1024], BF16),
                    "YP": self.dram("YP_" + n, [L, 512], BF16),
                    "XMID": self.dram("XMID_" + n, [L, D], F32),
                    "H2T": self.dram("H2T_" + n, [D, L], BF16),
                    "X1": self.dram("X1_" + n, [L, D], F32),
                }
            self.MOD = self.dram("MOD", [DEPTH, 2, 6 * D], F32)
            self.WINd = self.dram("WINd", [128, 8, WCOLS], BF16)
            self.WOUTd = self.dram("WOUTd", [128, 8, D], BF16)
            self.WUPd = self.dram("WUPd", [22, 128, 2, 8, 128], BF16)
            self.WDNd = self.dram("WDNd", [128, 22, D], BF16)
            self.identf = self.sb(gst, "identf", [128, 128], F32)
            self.identb = self.sb(gst, "identb", [128, 128], BF16)
            self.dma(self.identf[:], din["ident"][:, :], w=[self.identf])
            self.V(lambda h: h.tensor_copy(out=self.identb[:], in_=self.identf[:]), [self.identf], [self.identb])
            self.epsT = self.sb(gst, "epsT", [128, 1], F32)
            self.V(lambda h: h.memset(self.epsT[:], EPS), [], [self.epsT])
            self.Sf = self.sb(gst, "Sf", [128, 512], F32)
            self.Sb = self.sb(gst, "Sb", [128, 512], F32)
            self.SfB = self.sb(gst, "SfB", [128, 512], BF16)
            self.SbB = self.sb(gst, "SbB", [128, 512], BF16)
            NTT = SEQ // 128 + CTXL // 128
            self.DT = self.sb(gst, "DT", [128, NTT, 16], F32)
            self.dAb = self.sb(gst, "dAb", [128, NTT, 16], BF16)

            for l in range(self.nlayers):
                last = (l == DEPTH - 1)
                xin_lat = T(din["x"]) if l == 0 else lat.scr["X1"]
                xin_ctx = T(din["ctx"]) if l == 0 else ctx.scr["X1"]
                xout_lat = T(self.out) if last else lat.scr["X1"]
                if l == 0:
                    self._xin0 = (xin_lat, xin_ctx)
                self.phase_mod(l)
                if self.stop == "mod":
                    break
                self.phase_weights(l)
                if self.stop == "weights":
                    break
                self.phase_A(l, ctx, xin_ctx)
                self.phase_A(l, lat, xin_lat)
                if "DTd" in self.dbg and l == 0:
                    dtd = self.dram("DTd", [128, SEQ // 128 + CTXL // 128, 16], F32)
                    self.dma(dtd[:, :, :], self.DT[:], r=[self.DT], w=[dtd])
                if self.stop == "A":
                    break
                self.V(lambda h: h.memset(self.Sf[:], 0.0), [], [self.Sf])
                self.V(lambda h: h.memset(self.Sb[:], 0.0), [], [self.Sb])
                self.V(lambda h: h.memset(self.SfB[:], 0.0), [], [self.SfB])
                self.V(lambda h: h.memset(self.SbB[:], 0.0), [], [self.SbB])
                self.phase_sweep(l, ctx, fwd=True)
                self.phase_sweep(l, ctx, fwd=False, full=not last)
                if self.stop == "Sctx":
                    break
                self.phase_sweep(l, lat, fwd=True)
                if self.stop == "Sfwd":
                    break
                if not last:
                    self.phase_F_ctx(l, ctx)
                self.phase_F_lat(l, lat)
                if self.stop == "F":
                    break
                self.phase_sweep(l, lat, fwd=False, full=True)
                if self.stop == "Sbwd":
                    break
                if not last:
                    self.phase_C(l, ctx, xin_ctx)
                self.phase_C(l, lat, xin_lat)
                if self.stop == "C":
                    break
                if not last:
                    self.phase_M(l, ctx, ctx.scr["X1"])
                self.phase_M(l, lat, xout_lat)
            with nc.Block() as block:
                S.finish(block)
        return nc

    def phase_mod(self, l):
        din = self.din
        with contextlib.ExitStack() as st:
            cct = self.sb(st, "cct", [128, 8, 2])
            scs = self.sb(st, "scs", [128, 8, 2])
            self.dma(cct[:], din["cc"][:, :, :], w=[cct])
            self.A(lambda h: h.activation(out=scs[:], in_=cct[:], func=AF.Silu), [cct], [scs])
            brow = self.sb(st, "brow", [1, 6 * D])
            self.dma(brow[:], din["b_ada"][l, :, :], w=[brow])
            ones2 = self.sb(st, "ones2", [1, 2])
            self.V(lambda h: h.memset(ones2[:], 1.0), [], [ones2])
            modsb = self.sb(st, "modsb", [2, 6 * D])
            wa = [self.sb(st, "wa", [128, 8, 512]) for _ in range(2)]
            pm = [self.ps(st, "pm", [2, 512]) for _ in range(2)]
            wsrc = din["w_ada"][l].rearrange("(kc p) n -> p kc n", p=128)
            for cb in range(12):
                w_ = wa[cb % 2]
                p_ = pm[cb % 2]
                self.dma(w_[:], wsrc[:, :, cb * 512:(cb + 1) * 512], w=[w_])
                for kc in range(8):
                    self.mm(p_, p_[:], scs[:, kc, :], w_[:, kc, :], kc == 0, False, [scs, w_])
                self.mm(p_, p_[:], ones2[:], brow[:, cb * 512:(cb + 1) * 512], False, True, [ones2, brow])
                add = 1.0 if cb in (2, 3, 8, 9) else 0.0
                self.V(lambda h, p_=p_, cb=cb, add=add: h.tensor_scalar(
                    out=modsb[:, cb * 512:(cb + 1) * 512], in0=p_[:], scalar1=add, scalar2=None, op0=ALU.add),
                    [p_], [modsb])
            self.dma(self.MOD[l], modsb[:], r=[modsb], w=[self.MOD])
            self.S.barrier()

    def phase_weights(self, l):
        din = self.din
        with contextlib.ExitStack() as st:
            for half in range(2):
                wf32 = self.sb(st, "wi32", [128, 4, 1552])
                wb16 = self.sb(st, "wi16", [128, 4, 1552], BF16)
                self.dma(wf32[:], din["w_in_r"][l, :, half * 4:(half + 1) * 4, :], w=[wf32])
                for kc in range(4):
                    eng = self.V if kc % 2 == 0 else self.G
                    eng(lambda h, kc=kc, wf32=wf32, wb16=wb16: h.tensor_copy(out=wb16[:, kc, :], in_=wf32[:, kc, :]),
                        [wf32], [wb16])
                self.dma(self.WINd[:, half * 4:(half + 1) * 4, 0:1552], wb16[:], r=[wb16], w=[self.WINd])
            chd = self.sb(st, "chd", [64, 2, 128])
            self.dma(chd[:], din["chdft"][:, :, :], w=[chd])
            fnw = self.sb(st, "fnw", [64, 4, 64])
            self.dma(fnw[:], din["fnet_w"][l, :, :, :], w=[fnw])
            pab = self.ps(st, "pab", [128, 4, 128])
            for g in range(4):
                for ri in range(2):
                    self.mm(pab, pab[:, g, ri * 64:(ri + 1) * 64], chd[:, ri, :], fnw[:, g, :], True, True, [chd, fnw])
            wfT = self.sb(st, "wfT", [128, 2, D])
            wpT = self.sb(st, "wpT", [128, 2, D])
            self.dma(wfT[:], din["w_fT"][l, :, :, :], w=[wfT])
            self.dma(wpT[:], din["w_pT"][l, :, :, :], w=[wpT])
            wfold = self.sb(st, "wfold", [128, 8, 768], BF16)
            pf = [self.ps(st, "pfold", [128, 512]) for _ in range(2)]
            bdf = self.sb(st, "bdf", [128, 2, 256])
            bdp = self.sb(st, "bdp", [128, 2, 128])
            self.V(lambda h: h.memset(bdf[:], 0.0), [], [bdf])
            self.V(lambda h: h.memset(bdp[:], 0.0), [], [bdp])
            for j in range(2):
                for gp in range(2):
                    g = 2 * j + gp
                    self.V(lambda h, j=j, gp=gp, g=g: h.tensor_copy(
                        out=bdf[gp * 64:(gp + 1) * 64, j, gp * 128:(gp + 1) * 128],
                        in_=pab[gp * 64:(gp + 1) * 64, g, :]), [pab], [bdf])
                    self.dma(bdp[gp * 64:(gp + 1) * 64, j, gp * 64:(gp + 1) * 64], din["pool_w"][l, g, :, :], w=[bdp])
            i = 0
            for kc in range(8):
                p_ = pf[i % 2]
                i += 1
                for j in range(2):
                    self.mm(p_, p_[:, j * 256:(j + 1) * 256], wfT[:, j, kc * 128:(kc + 1) * 128], bdf[:, j, :],
                            True, True, [wfT, bdf])
                self.A(lambda h, p_=p_, kc=kc: h.copy(out=wfold[:, kc, 256:768], in_=p_[:]), [p_], [wfold])
                p_ = pf[i % 2]
                i += 1
                for j in range(2):
                    self.mm(p_, p_[:, j * 128:(j + 1) * 128], wpT[:, j, kc * 128:(kc + 1) * 128], bdp[:, j, :],
                            True, True, [wpT, bdp])
                self.V(lambda h, p_=p_, kc=kc: h.tensor_copy(out=wfold[:, kc, 0:256], in_=p_[:, 0:256]), [p_], [wfold])
            self.dma(self.WINd[:, :, 1552:WCOLS], wfold[:], r=[wfold], w=[self.WINd])
            rsc = self.sb(st, "rsc", [128, 8])
            self.dma(rsc[:], din["rowscale"][l, :, :], w=[rsc])
            for half in range(2):
                wo32 = self.sb(st, "wo32", [128, 4, D])
                wo16 = self.sb(st, "wo16", [128, 4, D], BF16)
                self.dma(wo32[:], din["w_out"][l, :, half * 4:(half + 1) * 4, :], w=[wo32])
                for kc in range(4):
                    c = half * 4 + kc
                    self.V(lambda h, kc=kc, c=c, wo32=wo32, wo16=wo16: h.tensor_scalar(
                        out=wo16[:, kc, :], in0=wo32[:, kc, :], scalar1=rsc[:, c:c + 1], scalar2=None, op0=ALU.mult),
                        [wo32, rsc], [wo16])
                self.dma(self.WOUTd[:, half * 4:(half + 1) * 4, :], wo16[:], r=[wo16], w=[self.WOUTd])
            self.S.barrier()
        with contextlib.ExitStack() as st:
            u32 = [self.sb(st, "u32", [128, 8, 256]) for _ in range(3)]
            u16 = [self.sb(st, "u16", [128, 2, 8, 128], BF16) for _ in range(3)]
            i = 0
            for vg in range(2):
                for pb in range(11):
                    a, b_ = u32[i % 3], u16[i % 3]
                    c0 = vg * DFF + pb * 256
                    self.dma(a[:], din["w_up"][l, :, :, c0:c0 + 256], w=[a])
                    eng = (self.V, self.G, self.A)[i % 3]
                    if i % 3 == 2:
                        eng(lambda h, a=a, b_=b_: h.copy(out=b_[:], in_=a[:].rearrange("p k (q c) -> p q k c", q=2)), [a], [b_])
                    else:
                        eng(lambda h, a=a, b_=b_: h.tensor_copy(out=b_[:], in_=a[:].rearrange("p k (q c) -> p q k c", q=2)), [a], [b_])
                    for q in range(2):
                        self.dma(self.WUPd[2 * pb + q, :, vg, :, :], b_[:, q, :, :], r=[b_], w=[self.WUPd])
                    i += 1
            d32 = [self.sb(st, "d32", [128, 2, D]) for _ in range(2)]
            d16 = [self.sb(st, "d16", [128, 2, D], BF16) for _ in range(2)]
            for i in range(11):
                a, b_ = d32[i % 2], d16[i % 2]
                self.dma(a[:], din["w_down"][l, :, 2 * i:2 * i + 2, :], w=[a])
                eng = self.V if i % 2 == 0 else self.G
                eng(lambda h, a=a, b_=b_: h.tensor_copy(out=b_[:], in_=a[:]), [a], [b_])
                self.dma(self.WDNd[:, 2 * i:2 * i + 2, :], b_[:], r=[b_], w=[self.WDNd])
            self.S.barrier()

    def phase_A(self, l, sq, xin):
        din = self.din
        scr = sq.scr
        with contextlib.ExitStack() as st:
            WIN = self.sb(st, "WIN", [128, 8, WCOLS], BF16)
            self.dma(WIN[:], self.WINd[:, :, :], r=[self.WINd], w=[WIN])
            scp = self.sb(st, "scp", [128, D])
            sh = self.sb(st, "sh", [128, D])
            self.dma(scp[:], self.MOD[l, sq.row:sq.row + 1, 1024:2048].partition_broadcast(128), r=[self.MOD], w=[scp])
            self.dma(sh[:], self.MOD[l, sq.row:sq.row + 1, 0:1024].partition_broadcast(128), r=[self.MOD], w=[sh])
            dtb = self.sb(st, "dtb", [128, 16])
            nega = self.sb(st, "nega", [128, 16])
            self.dma(dtb[:], din["dt_bias"][l, :, :].partition_broadcast(128), w=[dtb])
            self.dma(nega[:], din["a_log"][l, :, :].partition_broadcast(128), w=[nega])
            self.A(lambda h: h.activation(out=nega[:], in_=nega[:], func=AF.Exp), [nega], [nega])
            self.V(lambda h: h.tensor_scalar(out=nega[:], in0=nega[:], scalar1=-1.0, scalar2=None, op0=ALU.mult), [nega], [nega])
            xt = [self.sb(st, "xt", [128, D]) for _ in range(3)]
            xn = [self.sb(st, "xn", [128, D]) for _ in range(3)]
            hb = [self.sb(st, "hb", [128, D], BF16) for _ in range(3)]
            small = [(self.sb(st, "st6", [128, 2, 6]), self.sb(st, "mv", [128, 2]), self.sb(st, "rs", [128, 1])) for _ in range(3)]
            stw = min(4, sq.nt)
            hT = [self.sb(st, "hT", [128, 8, stw * 128], BF16) for _ in range(2)]
            ptr = [self.ps(st, "ptr", [128, 8, 128], BF16) for _ in range(2)]
            pz = self.ps(st, "pz", [128, 512])
            pfn = self.ps(st, "pfn", [128, 512])
            ppd = self.ps(st, "ppd", [128, 512])
            pxb = [self.ps(st, "pxb", [128, 512]) for _ in range(2)]
            szt = [self.sb(st, "szt", [128, 512], BF16) for _ in range(2)]
            uft = [self.sb(st, "uft", [128, 512], BF16) for _ in range(2)]
            upt = [self.sb(st, "upt", [128, 256], BF16) for _ in range(2)]
            dtr = [self.sb(st, "dtr", [128, 16]) for _ in range(2)]
            xbst = [self.sb(st, "xbst", [128, 8, stw * 128], BF16) for _ in range(2)]
            xbd = scr["XBCT"].t.rearrange("(c p) n -> p c n", p=128)

            def S1(t):
                k = t % 3
                self.dma(xt[k][:], xin.t[t * 128:(t + 1) * 128, :], r=[xin], w=[xt[k]])
                self.layernorm(st, xt[k], xt[k][:], xn[k], xn[k][:], "a", small[k])
                self.G(lambda h, k=k: h.tensor_tensor(out=xn[k][:], in0=xn[k][:], in1=scp[:], op=ALU.mult), [xn[k], scp], [xn[k]])
                self.G(lambda h, k=k: h.tensor_tensor(out=hb[k][:], in0=xn[k][:], in1=sh[:], op=ALU.add), [xn[k], sh], [hb[k]])

            def S2(t):
                k = t % 3
                p = t % 2
                sti, ti = divmod(t, stw)
                hT_ = hT[sti % 2]
                for kc in range(8):
                    self.P(lambda h, k=k, p=p, kc=kc: h.transpose(out=ptr[p][:, kc, :], in_=hb[k][:, kc * 128:(kc + 1) * 128],
                                                                 identity=self.identb[:]), [hb[k], self.identb], [ptr[p]])
                self.A(lambda h, p=p, ti=ti, hT_=hT_: h.copy(out=hT_[:, :, ti * 128:(ti + 1) * 128], in_=ptr[p][:]), [ptr[p]], [hT_])

            def S3(t):
                k = t % 2
                tt = sq.toff + t
                sti, ti = divmod(t, stw)
                hT_ = hT[sti % 2]
                for kc in range(8):
                    self.mm(ppd, ppd[:, 0:272], hT_[:, kc, ti * 128:(ti + 1) * 128], WIN[:, kc, C_DT:C_FN], kc == 0, kc == 7, [hT_, WIN])
                for kc in range(8):
                    self.mm(pz, pz[:], hT_[:, kc, ti * 128:(ti + 1) * 128], WIN[:, kc, 0:512], kc == 0, kc == 7, [hT_, WIN])
                for kc in range(8):
                    self.mm(pfn, pfn[:], hT_[:, kc, ti * 128:(ti + 1) * 128], WIN[:, kc, C_FN:WCOLS], kc == 0, kc == 7, [hT_, WIN])
                self.V(lambda h, k=k: h.tensor_tensor(out=dtr[k][:], in0=ppd[:, 0:16], in1=dtb[:], op=ALU.add), [ppd, dtb], [dtr[k]])
                self.V(lambda h, k=k: h.tensor_copy(out=upt[k][:], in_=ppd[:, 16:272]), [ppd], [upt[k]])
                self.A(lambda h, k=k: h.activation(out=dtr[k][:], in_=dtr[k][:], func=AF.Exp), [dtr[k]], [dtr[k]])
                self.A(lambda h, k=k, tt=tt: h.activation(out=self.DT[:, tt, :], in_=dtr[k][:], func=AF.Ln, bias=1.0), [dtr[k]], [self.DT])
                self.A(lambda h, k=k: h.activation(out=szt[k][:], in_=pz[:], func=AF.Silu), [pz], [szt[k]])
                self.V(lambda h, k=k: h.tensor_copy(out=uft[k][:], in_=pfn[:]), [pfn], [uft[k]])
                self.V(lambda h, tt=tt: h.tensor_tensor(out=self.dAb[:, tt, :], in0=self.DT[:, tt, :], in1=nega[:], op=ALU.mult),
                       [self.DT, nega], [self.dAb])
                self.dma(scr["MIX"][t * 128:(t + 1) * 128, 768:1024], upt[k][:], r=[upt[k]], w=[scr["MIX"]])
                self.dma(scr["SZ"][t * 128:(t + 1) * 128, :], szt[k][:], r=[szt[k]], w=[scr["SZ"]])
                self.dma(scr["UF"][t * 128:(t + 1) * 128, :], uft[k][:], r=[uft[k]], w=[scr["UF"]])
                if ti == stw - 1:
                    xb_ = xbst[sti % 2]
                    for ch in range(8):
                        p_ = pxb[ch % 2]
                        for kc in range(8):
                            self.mm(p_, p_[:, 0:stw * 128], WIN[:, kc, 512 + ch * 128:512 + (ch + 1) * 128], hT_[:, kc, :],
                                    kc == 0, kc == 7, [hT_, WIN])
                        if ch % 2 == 0:
                            self.A(lambda h, p_=p_, ch=ch, xb_=xb_: h.copy(out=xb_[:, ch, :], in_=p_[:, 0:stw * 128]), [p_], [xb_])
                        else:
                            self.V(lambda h, p_=p_, ch=ch, xb_=xb_: h.tensor_copy(out=xb_[:, ch, :], in_=p_[:, 0:stw * 128]), [p_], [xb_])
                    c0 = sti * stw * 128
                    self.dma(xbd[:, :, c0:c0 + stw * 128], xb_[:], r=[xb_], w=[scr["XBCT"]])

            for step in range(sq.nt + 2):
                if step < sq.nt:
                    S1(step)
                if 0 <= step - 1 < sq.nt:
                    S2(step - 1)
                if 0 <= step - 2 < sq.nt:
                    S3(step - 2)
            self.S.barrier()

    def phase_sweep(self, l, sq, fwd, full=True):
        din = self.din
        scr = sq.scr
        with contextlib.ExitStack() as st:
            cw = self.sb(st, "cw", [128, 8, 3])
            cb = self.sb(st, "cb", [128, 8])
            self.dma(cw[:], din["ssd_cw"][l, :, :, :], w=[cw])
            self.dma(cb[:], din["ssd_cb"][l, :, :], w=[cb])
            DG = self.sb(st, "DG", [128, 8, 3, 128], BF16)
            for ch in range(8):
                for tp in range(3):
                    self.V(lambda h, ch=ch, tp=tp: h.tensor_scalar(out=DG[:, ch, tp, :], in0=self.identf[:],
                                                                   scalar1=cw[:, ch, tp:tp + 1], scalar2=None, op0=ALU.mult),
                           [self.identf, cw], [DG])
            tri = self.cast_load(st, "tri", din["tri"][:, :, :], [128, 4, 128])
            if fwd:
                upat = self.cast_load(st, "upat", din["upat"][:, :, :], [128, 16, 128], eng="G")
                negm = self.cast_load(st, "negm", din["negm"][:, :, :], [128, 16, 128], eng="G")
                dtile = self.sb(st, "dtile", [128, 512])
                self.dma(dtile[:], din["d_rep"][l, :, :].partition_broadcast(128), w=[dtile])
            U, NU, Us, ONES = (tri[:, i, :] for i in range(4))
            Sx = self.Sf if fwd else self.Sb
            SxB = self.SfB if fwd else self.SbB
            xin = [self.sb(st, "xin", [128, 8, 130], BF16) for _ in range(2)]
            xc = [self.sb(st, "xc", [128, 8, 128], BF16) for _ in range(2)]
            XB = [self.sb(st, "XB", [128, 768], BF16) for _ in range(2)]
            pc0 = self.ps(st, "pc0", [128, 4, 128])
            pc1 = self.ps(st, "pc1", [128, 4, 128])
            ptr = self.ps(st, "ptr", [128, 1024], BF16)
            psm = self.ps(st, "psm", [128, 512])
            py = self.ps(st, "py", [128, 512])
            arg = self.sb(st, "arg", [128, 24])
            ex = [self.sb(st, "ex", [128, 24]) for _ in range(2)]
            wv = self.sb(st, "wv", [128, 8])
            Xw = [self.sb(st, "Xw", [128, 512], BF16) for _ in range(2)]
            yo = self.sb(st, "yo", [128, 512])
            if fwd:
                pseg = [self.ps(st, "pseg", [128, 4, 128]) for _ in range(3)]
                rhs1 = self.sb(st, "rhs1", [128, 16, 128], BF16)
                LT = self.sb(st, "LT", [128, 16, 128], BF16)
                MT = [self.sb(st, "MT", [128, 16, 128], BF16) for _ in range(2)]
                Xf = [self.sb(st, "Xf", [128, 512], BF16) for _ in range(2)]
                Xb = [self.sb(st, "Xb", [128, 512], BF16) for _ in range(2)]
                XD = [self.sb(st, "XD", [128, 512], BF16) for _ in range(2)]
                ypo = [self.sb(st, "ypo", [128, 512], BF16) for _ in range(2)]
            else:
                ypi = [self.sb(st, "ypi", [128, 512], BF16) for _ in range(2)]
                szi = [self.sb(st, "szi", [128, 512], BF16) for _ in range(2)]
                yz = self.sb(st, "yz", [128, 512])
                sqj = self.sb(st, "sqj", [128, 512])
                ss = self.sb(st, "ss", [128, 2])
                rg = self.sb(st, "rg", [128, 2])
                yn = [self.sb(st, "yn", [128, 512], BF16) for _ in range(2)]
            xbd = scr["XBCT"].t.rearrange("(c p) n -> p c n", p=128)
            order = list(range(sq.nt)) if fwd else list(range(sq.nt - 1, -1, -1))

            def bc8(ap):
                return ap.unsqueeze(2).to_broadcast([128, 8, 64])

            def Pst(i):
                t = order[i]
                k = i % 2
                tt = sq.toff + t
                xi, xc_, XB_, ex_, Xw_ = xin[k], xc[k], XB[k], ex[k], Xw[k]
                lo_, hi_ = max(t * 128 - 1, 0), min(t * 128 + 129, sq.L)
                o_ = lo_ - (t * 128 - 1)
                if o_ > 0:
                    self.G(lambda h, xi=xi: h.memset(xi[:, :, 0:1], 0.0), [], [xi])
                if hi_ < t * 128 + 129:
                    self.G(lambda h, xi=xi: h.memset(xi[:, :, 129:130], 0.0), [], [xi])
                self.dma(xi[:, :, o_:o_ + hi_ - lo_], xbd[:, :, lo_:hi_], r=[scr["XBCT"]], w=[xi])
                if (not fwd) and full:
                    self.dma(ypi[k][:], scr["YP"][t * 128:(t + 1) * 128, :], r=[scr["YP"]], w=[ypi[k]])
                    self.dma(szi[k][:], scr["SZ"][t * 128:(t + 1) * 128, :], r=[scr["SZ"]], w=[szi[k]])
                for ch in range(8):
                    pc = pc0 if ch < 4 else pc1
                    for tp in range(3):
                        self.mm(pc, pc[:, ch % 4, :], DG[:, ch, tp, :], xi[:, ch, tp:tp + 128], tp == 0, tp == 2, [DG, xi])
                for ch in range(8):
                    pc = pc0 if ch < 4 else pc1
                    self.A(lambda h, pc=pc, ch=ch, xc_=xc_: h.activation(out=xc_[:, ch, :], in_=pc[:, ch % 4, :], func=AF.Silu,
                                                                        bias=cb[:, ch:ch + 1]), [pc, cb], [xc_])
                dA_t = self.dAb[:, tt, :]
                self.mm(psm, psm[:, 0:16], U if fwd else Us, dA_t, True, True, [tri, self.dAb])
                self.mm(psm, psm[:, 16:32], ONES, dA_t, True, True, [tri, self.dAb])
                for ch in range(6):
                    self.P(lambda h, ch=ch, xc_=xc_: h.transpose(out=ptr[:, ch * 128:(ch + 1) * 128], in_=xc_[:, ch, :],
                                                                identity=self.identb[:]), [xc_, self.identb], [ptr])
                o = 0 if fwd else 8
                self.V(lambda h, o=o: h.tensor_copy(out=arg[:, 0:8], in_=psm[:, o:o + 8]), [psm], [arg])
                self.V(lambda h, o=o: h.tensor_copy(out=arg[:, 8:16], in_=psm[:, 16 + o:24 + o]), [psm], [arg])
                self.V(lambda h, o=o: h.tensor_tensor(out=arg[:, 16:24], in0=psm[:, 16 + o:24 + o], in1=arg[:, 0:8],
                                                      op=ALU.subtract), [psm, arg], [arg])
                self.A(lambda h, XB_=XB_: h.copy(out=XB_[:], in_=ptr[:, 0:768]), [ptr], [XB_])
                self.A(lambda h, ex_=ex_: h.activation(out=ex_[:], in_=arg[:], func=AF.Exp), [arg], [ex_])
                dt_d = self.DT[:, tt, o:o + 8]
                if fwd:
                    self.V(lambda h, dt_d=dt_d, ex_=ex_: h.tensor_tensor(out=wv[:], in0=ex_[:, 16:24], in1=dt_d, op=ALU.mult), [ex_, self.DT], [wv])
                else:
                    self.V(lambda h, dt_d=dt_d, ex_=ex_: h.tensor_tensor(out=wv[:], in0=ex_[:, 0:8], in1=dt_d, op=ALU.mult), [ex_, self.DT], [wv])
                X3 = XB_[:, 0:512].rearrange("p (a b) -> p a b", a=8)
                self.V(lambda h, X3=X3, Xw_=Xw_: h.tensor_tensor(out=Xw_[:].rearrange("p (a b) -> p a b", a=8), in0=X3, in1=bc8(wv[:]), op=ALU.mult),
                       [XB_, wv], [Xw_])
                if fwd:
                    MT_, Xf_, Xb_, XD_ = MT[k], Xf[k], Xb[k], XD[k]
                    dtf = self.DT[:, tt, 0:8]
                    dtb_ = self.DT[:, tt, 8:16]
                    self.G(lambda h, tt=tt: h.tensor_tensor(out=rhs1[:], in0=upat[:], in1=self.dAb[:, tt, :].unsqueeze(2).to_broadcast([128, 16, 128]),
                                                            op=ALU.mult), [upat, self.dAb], [rhs1])
                    self.V(lambda h, X3=X3, dtf=dtf, Xf_=Xf_: h.tensor_tensor(out=Xf_[:].rearrange("p (a b) -> p a b", a=8), in0=X3, in1=bc8(dtf), op=ALU.mult),
                           [XB_, self.DT], [Xf_])
                    self.G(lambda h, X3=X3, dtb_=dtb_, Xb_=Xb_: h.tensor_tensor(out=Xb_[:].rearrange("p (a b) -> p a b", a=8), in0=X3, in1=bc8(dtb_), op=ALU.mult),
                           [XB_, self.DT], [Xb_])
                    self.G(lambda h, XB_=XB_, XD_=XD_: h.tensor_tensor(out=XD_[:], in0=XB_[:, 0:512], in1=dtile[:], op=ALU.mult), [XB_, dtile], [XD_])
                    for g in range(2):
                        self.mm(psm, psm[:, 128 + g * 128:256 + g * 128], xc_[:, 4 + g, :], xc_[:, 6 + g, :], True, True, [xc_])
                    for q in range(4):
                        pq = pseg[q % 3]
                        lt2 = NU if q < 2 else Us
                        self.mm(pq, pq[:], ONES, rhs1[:, 4 * q:4 * q + 4, :], True, False, [tri, rhs1])
                        self.mm(pq, pq[:], lt2, self.dAb[:, tt, 4 * q:4 * q + 4].unsqueeze(2).to_broadcast([128, 4, 128]), False, False, [tri, self.dAb])
                        self.mm(pq, pq[:], self.identb[:], negm[:, 4 * q:4 * q + 4, :], False, True, [self.identb, negm])
                        self.A(lambda h, pq=pq, q=q: h.activation(out=LT[:, 4 * q:4 * q + 4, :], in_=pq[:], func=AF.Exp), [pq], [LT])
                        g = q % 2
                        self.V(lambda h, q=q, g=g, MT_=MT_: h.tensor_tensor(
                            out=MT_[:, 4 * q:4 * q + 4, :], in0=LT[:, 4 * q:4 * q + 4, :],
                            in1=psm[:, 128 + g * 128:256 + g * 128].unsqueeze(1).to_broadcast([128, 4, 128]), op=ALU.mult),
                            [LT, psm], [MT_])

            def Qst(i):
                t = order[i]
                k = i % 2
                xc_, XB_, ex_, Xw_ = xc[k], XB[k], ex[k], Xw[k]
                ysc = ex_[:, 0:8] if fwd else ex_[:, 16:24]
                if fwd:
                    MT_, Xf_, Xb_, XD_ = MT[k], Xf[k], Xb[k], XD[k]
                    self.mm(py, py[:], self.identb[:], XD_[:], True, False, [self.identb, XD_])
                    for hd in range(8):
                        self.mm(py, py[:, hd * 64:(hd + 1) * 64], MT_[:, hd, :], Xf_[:, hd * 64:(hd + 1) * 64], False, False, [MT_, Xf_])
                        self.mm(py, py[:, hd * 64:(hd + 1) * 64], MT_[:, 8 + hd, :], Xb_[:, hd * 64:(hd + 1) * 64], False, True, [MT_, Xb_])
                if fwd or full:
                    po = pc0
                    for g in range(2):
                        self.mm(po, po[:].rearrange("p a b -> p (a b)")[:, g * 256:(g + 1) * 256], xc_[:, 6 + g, :],
                                SxB[:, g * 256:(g + 1) * 256], True, True, [xc_, SxB])
                pd = pc1
                for g in range(2):
                    self.mm(pd, pd[:].rearrange("p a b -> p (a b)")[:, g * 256:(g + 1) * 256], XB_[:, 512 + g * 128:640 + g * 128],
                            Xw_[:, g * 256:(g + 1) * 256], True, True, [XB_, Xw_])
                if fwd or full:
                    self.V(lambda h, po=po, ysc=ysc: h.tensor_tensor(out=yo[:].rearrange("p (a b) -> p a b", a=8),
                                                                     in0=po[:].rearrange("p a (c b) -> p (a c) b", b=64),
                                                                     in1=bc8(ysc), op=ALU.mult), [po, ex_], [yo])
                self.V(lambda h, ex_=ex_: h.tensor_tensor(out=Sx[:].rearrange("p (a b) -> p a b", a=8), in0=Sx[:].rearrange("p (a b) -> p a b", a=8),
                                                          in1=bc8(ex_[:, 8:16]), op=ALU.mult), [Sx, ex_], [Sx])
                self.V(lambda h, pd=pd: h.tensor_tensor(out=Sx[:], in0=Sx[:], in1=pd[:].rearrange("p a b -> p (a b)"), op=ALU.add), [Sx, pd], [Sx])
                self.A(lambda h: h.copy(out=SxB[:], in_=Sx[:]), [Sx], [SxB])
                if fwd:
                    self.V(lambda h, k=k: h.tensor_tensor(out=ypo[k][:], in0=py[:], in1=yo[:], op=ALU.add), [py, yo], [ypo[k]])
                    self.dma(scr["YP"][t * 128:(t + 1) * 128, :], ypo[k][:], r=[ypo[k]], w=[scr["YP"]])
                elif full:
                    self.V(lambda h, k=k: h.tensor_tensor(out=yz[:], in0=yo[:], in1=ypi[k][:], op=ALU.add), [yo, ypi[k]], [yz])
                    self.G(lambda h, k=k: h.tensor_tensor(out=yz[:], in0=yz[:], in1=szi[k][:], op=ALU.mult), [yz, szi[k]], [yz])
                    for g in range(2):
                        self.A(lambda h, g=g: h.activation(out=sqj[:, g * 256:(g + 1) * 256], in_=yz[:, g * 256:(g + 1) * 256],
                                                           func=AF.Square, accum_out=ss[:, g:g + 1]), [yz], [sqj, ss])
                    self.rstd(ss[:], rg, rg[:], [ss], scale=1.0 / 256.0)
                    for g in range(2):
                        self.V(lambda h, g=g, k=k: h.tensor_scalar(out=yn[k][:, g * 256:(g + 1) * 256], in0=yz[:, g * 256:(g + 1) * 256],
                                                                  scalar1=rg[:, g:g + 1], scalar2=None, op0=ALU.mult), [yz, rg], [yn[k]])
                    self.dma(scr["MIX"][t * 128:(t + 1) * 128, 0:512], yn[k][:], r=[yn[k]], w=[scr["MIX"]])

            n_ = len(order)
            for step in range(n_ + 1):
                if step < n_:
                    Pst(step)
                if step >= 1:
                    Qst(step - 1)
            self.S.barrier()

    def phase_F_lat(self, l, sq):
        din = self.din
        scr = sq.scr
        with contextlib.ExitStack() as st:
            dA_ = self.cast_load(st, "dftA", din["dftA"][:, :, :], [128, 2, 256])
            dB_ = self.cast_load(st, "dftB", din["dftB"][:, :, :], [64, 2, 64])
            tw = self.sb(st, "tw", [64, 3, 128])
            self.dma(tw[:], din["twid"][:, :, :], w=[tw])
            Gt = [self.sb(st, "Gt", [128, 64, 128], BF16) for _ in range(2)]
            YF = self.sb(st, "YF", [128, 64, 256], BF16)
            pa = [self.ps(st, "pa", [64, 2, 2, 128]) for _ in range(3)]
            pb = [self.ps(st, "pb", [128, 8, 64]) for _ in range(2)]
            At = [self.sb(st, "At", [64, 2, 2, 128]) for _ in range(2)]
            Bt = [self.sb(st, "Bt", [64, 2, 2, 128]) for _ in range(2)]
            Yp = [self.sb(st, "Yp", [64, 2, 2, 128], BF16) for _ in range(3)]
            ufv = scr["UF"].t.rearrange("(a b) c -> a b c", b=64)

            def FA(n):
                g, cp = divmod(n, 32)
                G_ = Gt[g % 2]
                if cp == 0:
                    self.dma(G_[:], ufv[:, :, g * 128:(g + 1) * 128], r=[scr["UF"]], w=[G_])
                pa_ = pa[n % 3]
                for ch in range(2):
                    d_ = 2 * cp + ch
                    self.mm(pa_, pa_[:, ch, :, :].rearrange("p a b -> p (a b)"), G_[:, :, d_], dA_[:, 0, :], True, False, [G_, dA_])
                    self.mm(pa_, pa_[:, ch, :, :].rearrange("p a b -> p (a b)"), G_[:, :, 64 + d_], dA_[:, 1, :], False, True, [G_, dA_])

            def FT(n):
                pa_, At_, Bt_, Yp_ = pa[n % 3], At[n % 2], Bt[n % 2], Yp[n % 3]
                trb = tw[:, 0, :].unsqueeze(1).unsqueeze(1).to_broadcast([64, 2, 2, 128])
                self.V(lambda h, pa_=pa_, At_=At_, trb=trb: h.tensor_tensor(out=At_[:], in0=pa_[:], in1=trb, op=ALU.mult), [pa_, tw], [At_])
                self.V(lambda h, pa_=pa_, Bt_=Bt_: h.tensor_tensor(out=Bt_[:, :, 0, :], in0=pa_[:, :, 1, :],
                                                                  in1=tw[:, 1, :].unsqueeze(1).to_broadcast([64, 2, 128]), op=ALU.mult), [pa_, tw], [Bt_])
                self.V(lambda h, pa_=pa_, Bt_=Bt_: h.tensor_tensor(out=Bt_[:, :, 1, :], in0=pa_[:, :, 0, :],
                                                                  in1=tw[:, 2, :].unsqueeze(1).to_broadcast([64, 2, 128]), op=ALU.mult), [pa_, tw], [Bt_])
                self.G(lambda h, At_=At_, Bt_=Bt_, Yp_=Yp_: h.tensor_tensor(out=Yp_[:], in0=At_[:], in1=Bt_[:], op=ALU.add), [At_, Bt_], [Yp_])

            def FB(n):
                g, cp = divmod(n, 32)
                Yp_ = Yp[n % 3]
                pb_ = pb[(n // 4) % 2]
                for ch in range(2):
                    c8 = (cp % 4) * 2 + ch
                    self.mm(pb_, pb_[:, c8, :], Yp_[:, ch, 0, :], dB_[:, 0, :], True, False, [Yp_, dB_])
                    self.mm(pb_, pb_[:, c8, :], Yp_[:, ch, 1, :], dB_[:, 1, :], False, True, [Yp_, dB_])
                if cp % 4 == 3:
                    c0 = g * 64 + (cp // 4) * 8
                    self.A(lambda h, pb_=pb_, c0=c0: h.copy(out=YF[:, :, c0:c0 + 8].rearrange("p k c -> p c k"), in_=pb_[:]), [pb_], [YF])

            for step in range(128 + 2):
                if step < 128:
                    FA(step)
                if 0 <= step - 1 < 128:
                    FT(step - 1)
                if 0 <= step - 2 < 128:
                    FB(step - 2)
            self.dma(scr["MIX"].t.rearrange("(a b) c -> b a c", b=128)[:, :, 512:768], YF[:], r=[YF], w=[scr["MIX"]])
            self.S.barrier()

    def phase_F_ctx(self, l, sq):
        din = self.din
        scr = sq.scr
        with contextlib.ExitStack() as st:
            dC = self.cast_load(st, "dftC", din["dftC"][:, :, :, :], [128, 2, 2, 256])
            gc = self.sb(st, "gc", [128, 2, 512], BF16)
            self.dma(gc[:], scr["UF"].t.rearrange("(c p) n -> p c n", p=128), r=[scr["UF"]], w=[gc])
            pf = [self.ps(st, "pfc", [128, 256]) for _ in range(2)]
            yf = [self.sb(st, "yfc", [128, 256], BF16) for _ in range(2)]
            for kt in range(2):
                for g in range(4):
                    i = 0
                    for lc in range(2):
                        for ri in range(2):
                            self.mm(pf[kt], pf[kt][:, g * 64:(g + 1) * 64], dC[:, ri, lc, kt * 128:(kt + 1) * 128],
                                    gc[:, lc, g * 128 + ri * 64:g * 128 + ri * 64 + 64], i == 0, i == 3, [dC, gc])
                            i += 1
                self.V(lambda h, kt=kt: h.tensor_copy(out=yf[kt][:], in_=pf[kt][:]), [pf[kt]], [yf[kt]])
                self.dma(scr["MIX"][kt * 128:(kt + 1) * 128, 512:768], yf[kt][:], r=[yf[kt]], w=[scr["MIX"]])
            self.S.barrier()

    def phase_C(self, l, sq, xin):
        din = self.din
        scr = sq.scr
        with contextlib.ExitStack() as st:
            WOUT = self.sb(st, "WOUT", [128, 8, D], BF16)
            self.dma(WOUT[:], self.WOUTd[:, :, :], r=[self.WOUTd], w=[WOUT])
            PT = self.cast_load(st, "poolm", din["poolm"][:, :, :, :], [128, 5, 4, 128])

            def bct(name, src):
                t_ = self.sb(st, name, [128, D])
                self.dma(t_[:], src.partition_broadcast(128), r=[self.MOD], w=[t_])
                return t_
            g1t = bct("g1t", self.MOD[l, sq.row:sq.row + 1, 2048:3072])
            sc2 = bct("sc2", self.MOD[l, sq.row:sq.row + 1, 4096:5120])
            sh2 = bct("sh2", self.MOD[l, sq.row:sq.row + 1, 3072:4096])
            lng = bct("lng", din["ln1_g"][l, :, :])
            lnb = bct("lnb", din["ln1_b"][l, :, :])
            xt = [self.sb(st, "xt", [128, D]) for _ in range(2)]
            mx = [self.sb(st, "mx", [128, 768], BF16) for _ in range(2)]
            upw = [self.sb(st, "upw", [128, 3, 256], BF16) for _ in range(2)]
            mixT = [self.sb(st, "mixT", [128, 8, 128], BF16) for _ in range(2)]
            v = [self.sb(st, "v", [128, D]) for _ in range(2)]
            xm = [self.sb(st, "xm", [128, D]) for _ in range(3)]
            hn = [self.sb(st, "hn", [128, D]) for _ in range(2)]
            h2 = [self.sb(st, "h2", [128, D], BF16) for _ in range(2)]
            h2T = [self.sb(st, "h2T", [128, 8, 128], BF16) for _ in range(2)]
            small = [(self.sb(st, "st6", [128, 2, 6]), self.sb(st, "mv", [128, 2]), self.sb(st, "rs", [128, 1])) for _ in range(4)]
            ptm = self.ps(st, "ptm", [128, 8, 128], BF16)
            ppl = self.ps(st, "ppl", [128, 2, 128])
            pout = [self.ps(st, "pout", [128, D]) for _ in range(2)]
            pth = self.ps(st, "pth", [128, 8, 128], BF16)
            h2v = scr["H2T"].t.rearrange("(c p) n -> p c n", p=128)

            def S1(t):
                k = t % 2
                self.dma(xt[k][:], xin.t[t * 128:(t + 1) * 128, :], r=[xin], w=[xt[k]])
                self.dma(mx[k][:], scr["MIX"][t * 128:(t + 1) * 128, 0:768], r=[scr["MIX"]], w=[mx[k]])
                rels = []
                for r_, tn in enumerate((t - 1, t, t + 1)):
                    if 0 <= tn < sq.nt:
                        self.dma(upw[k][:, r_, :], scr["MIX"][tn * 128:(tn + 1) * 128, 768:1024], r=[scr["MIX"]], w=[upw[k]])
                        rels.append(r_)
                for c in range(6):
                    self.P(lambda h, k=k, c=c: h.transpose(out=ptm[:, c, :], in_=mx[k][:, c * 128:(c + 1) * 128], identity=self.identb[:]),
                           [mx[k], self.identb], [ptm])
                self.A(lambda h, k=k: h.copy(out=mixT[k][:, 0:6, :], in_=ptm[:, 0:6, :]), [ptm], [mixT[k]])
                for g in range(4):
                    for i, r_ in enumerate(rels):
                        ridx = r_
                        if r_ == 1 and t == 0:
                            ridx = 3
                        if r_ == 1 and t == sq.nt - 1:
                            ridx = 4
                        self.mm(ppl, ppl[(g % 2) * 64:(g % 2) * 64 + 64, g // 2, :], upw[k][:, r_, g * 64:(g + 1) * 64], PT[:, ridx, g, :],
                                i == 0, i == len(rels) - 1, [upw[k], PT])
                self.V(lambda h, k=k: h.tensor_copy(out=mixT[k][:, 6:8, :], in_=ppl[:]), [ppl], [mixT[k]])

            def S2a(t):
                k = t % 2
                k3 = t % 3
                po = pout[k]
                for half in range(2):
                    for c in range(8):
                        self.mm(po, po[:, half * 512:(half + 1) * 512], mixT[k][:, c, :], WOUT[:, c, half * 512:(half + 1) * 512],
                                c == 0, c == 7, [mixT[k], WOUT])
                self.V(lambda h, k=k, po=po: h.tensor_tensor(out=v[k][:], in0=po[:], in1=g1t[:], op=ALU.mult), [po, g1t], [v[k]])
                self.V(lambda h, k=k: h.scalar_tensor_tensor(out=v[k][:], in0=xt[k][:], scalar=ALPHA, in1=v[k][:], op0=ALU.mult, op1=ALU.add),
                       [xt[k], v[k]], [v[k]])
                self.layernorm(st, v[k], v[k][:], xm[k3], xm[k3][:], "c1", small[k])
                self.G(lambda h, k3=k3: h.tensor_tensor(out=xm[k3][:], in0=xm[k3][:], in1=lng[:], op=ALU.mult), [xm[k3], lng], [xm[k3]])
                self.G(lambda h, k3=k3: h.tensor_tensor(out=xm[k3][:], in0=xm[k3][:], in1=lnb[:], op=ALU.add), [xm[k3], lnb], [xm[k3]])
                self.dma(scr["XMID"][t * 128:(t + 1) * 128, :], xm[k3][:], r=[xm[k3]], w=[scr["XMID"]])

            def S2b(t):
                k = t % 2
                k3 = t % 3
                self.layernorm(st, xm[k3], xm[k3][:], hn[k], hn[k][:], "c2", small[2 + k])
                self.G(lambda h, k=k: h.tensor_tensor(out=hn[k][:], in0=hn[k][:], in1=sc2[:], op=ALU.mult), [hn[k], sc2], [hn[k]])
                self.G(lambda h, k=k: h.tensor_tensor(out=h2[k][:], in0=hn[k][:], in1=sh2[:], op=ALU.add), [hn[k], sh2], [h2[k]])

            def S3(t):
                k = t % 2
                for kc in range(8):
                    self.P(lambda h, k=k, kc=kc: h.transpose(out=pth[:, kc, :], in_=h2[k][:, kc * 128:(kc + 1) * 128], identity=self.identb[:]),
                           [h2[k], self.identb], [pth])
                self.A(lambda h, k=k: h.copy(out=h2T[k][:], in_=pth[:]), [pth], [h2T[k]])
                self.dma(h2v[:, :, t * 128:(t + 1) * 128], h2T[k][:], r=[h2T[k]], w=[scr["H2T"]])

            for step in range(sq.nt + 3):
                if step < sq.nt:
                    S1(step)
                if 0 <= step - 1 < sq.nt:
                    S2a(step - 1)
                if 0 <= step - 2 < sq.nt:
                    S2b(step - 2)
                if 0 <= step - 3 < sq.nt:
                    S3(step - 3)
            self.S.barrier()

    def phase_M(self, l, sq, xout):
        din = self.din
        scr = sq.scr
        is_ctx = sq.name == "ctx"
        if is_ctx:
            rows, W, RB = 1, CTXL, 1
            taps = [(0, dx, 3 + (dx + 1)) for dx in (-1, 0, 1)]
        else:
            rows, W, RB = SEQ // GRID_W, GRID_W, 16
            taps = [(dy, dx, (dy + 1) * 3 + (dx + 1)) for dy in (-1, 0, 1) for dx in (-1, 0, 1)]
        nq = max(1, (RB * W) // 512)
        qrows = RB // nq if not is_ctx else 1
        qn = qrows * W
        ntb = (RB * W) // 128
        with contextlib.ExitStack() as st:
            WDN = self.sb(st, "WDN", [128, 22, D], BF16)
            self.dma(WDN[:], self.WDNd[:, :, :], r=[self.WDNd], w=[WDN])
            fcw = self.sb(st, "fcw", [128, 44, 9])
            fcb = self.sb(st, "fcb", [128, 44])
            self.dma(fcw[:], din["ffn_cw"][l, :, :, :], w=[fcw])
            self.dma(fcb[:], din["ffn_cb"][l, :, :], w=[fcb])

            def bct(name, src):
                t_ = self.sb(st, name, [128, D])
                self.dma(t_[:], src.partition_broadcast(128), r=[self.MOD], w=[t_])
                return t_
            g2t = bct("g2t", self.MOD[l, sq.row:sq.row + 1, 5120:6144])
            lng = bct("lng2", din["ln2_g"][l, :, :])
            lnb = bct("lnb2", din["ln2_b"][l, :, :])
            nhr = RB + 2
            hb = [self.sb(st, "hbk", [128, 8, nhr * W], BF16)]
            Abuf = [self.sb(st, "Abuf", [128, nhr, W + 2], BF16) for _ in range(4)]
            for a_ in Abuf:
                self.G(lambda h, a_=a_: h.memset(a_[:], 0.0), [], [a_])
            actT = self.sb(st, "actT", [128, 22, RB * W], BF16)
            wu = [self.sb(st, "wu", [128, 2, 8, 128], BF16) for _ in range(3)]
            dgt = [self.sb(st, "dgt", [128, len(taps), 128], BF16) for _ in range(3)]
            gl = [self.sb(st, "gl", [128, qn]) for _ in range(2)]
            npu = (nhr * W + 511) // 512
            pu = [self.ps(st, "pu", [128, npu * 512]) for _ in range(2)]
            pcv = [self.ps(st, "pcv", [128, 512]) for _ in range(2)]
            xmt = [self.sb(st, "xmt", [128, D]) for _ in range(2)]
            v = [self.sb(st, "vf", [128, D]) for _ in range(2)]
            xo = xmt
            small = [(self.sb(st, "st6", [128, 2, 6]), self.sb(st, "mv", [128, 2]), self.sb(st, "rs", [128, 1])) for _ in range(2)]
            h2v = scr["H2T"].t.rearrange("(c p) n -> p c n", p=128)
            nblk = rows // RB
            iu = 0
            icv = 0
            idg = 0
            for bi in range(nblk):
                r0, r1 = bi * RB, (bi + 1) * RB
                hr0, hr1 = max(r0 - 1, 0), min(r1 + 1, rows)
                nh = hr1 - hr0
                ar0 = hr0 - (r0 - 1)
                hb_ = hb[0]
                self.dma(hb_[:, :, 0:nh * W], h2v[:, :, hr0 * W:hr1 * W], r=[scr["H2T"]], w=[hb_])
                if bi == nblk - 1 and nblk > 1:
                    for a_ in Abuf:
                        self.G(lambda h, a_=a_: h.memset(a_[:, nhr - 1, :], 0.0), [], [a_])
                units = []
                for pr in range(22):
                    for vg in (1, 0):
                        units.append((pr, vg))
                state = {}

                def up_stage(pr, vg, hb_=hb_, nh=nh, ar0=ar0):
                    nonlocal iu, idg
                    wu_ = wu[pr % 3]
                    if vg == 1:
                        self.dma(wu_[:], self.WUPd[pr, :, :, :, :], r=[self.WUPd], w=[wu_])
                    cc = vg * 22 + pr
                    pu_ = pu[iu % 2]
                    iu += 1
                    A_ = Abuf[vg * 2 + (pr % 2)]
                    ntok = nh * W
                    for nb in range((ntok + 511) // 512):
                        n0, n1 = nb * 512, min(ntok, (nb + 1) * 512)
                        for kc in range(8):
                            self.mm(pu_, pu_[:, n0:n1], wu_[:, vg, kc, :], hb_[:, kc, n0:n1], kc == 0, kc == 7, [wu_, hb_])
                    self.A(lambda h, pu_=pu_, A_=A_, ntok=ntok, nh=nh, ar0=ar0: h.copy(
                        out=A_[:, ar0:ar0 + nh, 1:W + 1], in_=pu_[:, 0:ntok].rearrange("p (a b) -> p a b", b=W)), [pu_], [A_])
                    dg_ = dgt[idg % 3]
                    idg += 1
                    nt_ = len(taps)
                    w0 = taps[0][2]
                    self.V(lambda h, dg_=dg_, cc=cc, nt_=nt_, w0=w0: h.tensor_tensor(
                        out=dg_[:], in0=self.identf[:].unsqueeze(1).to_broadcast([128, nt_, 128]),
                        in1=fcw[:, cc, w0:w0 + nt_].unsqueeze(2).to_broadcast([128, nt_, 128]), op=ALU.mult),
                        [self.identf, fcw], [dg_])
                    state[(pr, vg)] = (A_, dg_, cc)

                def conv_stage(pr, vg):
                    nonlocal icv
                    A_, dg_, cc = state[(pr, vg)]
                    for q in range(nq):
                        pc_ = pcv[icv % 2]
                        icv += 1
                        for ti, (dy, dx, widx) in enumerate(taps):
                            ra = 1 + q * qrows + dy
                            self.mm(pc_, pc_[:, 0:qn], dg_[:, ti, :], A_[:, ra:ra + qrows, 1 + dx:1 + dx + W],
                                    ti == 0, ti == len(taps) - 1, [dg_, A_])
                        g_ = gl[q % 2]
                        if vg == 1:
                            self.A(lambda h, pc_=pc_, g_=g_, cc=cc: h.activation(out=g_[:], in_=pc_[:, 0:qn], func=AF.Gelu_apprx_tanh,
                                                                              bias=fcb[:, cc:cc + 1]), [pc_, fcb], [g_])
                        else:
                            self.V(lambda h, pc_=pc_, g_=g_, cc=cc, pr=pr, q=q: h.scalar_tensor_tensor(
                                out=actT[:, pr, q * qn:(q + 1) * qn], in0=pc_[:, 0:qn], scalar=fcb[:, cc:cc + 1], in1=g_[:],
                                op0=ALU.add, op1=ALU.mult), [pc_, fcb, g_], [actT])
                for ui in range(len(units) + 1):
                    if ui < len(units):
                        up_stage(*units[ui])
                    if ui >= 1:
                        conv_stage(*units[ui - 1])
                for tb in range(ntb):
                    t = bi * ntb + tb
                    k = t % 2
                    pd = pu[iu % 2]
                    iu += 1
                    self.dma(xmt[k][:], scr["XMID"][t * 128:(t + 1) * 128, :], r=[scr["XMID"]], w=[xmt[k]])
                    for half in range(2):
                        for fc in range(22):
                            self.mm(pd, pd[:, half * 512:(half + 1) * 512], actT[:, fc, tb * 128:(tb + 1) * 128],
                                    WDN[:, fc, half * 512:(half + 1) * 512], fc == 0, fc == 21, [actT, WDN])
                    self.V(lambda h, k=k, pd=pd: h.tensor_tensor(out=v[k][:], in0=pd[:, 0:D], in1=g2t[:], op=ALU.mult), [pd, g2t], [v[k]])
                    self.V(lambda h, k=k: h.scalar_tensor_tensor(out=v[k][:], in0=xmt[k][:], scalar=ALPHA, in1=v[k][:], op0=ALU.mult, op1=ALU.add),
                           [xmt[k], v[k]], [v[k]])
                    self.layernorm(st, v[k], v[k][:], xo[k], xo[k][:], "m", small[k])
                    self.G(lambda h, k=k: h.tensor_tensor(out=xo[k][:], in0=xo[k][:], in1=lng[:], op=ALU.mult), [xo[k], lng], [xo[k]])
                    self.G(lambda h, k=k: h.tensor_tensor(out=xo[k][:], in0=xo[k][:], in1=lnb[:], op=ALU.add), [xo[k], lnb], [xo[k]])
                    self.dma(xout.t[t * 128:(t + 1) * 128, :], xo[k][:], r=[xo[k]], w=[xout])
            self.S.barrier()


_PROG_CACHE = {}


def _get_prog():
    if "nc" not in _PROG_CACHE:
        p = Prog()
        _PROG_CACHE["nc"] = p.build()
    return _PROG_CACHE["nc"]


def kernel(**inputs):
    nc = _get_prog()
    in_maps = [_host_inputs(inputs, b) for b in range(NCORES)]
    res = run_bass_kernel_spmd(nc, in_maps, core_ids=list(range(NCORES)))
    out = np.stack([np.asarray(res.results[b]["y"], dtype=np.float32) for b in range(NCORES)], 0)
    return out
```

```python
import contextlib
import math
import numpy as np
import concourse.bass as bass
import concourse.mybir as mybir
from concourse.bass_utils import run_bass_kernel_spmd

F32 = mybir.dt.float32
BF16 = mybir.dt.bfloat16
AF = mybir.ActivationFunctionType
ALU = mybir.AluOpType

D = 1024
SEQ = 8192
CTXL = 256
DEPTH = 2
DFF = 2816
NIN = 2064
OFF_Z, OFF_XBC, OFF_DT, OFF_FNET, OFF_POOL = 0, 512, 1536, 1552, 1808
ALPHA = (2.0 * DEPTH) ** 0.25
EPS = 1e-6
GRID_W = 64
NCORES = 4
WCOLS = 2320
C_DT, C_POOL, C_FN = 1536, 1552, 1808
NEGBIG = -30000.0
EPOCH = 30000


class Buf:
    __slots__ = ("lw", "rd")

    def __init__(self):
        self.lw = None
        self.rd = []


class Sched:
    ENG = ["tensor", "vector", "scalar", "gpsimd", "sync"]

    def __init__(self, nc, stack):
        self.nc = nc
        self.stack = stack
        self.ops = {e: [] for e in self.ENG}
        self.cnt = {e: 0 for e in self.ENG}
        self.sems = {}
        self.waited = {e: {} for e in self.ENG}
        self.same = {"vector", "scalar", "gpsimd"}
        self.dcnt = {}
        self.dma_rr = {e: 0 for e in self.ENG}
        self.NDMA = 8
        self.last_ev = {}

    def sem(self, key):
        if key not in self.sems:
            self.sems[key] = self.stack.enter_context(self.nc.semaphore("s_%s_%s_%s" % key))
        return self.sems[key]

    def _waits(self, eng, reads, writes):
        deps = {}

        def add(ev):
            if ev is None:
                return
            k, v = ev
            if deps.get(k, 0) < v:
                deps[k] = v
        for b in reads:
            add(b.lw)
        for b in writes:
            add(b.lw)
            for r in b.rd:
                add(r)
        out = []
        for k, v in deps.items():
            if k[0] == eng and k[2] == "c" and eng not in self.same:
                continue
            if self.waited[eng].get(k, 0) >= v:
                continue
            self.waited[eng][k] = v
            out.append((self.sem(k), v))
        return out

    def _commit(self, ev, reads, writes):
        for b in reads:
            b.rd.append(ev)
            if len(b.rd) > 24:
                best = {}
                for k, v in b.rd:
                    if best.get(k, 0) < v:
                        best[k] = v
                b.rd = list(best.items())
        for b in writes:
            b.lw = ev
            b.rd = []
        self.last_ev[ev[0]] = ev[1]

    def op(self, eng, fn, reads=(), writes=()):
        waits = self._waits(eng, reads, writes)
        c = self.cnt[eng]
        self.cnt[eng] = c + 1
        key = (eng, c // EPOCH, "c")
        val = c % EPOCH + 1
        s = self.sem(key)

        def run(h, waits=waits, fn=fn, s=s):
            for (ws, wv) in waits:
                h.wait_ge(ws, wv)
            fn(h).then_inc(s, 1)
        self.ops[eng].append(run)
        self._commit((key, val), reads, writes)

    def dma(self, out, in_, reads=(), writes=(), eng="sync"):
        waits = self._waits(eng, reads, writes)
        i = self.dma_rr[eng]
        self.dma_rr[eng] = (i + 1) % self.NDMA
        key = (eng, i, "d")
        n = self.dcnt.get(key, 0) + 1
        self.dcnt[key] = n
        s = self.sem(key)
        prev = (n - 1) * 16
        if self.waited[eng].get(key, 0) < prev:
            self.waited[eng][key] = prev
        else:
            prev = 0

        def run(h, waits=waits, s=s, prev=prev, out=out, in_=in_):
            for (ws, wv) in waits:
                h.wait_ge(ws, wv)
            if prev > 0:
                h.wait_ge(s, prev)
            h.dma_start(out=out, in_=in_).then_inc(s, 16)
        self.ops[eng].append(run)
        self._commit((key, n * 16), reads, writes)

    def barrier(self):
        evs = dict(self.last_ev)
        for eng in self.ENG:
            waits = []
            for k, v in evs.items():
                if k[0] == eng and k[2] == "c":
                    continue
                if self.waited[eng].get(k, 0) >= v:
                    continue
                self.waited[eng][k] = v
                waits.append((self.sem(k), v))

            def run(h, waits=waits):
                for (ws, wv) in waits:
                    h.wait_ge(ws, wv)
            self.ops[eng].append(run)

    def finish(self, block):
        self.barrier()
        ops = self.ops

        @block.tensor
        def _(h):
            for f in ops["tensor"]:
                f(h)

        @block.vector
        def _(h):
            for f in ops["vector"]:
                f(h)

        @block.scalar
        def _(h):
            for f in ops["scalar"]:
                f(h)

        @block.gpsimd
        def _(h):
            for f in ops["gpsimd"]:
                f(h)

        @block.sync
        def _(h):
            for f in ops["sync"]:
                f(h)


class T:
    __slots__ = ("t", "b")

    def __init__(self, t):
        self.t = t
        self.b = Buf()

    def __getitem__(self, k):
        return self.t[k]


def _consts():
    c = {}
    c["ident"] = np.eye(128, dtype=np.float32)
    k = np.arange(128)
    U = (k[:, None] <= k[None, :]).astype(np.float32)
    Us = (k[:, None] < k[None, :]).astype(np.float32)
    c["tri"] = np.stack([U, -U, Us, np.ones((128, 128), np.float32)], 1).astype(np.float32)
    upat = np.zeros((128, 16, 128), np.float32)
    upat[:, 0:8, :] = U[:, None, :]
    upat[:, 8:16, :] = -Us[:, None, :]
    c["upat"] = upat
    neg = np.zeros((128, 16, 128), np.float32)
    neg[:, 0:8, :] = np.where(k[:, None] > k[None, :], NEGBIG, 0.0)[:, None, :]
    neg[:, 8:16, :] = np.where(k[:, None] < k[None, :], NEGBIG, 0.0)[:, None, :]
    c["negm"] = neg
    m = np.arange(64)
    ang = 2 * np.pi * np.outer(m, m) / 64.0
    nrm = 1.0 / math.sqrt(SEQ * 64.0)
    cc = np.cos(ang) * nrm
    sc = -np.sin(ang) * nrm
    c["chdft"] = np.stack([np.concatenate([cc, cc], 1), np.concatenate([sc, sc], 1)], 1).astype(np.float32)
    a128 = 2 * np.pi * np.outer(k, k) / 128.0
    C, S = np.cos(a128), np.sin(a128)
    c["dftA"] = np.stack([np.concatenate([C, -S], 1), np.concatenate([S, C], 1)], 1).astype(np.float32)
    l2 = np.arange(64)
    th = 2 * np.pi * np.outer(l2, k) / 8192.0
    c["twid"] = np.stack([np.cos(th), np.sin(th), -np.sin(th)], 1).astype(np.float32)
    a64 = 2 * np.pi * np.outer(l2, l2) / 64.0
    c["dftB"] = np.stack([np.cos(a64), np.sin(a64)], 1).astype(np.float32)
    kk = np.arange(256)
    a256 = 2 * np.pi * np.outer(kk, kk) / 256.0
    sc256 = math.sqrt(SEQ / CTXL)
    c256 = (np.cos(a256) * sc256).reshape(2, 128, 256).transpose(1, 0, 2)
    s256 = (np.sin(a256) * sc256).reshape(2, 128, 256).transpose(1, 0, 2)
    c["dftC"] = np.stack([c256, s256], 1).astype(np.float32)
    pt = np.zeros((128, 5, 4, 128), np.float32)
    for gi, win in enumerate((2, 4, 8, 16)):
        left = win // 2
        right = win - 1 - left
        for l in range(128):
            for rel, (off, first, last) in enumerate([(-128, False, False), (0, False, False), (128, False, False),
                                                      (0, True, False), (0, False, True)]):
                lo = l - left
                hi = l + right + 1
                if first:
                    lo = max(lo, 0)
                if last:
                    hi = min(hi, 128)
                cnt = hi - lo
                for s in range(lo, hi):
                    sl = s - off
                    if 0 <= sl < 128:
                        pt[sl, rel, gi, l] += 1.0 / cnt
                if off == 0:
                    pt[l, rel, gi, l] -= 1.0
    c["poolm"] = pt
    return c


_CONST = None


def _host_inputs(inp, b):
    global _CONST
    if _CONST is None:
        _CONST = _consts()
    f = np.float32
    A = np.ascontiguousarray
    d = dict(_CONST)
    d["x"] = A(inp["x"][b], dtype=f)
    d["ctx"] = A(inp["ctx"][b], dtype=f)
    cc = np.stack([np.asarray(inp["c"][b], f), np.asarray(inp["c_ctx"], f)], 1)
    d["cc"] = A(cc.reshape(8, 128, 2).transpose(1, 0, 2))
    d["w_ada"] = A(inp["w_ada"], dtype=f)
    d["b_ada"] = A(np.asarray(inp["b_ada"], f).reshape(DEPTH, 1, 6 * D))
    w_in = np.asarray(inp["w_in"], f)
    main = np.concatenate([w_in[:, :, OFF_Z:OFF_DT], w_in[:, :, OFF_DT:OFF_FNET]], 2)
    d["w_in_r"] = A(main.reshape(DEPTH, 8, 128, 1552).transpose(0, 2, 1, 3))
    wf = w_in[:, :, OFF_FNET:OFF_POOL]
    d["w_fT"] = A(wf.transpose(0, 2, 1).reshape(DEPTH, 2, 128, D).transpose(0, 2, 1, 3))
    wp = w_in[:, :, OFF_POOL:NIN]
    d["w_pT"] = A(wp.transpose(0, 2, 1).reshape(DEPTH, 2, 128, D).transpose(0, 2, 1, 3))
    d["fnet_w"] = A(np.asarray(inp["fnet_w"], f).transpose(0, 2, 1, 3))
    d["pool_w"] = A(inp["pool_w"], dtype=f)
    d["ssd_cw"] = A(np.asarray(inp["ssd_conv_w"], f).reshape(DEPTH, 3, 8, 128).transpose(0, 3, 2, 1))
    d["ssd_cb"] = A(np.asarray(inp["ssd_conv_b"], f).reshape(DEPTH, 8, 128).transpose(0, 2, 1))
    d["dt_bias"] = A(np.asarray(inp["ssd_dt_bias"], f).reshape(DEPTH, 1, 16))
    d["a_log"] = A(np.asarray(inp["ssd_a_log"], f).reshape(DEPTH, 1, 16))
    d["d_rep"] = A(np.repeat(np.asarray(inp["ssd_d"], f), 64, axis=1).reshape(DEPTH, 1, 512))
    rs = np.concatenate([np.asarray(inp["ssd_norm_w"], f), np.ones((DEPTH, 256), f),
                         np.asarray(inp["pool_scale"], f)], 1)
    d["rowscale"] = A(rs.reshape(DEPTH, 8, 128).transpose(0, 2, 1))
    d["w_out"] = A(np.asarray(inp["w_out"], f).reshape(DEPTH, 8, 128, D).transpose(0, 2, 1, 3))
    for n in ("ln1_g", "ln1_b", "ln2_g", "ln2_b"):
        d[n] = A(np.asarray(inp[n], f).reshape(DEPTH, 1, D))
    d["w_up"] = A(np.asarray(inp["ffn_w_up"], f).reshape(DEPTH, 8, 128, 2 * DFF).transpose(0, 2, 1, 3))
    d["w_down"] = A(np.asarray(inp["ffn_w_down"], f).reshape(DEPTH, 22, 128, D).transpose(0, 2, 1, 3))
    d["ffn_cw"] = A(np.asarray(inp["ffn_conv_w"], f).reshape(DEPTH, 9, 44, 128).transpose(0, 3, 2, 1))
    d["ffn_cb"] = A(np.asarray(inp["ffn_conv_b"], f).reshape(DEPTH, 44, 128).transpose(0, 2, 1))
    return d


_IN_SHAPES = {
    "x": [SEQ, D], "ctx": [CTXL, D], "cc": [128, 8, 2], "w_ada": [DEPTH, D, 6 * D], "b_ada": [DEPTH, 1, 6 * D],
    "w_in_r": [DEPTH, 128, 8, 1552], "w_fT": [DEPTH, 128, 2, D], "w_pT": [DEPTH, 128, 2, D],
    "fnet_w": [DEPTH, 64, 4, 64], "pool_w": [DEPTH, 4, 64, 64], "ssd_cw": [DEPTH, 128, 8, 3],
    "ssd_cb": [DEPTH, 128, 8], "dt_bias": [DEPTH, 1, 16], "a_log": [DEPTH, 1, 16], "d_rep": [DEPTH, 1, 512],
    "rowscale": [DEPTH, 128, 8], "w_out": [DEPTH, 128, 8, D], "ln1_g": [DEPTH, 1, D], "ln1_b": [DEPTH, 1, D],
    "ln2_g": [DEPTH, 1, D], "ln2_b": [DEPTH, 1, D], "w_up": [DEPTH, 128, 8, 2 * DFF],
    "w_down": [DEPTH, 128, 22, D], "ffn_cw": [DEPTH, 128, 44, 9], "ffn_cb": [DEPTH, 128, 44],
    "ident": [128, 128], "tri": [128, 4, 128], "upat": [128, 16, 128], "negm": [128, 16, 128],
    "chdft": [64, 2, 128], "dftA": [128, 2, 256], "twid": [64, 3, 128], "dftB": [64, 2, 64],
    "dftC": [128, 2, 2, 256], "poolm": [128, 5, 4, 128],
}


class Seq:
    def __init__(self, name, L, row, toff):
        self.name = name
        self.L = L
        self.nt = L // 128
        self.row = row
        self.toff = toff
        self.scr = {}


class Prog:
    def __init__(self, dbg=(), nlayers=DEPTH, stop=None):
        self.dbg = set(dbg)
        self.nlayers = nlayers
        self.stop = stop
        self.nc = bass.Bass("TRN2", target_bir_lowering=False)
        nc = self.nc
        self.din = {n: nc.dram_tensor(n, s, F32, kind="ExternalInput").ap() for n, s in _IN_SHAPES.items()}
        self.out = nc.dram_tensor("y", [SEQ, D], F32, kind="ExternalOutput").ap()
        self.dbg_outs = {}

    def dram(self, name, shape, dt):
        kind = "ExternalOutput" if name in self.dbg else "Internal"
        t = self.nc.dram_tensor(name, shape, dt, kind=kind)
        if name in self.dbg:
            self.dbg_outs[name] = (shape, dt)
        return T(t.ap())

    def sb(self, st, name, shape, dt=F32):
        self._n += 1
        return T(st.enter_context(self.nc.sbuf_tensor("%s_%d" % (name, self._n), shape, dt)))

    def ps(self, st, name, shape, dt=F32):
        self._n += 1
        return T(st.enter_context(self.nc.psum_tensor("%s_%d" % (name, self._n), shape, dt)))

    def V(self, fn, r=(), w=()):
        self.S.op("vector", fn, [t.b for t in r], [t.b for t in w])

    def G(self, fn, r=(), w=()):
        self.S.op("gpsimd", fn, [t.b for t in r], [t.b for t in w])

    def A(self, fn, r=(), w=()):
        self.S.op("scalar", fn, [t.b for t in r], [t.b for t in w])

    def P(self, fn, r=(), w=()):
        self.S.op("tensor", fn, [t.b for t in r], [t.b for t in w])

    def dma(self, out, in_, r=(), w=()):
        self.S.dma(out, in_, [t.b for t in r], [t.b for t in w])

    def mm(self, out_t, out_ap, lhsT, rhs, start, stop, r):
        self.P(lambda h: h.matmul(out_ap, lhsT=lhsT, rhs=rhs, start=start, stop=stop), r, [out_t])

    def cast_load(self, st, name, src_ap, shape, dt=BF16, eng="V"):
        tmp = self.sb(st, name + "_f", shape, F32)
        dst = self.sb(st, name, shape, dt)
        self.dma(tmp[:], src_ap, w=[tmp])
        (self.V if eng == "V" else self.G)(lambda h: h.tensor_copy(out=dst[:], in_=tmp[:]), [tmp], [dst])
        return dst

    def rstd(self, var_ap, out_t, out_ap, r, scale=1.0):
        self.A(lambda h: h.activation(out=out_ap, in_=var_ap, func=AF.Ln, bias=self.epsT[:], scale=scale),
               list(r) + [self.epsT], [out_t])
        self.A(lambda h: h.activation(out=out_ap, in_=out_ap, func=AF.Exp, scale=-0.5), [out_t], [out_t])

    def layernorm(self, st, src_t, src_ap, dst_t, dst_ap, tag, small):
        st6, mv, rs = small
        for hh in range(2):
            self.V(lambda h, hh=hh: h.bn_stats(out=st6[:, hh, :], in_=src_ap[:, hh * 512:(hh + 1) * 512]), [src_t], [st6])
        self.V(lambda h: h.bn_aggr(out=mv[:], in_=st6[:].rearrange("p a b -> p (a b)")), [st6], [mv])
        self.rstd(mv[:, 1:2], rs, rs[:], [mv])
        self.V(lambda h: h.tensor_scalar(out=dst_ap, in0=src_ap, scalar1=mv[:, 0:1], scalar2=rs[:, 0:1],
                                         op0=ALU.subtract, op1=ALU.mult), [src_t, mv, rs], [dst_t])

    def layernorm_affine(self, src_t, src_ap, tmp_t, tmp_ap, dst_t, dst_ap, gain_t, bias_t, small):
        st6, mv, rs = small
        for hh in range(2):
            self.V(lambda h, hh=hh: h.bn_stats(out=st6[:, hh, :], in_=src_ap[:, hh * 512:(hh + 1) * 512]), [src_t], [st6])
        self.V(lambda h: h.bn_aggr(out=mv[:], in_=st6[:].rearrange("p a b -> p (a b)")), [st6], [mv])
        self.rstd(mv[:, 1:2], rs, rs[:], [mv])
        self.V(lambda h: h.scalar_tensor_tensor(out=tmp_ap, in0=src_ap, scalar=mv[:, 0:1], in1=gain_t[:], op0=ALU.subtract, op1=ALU.mult),
               [src_t, mv, gain_t], [tmp_t])
        self.V(lambda h: h.scalar_tensor_tensor(out=dst_ap, in0=tmp_ap, scalar=rs[:, 0:1], in1=bias_t[:], op0=ALU.mult, op1=ALU.add),
               [tmp_t, rs, bias_t], [dst_t])

    def build(self):
        nc = self.nc
        self._n = 0
        with contextlib.ExitStack() as gst:
            self.S = Sched(nc, gst)
            S = self.S
            din = self.din
            lat = Seq("lat", SEQ, 0, 0)
            ctx = Seq("ctx", CTXL, 1, SEQ // 128)
            for sq in (lat, ctx):
                L = sq.L
                n = sq.name
                sq.scr = {
                    "SZ": self.dram("SZ_" + n, [L, 512], BF16),
                    "XBCT": self.dram("XBCT_" + n, [D, L], BF16),
                    "UF": self.dram("UF_" + n, [L, 512], BF16),
                    "MIX": self.dram("MIX_" + n, [L, 1024], BF16),
                    "YP": self.dram("YP_" + n, [L, 512], BF16),
                    "XMID": self.dram("XMID_" + n, [L, D], F32),
                    "H2T": self.dram("H2T_" + n, [D, L], BF16),
                    "X1": self.dram("X1_" + n, [L, D], F32),
                }
            self.MOD = self.dram("MOD", [DEPTH, 2, 6 * D], F32)
            self.WINd = self.dram("WINd", [128, 8, WCOLS], BF16)
            self.WOUTd = self.dram("WOUTd", [128, 8, D], BF16)
            self.WUPd = self.dram("WUPd", [22, 128, 2, 8, 128], BF16)
            self.WDNd = self.dram("WDNd", [128, 22, D], BF16)
            self.identf = self.sb(gst, "identf", [128, 128], F32)
            self.identb = self.sb(gst, "identb", [128, 128], BF16)
            self.dma(self.identf[:], din["ident"][:, :], w=[self.identf])
            self.V(lambda h: h.tensor_copy(out=self.identb[:], in_=self.identf[:]), [self.identf], [self.identb])
            self.epsT = self.sb(gst, "epsT", [128, 1], F32)
            self.V(lambda h: h.memset(self.epsT[:], EPS), [], [self.epsT])
            self.Sf = self.sb(gst, "Sf", [128, 512], F32)
            self.Sb = self.sb(gst, "Sb", [128, 512], F32)
            self.SfB = self.sb(gst, "SfB", [128, 512], BF16)
            self.SbB = self.sb(gst, "SbB", [128, 512], BF16)
            NTT = SEQ // 128 + CTXL // 128
            self.DT = self.sb(gst, "DT", [128, NTT, 16], F32)
            self.dAb = self.sb(gst, "dAb", [128, NTT, 16], BF16)

            for l in range(self.nlayers):
                last = (l == DEPTH - 1)
                xin_lat = T(din["x"]) if l == 0 else lat.scr["X1"]
                xin_ctx = T(din["ctx"]) if l == 0 else ctx.scr["X1"]
                xout_lat = T(self.out) if last else lat.scr["X1"]
                if l == 0:
                    self._xin0 = (xin_lat, xin_ctx)
                self.phase_mod(l)
                if self.stop == "mod":
                    break
                self.phase_weights(l)
                if self.stop == "weights":
                    break
                self.phase_A(l, ctx, xin_ctx)
                self.phase_A(l, lat, xin_lat)
                if "DTd" in self.dbg and l == 0:
                    dtd = self.dram("DTd", [128, SEQ // 128 + CTXL // 128, 16], F32)
                    self.dma(dtd[:, :, :], self.DT[:], r=[self.DT], w=[dtd])
                if self.stop == "A":
                    break
                self.V(lambda h: h.memset(self.Sf[:], 0.0), [], [self.Sf])
                self.V(lambda h: h.memset(self.Sb[:], 0.0), [], [self.Sb])
                self.V(lambda h: h.memset(self.SfB[:], 0.0), [], [self.SfB])
                self.V(lambda h: h.memset(self.SbB[:], 0.0), [], [self.SbB])
                self.phase_sweep(l, ctx, fwd=True)
                self.phase_sweep(l, ctx, fwd=False, full=not last)
                if self.stop == "Sctx":
                    break
                self.phase_sweep(l, lat, fwd=True)
                if self.stop == "Sfwd":
                    break
                if not last:
                    self.phase_F_ctx(l, ctx)
                self.phase_F_lat(l, lat)
                if self.stop == "F":
                    break
                self.phase_sweep(l, lat, fwd=False, full=True)
                if self.stop == "Sbwd":
                    break
                if not last:
                    self.phase_C(l, ctx, xin_ctx)
                self.phase_C(l, lat, xin_lat)
                if self.stop == "C":
                    break
                if not last:
                    self.phase_M(l, ctx, ctx.scr["X1"])
                self.phase_M(l, lat, xout_lat)
            with nc.Block() as block:
                S.finish(block)
        return nc

    def phase_mod(self, l):
        din = self.din
        with contextlib.ExitStack() as st:
            cct = self.sb(st, "cct", [128, 8, 2])
            scs = self.sb(st, "scs", [128, 8, 2])
            self.dma(cct[:], din["cc"][:, :, :], w=[cct])
            self.A(lambda h: h.activation(out=scs[:], in_=cct[:], func=AF.Silu), [cct], [scs])
            brow = self.sb(st, "brow", [1, 6 * D])
            self.dma(brow[:], din["b_ada"][l, :, :], w=[brow])
            ones2 = self.sb(st, "ones2", [1, 2])
            self.V(lambda h: h.memset(ones2[:], 1.0), [], [ones2])
            modsb = self.sb(st, "modsb", [2, 6 * D])
            wa = [self.sb(st, "wa", [128, 8, 512]) for _ in range(2)]
            pm = [self.ps(st, "pm", [2, 512]) for _ in range(2)]
            wsrc = din["w_ada"][l].rearrange("(kc p) n -> p kc n", p=128)
            for cb in range(12):
                w_ = wa[cb % 2]
                p_ = pm[cb % 2]
                self.dma(w_[:], wsrc[:, :, cb * 512:(cb + 1) * 512], w=[w_])
                for kc in range(8):
                    self.mm(p_, p_[:], scs[:, kc, :], w_[:, kc, :], kc == 0, False, [scs, w_])
                self.mm(p_, p_[:], ones2[:], brow[:, cb * 512:(cb + 1) * 512], False, True, [ones2, brow])
                add = 1.0 if cb in (2, 3, 8, 9) else 0.0
                self.V(lambda h, p_=p_, cb=cb, add=add: h.tensor_scalar(
                    out=modsb[:, cb * 512:(cb + 1) * 512], in0=p_[:], scalar1=add, scalar2=None, op0=ALU.add),
                    [p_], [modsb])
            self.dma(self.MOD[l], modsb[:], r=[modsb], w=[self.MOD])
            self.S.barrier()

    def phase_weights(self, l):
        din = self.din
        with contextlib.ExitStack() as st:
            for half in range(2):
                wf32 = self.sb(st, "wi32", [128, 4, 1552])
                wb16 = self.sb(st, "wi16", [128, 4, 1552], BF16)
                self.dma(wf32[:], din["w_in_r"][l, :, half * 4:(half + 1) * 4, :], w=[wf32])
                for kc in range(4):
                    eng = self.V if kc % 2 == 0 else self.G
                    eng(lambda h, kc=kc, wf32=wf32, wb16=wb16: h.tensor_copy(out=wb16[:, kc, :], in_=wf32[:, kc, :]),
                        [wf32], [wb16])
                self.dma(self.WINd[:, half * 4:(half + 1) * 4, 0:1552], wb16[:], r=[wb16], w=[self.WINd])
            chd = self.sb(st, "chd", [64, 2, 128])
            self.dma(chd[:], din["chdft"][:, :, :], w=[chd])
            fnw = self.sb(st, "fnw", [64, 4, 64])
            self.dma(fnw[:], din["fnet_w"][l, :, :, :], w=[fnw])
            pab = self.ps(st, "pab", [128, 4, 128])
            for g in range(4):
                for ri in range(2):
                    self.mm(pab, pab[:, g, ri * 64:(ri + 1) * 64], chd[:, ri, :], fnw[:, g, :], True, True, [chd, fnw])
            wfT = self.sb(st, "wfT", [128, 2, D])
            wpT = self.sb(st, "wpT", [128, 2, D])
            self.dma(wfT[:], din["w_fT"][l, :, :, :], w=[wfT])
            self.dma(wpT[:], din["w_pT"][l, :, :, :], w=[wpT])
            wfold = self.sb(st, "wfold", [128, 8, 768], BF16)
            pf = [self.ps(st, "pfold", [128, 512]) for _ in range(2)]
            bdf = self.sb(st, "bdf", [128, 2, 256])
            bdp = self.sb(st, "bdp", [128, 2, 128])
            self.V(lambda h: h.memset(bdf[:], 0.0), [], [bdf])
            self.V(lambda h: h.memset(bdp[:], 0.0), [], [bdp])
            for j in range(2):
                for gp in range(2):
                    g = 2 * j + gp
                    self.V(lambda h, j=j, gp=gp, g=g: h.tensor_copy(
                        out=bdf[gp * 64:(gp + 1) * 64, j, gp * 128:(gp + 1) * 128],
                        in_=pab[gp * 64:(gp + 1) * 64, g, :]), [pab], [bdf])
                    self.dma(bdp[gp * 64:(gp + 1) * 64, j, gp * 64:(gp + 1) * 64], din["pool_w"][l, g, :, :], w=[bdp])
            i = 0
            for kc in range(8):
                p_ = pf[i % 2]
                i += 1
                for j in range(2):
                    self.mm(p_, p_[:, j * 256:(j + 1) * 256], wfT[:, j, kc * 128:(kc + 1) * 128], bdf[:, j, :],
                            True, True, [wfT, bdf])
                self.A(lambda h, p_=p_, kc=kc: h.copy(out=wfold[:, kc, 256:768], in_=p_[:]), [p_], [wfold])
                p_ = pf[i % 2]
                i += 1
                for j in range(2):
                    self.mm(p_, p_[:, j * 128:(j + 1) * 128], wpT[:, j, kc * 128:(kc + 1) * 128], bdp[:, j, :],
                            True, True, [wpT, bdp])
                self.V(lambda h, p_=p_, kc=kc: h.tensor_copy(out=wfold[:, kc, 0:256], in_=p_[:, 0:256]), [p_], [wfold])
            self.dma(self.WINd[:, :, 1552:WCOLS], wfold[:], r=[wfold], w=[self.WINd])
            rsc = self.sb(st, "rsc", [128, 8])
            self.dma(rsc[:], din["rowscale"][l, :, :], w=[rsc])
            for half in range(2):
                wo32 = self.sb(st, "wo32", [128, 4, D])
                wo16 = self.sb(st, "wo16", [128, 4, D], BF16)
                self.dma(wo32[:], din["w_out"][l, :, half * 4:(half + 1) * 4, :], w=[wo32])
                for kc in range(4):
                    c = half * 4 + kc
                    self.V(lambda h, kc=kc, c=c, wo32=wo32, wo16=wo16: h.tensor_scalar(
                        out=wo16[:, kc, :], in0=wo32[:, kc, :], scalar1=rsc[:, c:c + 1], scalar2=None, op0=ALU.mult),
                        [wo32, rsc], [wo16])
                self.dma(self.WOUTd[:, half * 4:(half + 1) * 4, :], wo16[:], r=[wo16], w=[self.WOUTd])
            self.S.barrier()
        with contextlib.ExitStack() as st:
            u32 = [self.sb(st, "u32", [128, 8, 256]) for _ in range(3)]
            u16 = [self.sb(st, "u16", [128, 2, 8, 128], BF16) for _ in range(3)]
            i = 0
            for vg in range(2):
                for pb in range(11):
                    a, b_ = u32[i % 3], u16[i % 3]
                    c0 = vg * DFF + pb * 256
                    self.dma(a[:], din["w_up"][l, :, :, c0:c0 + 256], w=[a])
                    eng = (self.V, self.G, self.A)[i % 3]
                    if i % 3 == 2:
                        eng(lambda h, a=a, b_=b_: h.copy(out=b_[:], in_=a[:].rearrange("p k (q c) -> p q k c", q=2)), [a], [b_])
                    else:
                        eng(lambda h, a=a, b_=b_: h.tensor_copy(out=b_[:], in_=a[:].rearrange("p k (q c) -> p q k c", q=2)), [a], [b_])
                    for q in range(2):
                        self.dma(self.WUPd[2 * pb + q, :, vg, :, :], b_[:, q, :, :], r=[b_], w=[self.WUPd])
                    i += 1
            d32 = [self.sb(st, "d32", [128, 2, D]) for _ in range(2)]
            d16 = [self.sb(st, "d16", [128, 2, D], BF16) for _ in range(2)]
            for i in range(11):
                a, b_ = d32[i % 2], d16[i % 2]
                self.dma(a[:], din["w_down"][l, :, 2 * i:2 * i + 2, :], w=[a])
                eng = self.V if i % 2 == 0 else self.G
                eng(lambda h, a=a, b_=b_: h.tensor_copy(out=b_[:], in_=a[:]), [a], [b_])
                self.dma(self.WDNd[:, 2 * i:2 * i + 2, :], b_[:], r=[b_], w=[self.WDNd])
            self.S.barrier()

    def phase_A(self, l, sq, xin):
        din = self.din
        scr = sq.scr
        with contextlib.ExitStack() as st:
            WIN = self.sb(st, "WIN", [128, 8, WCOLS], BF16)
            self.dma(WIN[:], self.WINd[:, :, :], r=[self.WINd], w=[WIN])
            scp = self.sb(st, "scp", [128, D])
            sh = self.sb(st, "sh", [128, D])
            self.dma(scp[:], self.MOD[l, sq.row:sq.row + 1, 1024:2048].partition_broadcast(128), r=[self.MOD], w=[scp])
            self.dma(sh[:], self.MOD[l, sq.row:sq.row + 1, 0:1024].partition_broadcast(128), r=[self.MOD], w=[sh])
            dtb = self.sb(st, "dtb", [128, 16])
            nega = self.sb(st, "nega", [128, 16])
            self.dma(dtb[:], din["dt_bias"][l, :, :].partition_broadcast(128), w=[dtb])
            self.dma(nega[:], din["a_log"][l, :, :].partition_broadcast(128), w=[nega])
            self.A(lambda h: h.activation(out=nega[:], in_=nega[:], func=AF.Exp), [nega], [nega])
            self.V(lambda h: h.tensor_scalar(out=nega[:], in0=nega[:], scalar1=-1.0, scalar2=None, op0=ALU.mult), [nega], [nega])
            xt = [self.sb(st, "xt", [128, D]) for _ in range(4)]
            xn = [self.sb(st, "xn", [128, D]) for _ in range(2)]
            hb = [self.sb(st, "hb", [128, D], BF16) for _ in range(3)]
            small = [(self.sb(st, "st6", [128, 2, 6]), self.sb(st, "mv", [128, 2]), self.sb(st, "rs", [128, 1])) for _ in range(3)]
            stw = min(4, sq.nt)
            hT = [self.sb(st, "hT", [128, 8, stw * 128], BF16) for _ in range(2)]
            ptr = [self.ps(st, "ptr", [128, 8, 128], BF16) for _ in range(2)]
            pz = self.ps(st, "pz", [128, 512])
            pfn = self.ps(st, "pfn", [128, 512])
            ppd = self.ps(st, "ppd", [128, 512])
            pxb = [self.ps(st, "pxb", [128, 512]) for _ in range(2)]
            szt = [self.sb(st, "szt", [128, 512], BF16) for _ in range(2)]
            uft = [self.sb(st, "uft", [128, 512], BF16) for _ in range(2)]
            upt = [self.sb(st, "upt", [128, 256], BF16) for _ in range(2)]
            dtr = [self.sb(st, "dtr", [128, 16]) for _ in range(2)]
            xbst = [self.sb(st, "xbst", [128, 8, stw * 128], BF16) for _ in range(2)]
            xbd = scr["XBCT"].t.rearrange("(c p) n -> p c n", p=128)

            def L0(t):
                self.dma(xt[t % 4][:], xin.t[t * 128:(t + 1) * 128, :], r=[xin], w=[xt[t % 4]])

            def S1(t):
                k = t % 3
                x_ = xt[t % 4]
                n_ = xn[t % 2]
                self.layernorm_affine(x_, x_[:], n_, n_[:], hb[k], hb[k][:], scp, sh, small[k])

            def S2(t):
                k = t % 3
                p = t % 2
                sti, ti = divmod(t, stw)
                hT_ = hT[sti % 2]
                for kc in range(8):
                    self.P(lambda h, k=k, p=p, kc=kc: h.transpose(out=ptr[p][:, kc, :], in_=hb[k][:, kc * 128:(kc + 1) * 128],
                                                                 identity=self.identb[:]), [hb[k], self.identb], [ptr[p]])
                self.A(lambda h, p=p, ti=ti, hT_=hT_: h.copy(out=hT_[:, :, ti * 128:(ti + 1) * 128], in_=ptr[p][:]), [ptr[p]], [hT_])

            def S3(t):
                k = t % 2
                tt = sq.toff + t
                sti, ti = divmod(t, stw)
                hT_ = hT[sti % 2]
                for kc in range(8):
                    self.mm(ppd, ppd[:, 0:272], hT_[:, kc, ti * 128:(ti + 1) * 128], WIN[:, kc, C_DT:C_FN], kc == 0, kc == 7, [hT_, WIN])
                for kc in range(8):
                    self.mm(pz, pz[:], hT_[:, kc, ti * 128:(ti + 1) * 128], WIN[:, kc, 0:512], kc == 0, kc == 7, [hT_, WIN])
                for kc in range(8):
                    self.mm(pfn, pfn[:], hT_[:, kc, ti * 128:(ti + 1) * 128], WIN[:, kc, C_FN:WCOLS], kc == 0, kc == 7, [hT_, WIN])
                self.V(lambda h, k=k: h.tensor_tensor(out=dtr[k][:], in0=ppd[:, 0:16], in1=dtb[:], op=ALU.add), [ppd, dtb], [dtr[k]])
                self.V(lambda h, k=k: h.tensor_copy(out=upt[k][:], in_=ppd[:, 16:272]), [ppd], [upt[k]])
                self.A(lambda h, k=k: h.activation(out=dtr[k][:], in_=dtr[k][:], func=AF.Exp), [dtr[k]], [dtr[k]])
                self.A(lambda h, k=k, tt=tt: h.activation(out=self.DT[:, tt, :], in_=dtr[k][:], func=AF.Ln, bias=1.0), [dtr[k]], [self.DT])
                self.A(lambda h, k=k: h.activation(out=szt[k][:], in_=pz[:], func=AF.Silu), [pz], [szt[k]])
                self.V(lambda h, k=k: h.tensor_copy(out=uft[k][:], in_=pfn[:]), [pfn], [uft[k]])
                self.V(lambda h, tt=tt: h.tensor_tensor(out=self.dAb[:, tt, :], in0=self.DT[:, tt, :], in1=nega[:], op=ALU.mult),
                       [self.DT, nega], [self.dAb])
                self.dma(scr["MIX"][t * 128:(t + 1) * 128, 768:1024], upt[k][:], r=[upt[k]], w=[scr["MIX"]])
                self.dma(scr["SZ"][t * 128:(t + 1) * 128, :], szt[k][:], r=[szt[k]], w=[scr["SZ"]])
                self.dma(scr["UF"][t * 128:(t + 1) * 128, :], uft[k][:], r=[uft[k]], w=[scr["UF"]])
                if ti == stw - 1:
                    xb_ = xbst[sti % 2]
                    for ch in range(8):
                        p_ = pxb[ch % 2]
                        for kc in range(8):
                            self.mm(p_, p_[:, 0:stw * 128], WIN[:, kc, 512 + ch * 128:512 + (ch + 1) * 128], hT_[:, kc, :],
                                    kc == 0, kc == 7, [hT_, WIN])
                        if ch % 2 == 0:
                            self.A(lambda h, p_=p_, ch=ch, xb_=xb_: h.copy(out=xb_[:, ch, :], in_=p_[:, 0:stw * 128]), [p_], [xb_])
                        else:
                            self.V(lambda h, p_=p_, ch=ch, xb_=xb_: h.tensor_copy(out=xb_[:, ch, :], in_=p_[:, 0:stw * 128]), [p_], [xb_])
                    c0 = sti * stw * 128
                    self.dma(xbd[:, :, c0:c0 + stw * 128], xb_[:], r=[xb_], w=[scr["XBCT"]])

            for step in range(-2, sq.nt + 2):
                if 0 <= step + 2 < sq.nt:
                    L0(step + 2)
                if 0 <= step < sq.nt:
                    S1(step)
                if 0 <= step - 1 < sq.nt:
                    S2(step - 1)
                if 0 <= step - 2 < sq.nt:
                    S3(step - 2)
            self.S.barrier()

    def phase_sweep(self, l, sq, fwd, full=True):
        din = self.din
        scr = sq.scr
        with contextlib.ExitStack() as st:
            cw = self.sb(st, "cw", [128, 8, 3])
            cb = self.sb(st, "cb", [128, 8])
            self.dma(cw[:], din["ssd_cw"][l, :, :, :], w=[cw])
            self.dma(cb[:], din["ssd_cb"][l, :, :], w=[cb])
            DG = self.sb(st, "DG", [128, 8, 3, 128], BF16)
            for ch in range(8):
                for tp in range(3):
                    self.V(lambda h, ch=ch, tp=tp: h.tensor_scalar(out=DG[:, ch, tp, :], in0=self.identf[:],
                                                                   scalar1=cw[:, ch, tp:tp + 1], scalar2=None, op0=ALU.mult),
                           [self.identf, cw], [DG])
            tri = self.cast_load(st, "tri", din["tri"][:, :, :], [128, 4, 128])
            if fwd:
                upat = self.cast_load(st, "upat", din["upat"][:, :, :], [128, 16, 128], eng="G")
                negm = self.cast_load(st, "negm", din["negm"][:, :, :], [128, 16, 128], eng="G")
                dtile = self.sb(st, "dtile", [128, 512])
                self.dma(dtile[:], din["d_rep"][l, :, :].partition_broadcast(128), w=[dtile])
            U, NU, Us, ONES = (tri[:, i, :] for i in range(4))
            Sx = self.Sf if fwd else self.Sb
            SxB = self.SfB if fwd else self.SbB
            xin = [self.sb(st, "xin", [128, 8, 130], BF16) for _ in range(4)]
            xc = [self.sb(st, "xc", [128, 8, 128], BF16) for _ in range(2)]
            XB = [self.sb(st, "XB", [128, 768], BF16) for _ in range(2)]
            pc0 = self.ps(st, "pc0", [128, 4, 128])
            pc1 = self.ps(st, "pc1", [128, 4, 128])
            ptr = self.ps(st, "ptr", [128, 1024], BF16)
            psm = self.ps(st, "psm", [128, 512])
            py = self.ps(st, "py", [128, 512])
            arg = self.sb(st, "arg", [128, 24])
            ex = [self.sb(st, "ex", [128, 24]) for _ in range(2)]
            wv = self.sb(st, "wv", [128, 8])
            Xw = [self.sb(st, "Xw", [128, 512], BF16) for _ in range(2)]
            yo = self.sb(st, "yo", [128, 512])
            if fwd:
                pseg = [self.ps(st, "pseg", [128, 4, 128]) for _ in range(3)]
                rhs1 = self.sb(st, "rhs1", [128, 16, 128], BF16)
                LT = self.sb(st, "LT", [128, 16, 128], BF16)
                MT = [self.sb(st, "MT", [128, 16, 128], BF16) for _ in range(2)]
                Xf = [self.sb(st, "Xf", [128, 512], BF16) for _ in range(2)]
                Xb = [self.sb(st, "Xb", [128, 512], BF16) for _ in range(2)]
                XD = [self.sb(st, "XD", [128, 512], BF16) for _ in range(2)]
                ypo = [self.sb(st, "ypo", [128, 512], BF16) for _ in range(2)]
            else:
                ypi = [self.sb(st, "ypi", [128, 512], BF16) for _ in range(4)]
                szi = [self.sb(st, "szi", [128, 512], BF16) for _ in range(4)]
                yz = self.sb(st, "yz", [128, 512])
                sqj = self.sb(st, "sqj", [128, 512])
                ss = self.sb(st, "ss", [128, 2])
                rg = self.sb(st, "rg", [128, 2])
                yn = [self.sb(st, "yn", [128, 512], BF16) for _ in range(2)]
            xbd = scr["XBCT"].t.rearrange("(c p) n -> p c n", p=128)
            order = list(range(sq.nt)) if fwd else list(range(sq.nt - 1, -1, -1))

            def bc8(ap):
                return ap.unsqueeze(2).to_broadcast([128, 8, 64])

            def Lst(i):
                t = order[i]
                xi = xin[i % 4]
                lo_, hi_ = max(t * 128 - 1, 0), min(t * 128 + 129, sq.L)
                o_ = lo_ - (t * 128 - 1)
                if o_ > 0:
                    self.G(lambda h, xi=xi: h.memset(xi[:, :, 0:1], 0.0), [], [xi])
                if hi_ < t * 128 + 129:
                    self.G(lambda h, xi=xi: h.memset(xi[:, :, 129:130], 0.0), [], [xi])
                self.dma(xi[:, :, o_:o_ + hi_ - lo_], xbd[:, :, lo_:hi_], r=[scr["XBCT"]], w=[xi])
                if (not fwd) and full:
                    self.dma(ypi[i % 4][:], scr["YP"][t * 128:(t + 1) * 128, :], r=[scr["YP"]], w=[ypi[i % 4]])
                    self.dma(szi[i % 4][:], scr["SZ"][t * 128:(t + 1) * 128, :], r=[scr["SZ"]], w=[szi[i % 4]])

            def Pst(i):
                t = order[i]
                k = i % 2
                tt = sq.toff + t
                xi, xc_, XB_, ex_, Xw_ = xin[i % 4], xc[k], XB[k], ex[k], Xw[k]
                for ch in range(8):
                    pc = pc0 if ch < 4 else pc1
                    for tp in range(3):
                        self.mm(pc, pc[:, ch % 4, :], DG[:, ch, tp, :], xi[:, ch, tp:tp + 128], tp == 0, tp == 2, [DG, xi])
                for ch in range(8):
                    pc = pc0 if ch < 4 else pc1
                    self.A(lambda h, pc=pc, ch=ch, xc_=xc_: h.activation(out=xc_[:, ch, :], in_=pc[:, ch % 4, :], func=AF.Silu,
                                                                        bias=cb[:, ch:ch + 1]), [pc, cb], [xc_])
                dA_t = self.dAb[:, tt, :]
                self.mm(psm, psm[:, 0:16], U if fwd else Us, dA_t, True, True, [tri, self.dAb])
                self.mm(psm, psm[:, 16:32], ONES, dA_t, True, True, [tri, self.dAb])
                for ch in range(6):
                    self.P(lambda h, ch=ch, xc_=xc_: h.transpose(out=ptr[:, ch * 128:(ch + 1) * 128], in_=xc_[:, ch, :],
                                                                identity=self.identb[:]), [xc_, self.identb], [ptr])
                o = 0 if fwd else 8
                self.V(lambda h, o=o: h.tensor_copy(out=arg[:, 0:8], in_=psm[:, o:o + 8]), [psm], [arg])
                self.V(lambda h, o=o: h.tensor_copy(out=arg[:, 8:16], in_=psm[:, 16 + o:24 + o]), [psm], [arg])
                self.V(lambda h, o=o: h.tensor_tensor(out=arg[:, 16:24], in0=psm[:, 16 + o:24 + o], in1=arg[:, 0:8],
                                                      op=ALU.subtract), [psm, arg], [arg])
                self.A(lambda h, XB_=XB_: h.copy(out=XB_[:], in_=ptr[:, 0:768]), [ptr], [XB_])
                self.A(lambda h, ex_=ex_: h.activation(out=ex_[:], in_=arg[:], func=AF.Exp), [arg], [ex_])
                dt_d = self.DT[:, tt, o:o + 8]
                if fwd:
                    self.V(lambda h, dt_d=dt_d, ex_=ex_: h.tensor_tensor(out=wv[:], in0=ex_[:, 16:24], in1=dt_d, op=ALU.mult), [ex_, self.DT], [wv])
                else:
                    self.V(lambda h, dt_d=dt_d, ex_=ex_: h.tensor_tensor(out=wv[:], in0=ex_[:, 0:8], in1=dt_d, op=ALU.mult), [ex_, self.DT], [wv])
                X3 = XB_[:, 0:512].rearrange("p (a b) -> p a b", a=8)
                self.V(lambda h, X3=X3, Xw_=Xw_: h.tensor_tensor(out=Xw_[:].rearrange("p (a b) -> p a b", a=8), in0=X3, in1=bc8(wv[:]), op=ALU.mult),
                       [XB_, wv], [Xw_])
                if fwd:
                    MT_, Xf_, Xb_, XD_ = MT[k], Xf[k], Xb[k], XD[k]
                    dtf = self.DT[:, tt, 0:8]
                    dtb_ = self.DT[:, tt, 8:16]
                    self.G(lambda h, tt=tt: h.tensor_tensor(out=rhs1[:], in0=upat[:], in1=self.dAb[:, tt, :].unsqueeze(2).to_broadcast([128, 16, 128]),
                                                            op=ALU.mult), [upat, self.dAb], [rhs1])
                    self.V(lambda h, X3=X3, dtf=dtf, Xf_=Xf_: h.tensor_tensor(out=Xf_[:].rearrange("p (a b) -> p a b", a=8), in0=X3, in1=bc8(dtf), op=ALU.mult),
                           [XB_, self.DT], [Xf_])
                    self.G(lambda h, X3=X3, dtb_=dtb_, Xb_=Xb_: h.tensor_tensor(out=Xb_[:].rearrange("p (a b) -> p a b", a=8), in0=X3, in1=bc8(dtb_), op=ALU.mult),
                           [XB_, self.DT], [Xb_])
                    self.G(lambda h, XB_=XB_, XD_=XD_: h.tensor_tensor(out=XD_[:], in0=XB_[:, 0:512], in1=dtile[:], op=ALU.mult), [XB_, dtile], [XD_])
                    for g in range(2):
                        self.mm(psm, psm[:, 128 + g * 128:256 + g * 128], xc_[:, 4 + g, :], xc_[:, 6 + g, :], True, True, [xc_])
                    for q in range(4):
                        pq = pseg[q % 3]
                        lt2 = NU if q < 2 else Us
                        self.mm(pq, pq[:], ONES, rhs1[:, 4 * q:4 * q + 4, :], True, False, [tri, rhs1])
                        self.mm(pq, pq[:], lt2, self.dAb[:, tt, 4 * q:4 * q + 4].unsqueeze(2).to_broadcast([128, 4, 128]), False, False, [tri, self.dAb])
                        self.mm(pq, pq[:], self.identb[:], negm[:, 4 * q:4 * q + 4, :], False, True, [self.identb, negm])
                        self.A(lambda h, pq=pq, q=q: h.activation(out=LT[:, 4 * q:4 * q + 4, :], in_=pq[:], func=AF.Exp), [pq], [LT])
                        g = q % 2
                        self.V(lambda h, q=q, g=g, MT_=MT_: h.tensor_tensor(
                            out=MT_[:, 4 * q:4 * q + 4, :], in0=LT[:, 4 * q:4 * q + 4, :],
                            in1=psm[:, 128 + g * 128:256 + g * 128].unsqueeze(1).to_broadcast([128, 4, 128]), op=ALU.mult),
                            [LT, psm], [MT_])

            def Qst(i):
                t = order[i]
                k = i % 2
                xc_, XB_, ex_, Xw_ = xc[k], XB[k], ex[k], Xw[k]
                ysc = ex_[:, 0:8] if fwd else ex_[:, 16:24]
                if fwd:
                    MT_, Xf_, Xb_, XD_ = MT[k], Xf[k], Xb[k], XD[k]
                    self.mm(py, py[:], self.identb[:], XD_[:], True, False, [self.identb, XD_])
                    for hd in range(8):
                        self.mm(py, py[:, hd * 64:(hd + 1) * 64], MT_[:, hd, :], Xf_[:, hd * 64:(hd + 1) * 64], False, False, [MT_, Xf_])
                        self.mm(py, py[:, hd * 64:(hd + 1) * 64], MT_[:, 8 + hd, :], Xb_[:, hd * 64:(hd + 1) * 64], False, True, [MT_, Xb_])
                if fwd or full:
                    po = pc0
                    for g in range(2):
                        self.mm(po, po[:].rearrange("p a b -> p (a b)")[:, g * 256:(g + 1) * 256], xc_[:, 6 + g, :],
                                SxB[:, g * 256:(g + 1) * 256], True, True, [xc_, SxB])
                pd = pc1
                for g in range(2):
                    self.mm(pd, pd[:].rearrange("p a b -> p (a b)")[:, g * 256:(g + 1) * 256], XB_[:, 512 + g * 128:640 + g * 128],
                            Xw_[:, g * 256:(g + 1) * 256], True, True, [XB_, Xw_])
                if fwd or full:
                    self.V(lambda h, po=po, ysc=ysc: h.tensor_tensor(out=yo[:].rearrange("p (a b) -> p a b", a=8),
                                                                     in0=po[:].rearrange("p a (c b) -> p (a c) b", b=64),
                                                                     in1=bc8(ysc), op=ALU.mult), [po, ex_], [yo])
                self.V(lambda h, ex_=ex_: h.tensor_tensor(out=Sx[:].rearrange("p (a b) -> p a b", a=8), in0=Sx[:].rearrange("p (a b) -> p a b", a=8),
                                                          in1=bc8(ex_[:, 8:16]), op=ALU.mult), [Sx, ex_], [Sx])
                self.V(lambda h, pd=pd: h.tensor_tensor(out=Sx[:], in0=Sx[:], in1=pd[:].rearrange("p a b -> p (a b)"), op=ALU.add), [Sx, pd], [Sx])
                self.A(lambda h: h.copy(out=SxB[:], in_=Sx[:]), [Sx], [SxB])
                if fwd:
                    self.V(lambda h, k=k: h.tensor_tensor(out=ypo[k][:], in0=py[:], in1=yo[:], op=ALU.add), [py, yo], [ypo[k]])
                    self.dma(scr["YP"][t * 128:(t + 1) * 128, :], ypo[k][:], r=[ypo[k]], w=[scr["YP"]])
                elif full:
                    k4 = i % 4
                    self.V(lambda h, k4=k4: h.tensor_tensor(out=yz[:], in0=yo[:], in1=ypi[k4][:], op=ALU.add), [yo, ypi[k4]], [yz])
                    self.V(lambda h, k4=k4: h.tensor_tensor(out=yz[:], in0=yz[:], in1=szi[k4][:], op=ALU.mult), [yz, szi[k4]], [yz])
                    for g in range(2):
                        self.A(lambda h, g=g: h.activation(out=sqj[:, g * 256:(g + 1) * 256], in_=yz[:, g * 256:(g + 1) * 256],
                                                           func=AF.Square, accum_out=ss[:, g:g + 1]), [yz], [sqj, ss])
                    self.rstd(ss[:], rg, rg[:], [ss], scale=1.0 / 256.0)
                    for g in range(2):
                        self.V(lambda h, g=g, k=k: h.tensor_scalar(out=yn[k][:, g * 256:(g + 1) * 256], in0=yz[:, g * 256:(g + 1) * 256],
                                                                  scalar1=rg[:, g:g + 1], scalar2=None, op0=ALU.mult), [yz, rg], [yn[k]])
                    self.dma(scr["MIX"][t * 128:(t + 1) * 128, 0:512], yn[k][:], r=[yn[k]], w=[scr["MIX"]])

            n_ = len(order)
            for step in range(-2, n_ + 1):
                if 0 <= step + 2 < n_:
                    Lst(step + 2)
                if 0 <= step < n_:
                    Pst(step)
                if 1 <= step:
                    Qst(step - 1)
            self.S.barrier()

    def phase_F_lat(self, l, sq):
        din = self.din
        scr = sq.scr
        with contextlib.ExitStack() as st:
            dA_ = self.cast_load(st, "dftA", din["dftA"][:, :, :], [128, 2, 256])
            dB_ = self.cast_load(st, "dftB", din["dftB"][:, :, :], [64, 2, 64])
            tw = self.sb(st, "tw", [64, 3, 128])
            self.dma(tw[:], din["twid"][:, :, :], w=[tw])
            Gt = [self.sb(st, "Gt", [128, 64, 128], BF16) for _ in range(2)]
            YF = self.sb(st, "YF", [128, 64, 256], BF16)
            pa = [self.ps(st, "pa", [64, 2, 2, 128]) for _ in range(3)]
            pb = [self.ps(st, "pb", [128, 8, 64]) for _ in range(2)]
            At = [self.sb(st, "At", [64, 2, 2, 128]) for _ in range(2)]
            Bt = [self.sb(st, "Bt", [64, 2, 2, 128]) for _ in range(2)]
            Yp = [self.sb(st, "Yp", [64, 2, 2, 128], BF16) for _ in range(3)]
            ufv = scr["UF"].t.rearrange("(a b) c -> a b c", b=64)

            def FA(n):
                g, cp = divmod(n, 32)
                G_ = Gt[g % 2]
                if cp == 0:
                    self.dma(G_[:], ufv[:, :, g * 128:(g + 1) * 128], r=[scr["UF"]], w=[G_])
                pa_ = pa[n % 3]
                for ch in range(2):
                    d_ = 2 * cp + ch
                    self.mm(pa_, pa_[:, ch, :, :].rearrange("p a b -> p (a b)"), G_[:, :, d_], dA_[:, 0, :], True, False, [G_, dA_])
                    self.mm(pa_, pa_[:, ch, :, :].rearrange("p a b -> p (a b)"), G_[:, :, 64 + d_], dA_[:, 1, :], False, True, [G_, dA_])

            def FT(n):
                pa_, At_, Bt_, Yp_ = pa[n % 3], At[n % 2], Bt[n % 2], Yp[n % 3]
                trb = tw[:, 0, :].unsqueeze(1).unsqueeze(1).to_broadcast([64, 2, 2, 128])
                self.V(lambda h, pa_=pa_, At_=At_, trb=trb: h.tensor_tensor(out=At_[:], in0=pa_[:], in1=trb, op=ALU.mult), [pa_, tw], [At_])
                self.V(lambda h, pa_=pa_, Bt_=Bt_: h.tensor_tensor(out=Bt_[:, :, 0, :], in0=pa_[:, :, 1, :],
                                                                  in1=tw[:, 1, :].unsqueeze(1).to_broadcast([64, 2, 128]), op=ALU.mult), [pa_, tw], [Bt_])
                self.V(lambda h, pa_=pa_, Bt_=Bt_: h.tensor_tensor(out=Bt_[:, :, 1, :], in0=pa_[:, :, 0, :],
                                                                  in1=tw[:, 2, :].unsqueeze(1).to_broadcast([64, 2, 128]), op=ALU.mult), [pa_, tw], [Bt_])
                self.G(lambda h, At_=At_, Bt_=Bt_, Yp_=Yp_: h.tensor_tensor(out=Yp_[:], in0=At_[:], in1=Bt_[:], op=ALU.add), [At_, Bt_], [Yp_])

            def FB(n):
                g, cp = divmod(n, 32)
                Yp_ = Yp[n % 3]
                pb_ = pb[(n // 4) % 2]
                for ch in range(2):
                    c8 = (cp % 4) * 2 + ch
                    self.mm(pb_, pb_[:, c8, :], Yp_[:, ch, 0, :], dB_[:, 0, :], True, False, [Yp_, dB_])
                    self.mm(pb_, pb_[:, c8, :], Yp_[:, ch, 1, :], dB_[:, 1, :], False, True, [Yp_, dB_])
                if cp % 4 == 3:
                    c0 = g * 64 + (cp // 4) * 8
                    self.A(lambda h, pb_=pb_, c0=c0: h.copy(out=YF[:, :, c0:c0 + 8].rearrange("p k c -> p c k"), in_=pb_[:]), [pb_], [YF])

            for step in range(128 + 2):
                if step < 128:
                    FA(step)
                if 0 <= step - 1 < 128:
                    FT(step - 1)
                if 0 <= step - 2 < 128:
                    FB(step - 2)
            self.dma(scr["MIX"].t.rearrange("(a b) c -> b a c", b=128)[:, :, 512:768], YF[:], r=[YF], w=[scr["MIX"]])
            self.S.barrier()

    def phase_F_ctx(self, l, sq):
        din = self.din
        scr = sq.scr
        with contextlib.ExitStack() as st:
            dC = self.cast_load(st, "dftC", din["dftC"][:, :, :, :], [128, 2, 2, 256])
            gc = self.sb(st, "gc", [128, 2, 512], BF16)
            self.dma(gc[:], scr["UF"].t.rearrange("(c p) n -> p c n", p=128), r=[scr["UF"]], w=[gc])
            pf = [self.ps(st, "pfc", [128, 256]) for _ in range(2)]
            yf = [self.sb(st, "yfc", [128, 256], BF16) for _ in range(2)]
            for kt in range(2):
                for g in range(4):
                    i = 0
                    for lc in range(2):
                        for ri in range(2):
                            self.mm(pf[kt], pf[kt][:, g * 64:(g + 1) * 64], dC[:, ri, lc, kt * 128:(kt + 1) * 128],
                                    gc[:, lc, g * 128 + ri * 64:g * 128 + ri * 64 + 64], i == 0, i == 3, [dC, gc])
                            i += 1
                self.V(lambda h, kt=kt: h.tensor_copy(out=yf[kt][:], in_=pf[kt][:]), [pf[kt]], [yf[kt]])
                self.dma(scr["MIX"][kt * 128:(kt + 1) * 128, 512:768], yf[kt][:], r=[yf[kt]], w=[scr["MIX"]])
            self.S.barrier()

    def phase_C(self, l, sq, xin):
        din = self.din
        scr = sq.scr
        with contextlib.ExitStack() as st:
            WOUT = self.sb(st, "WOUT", [128, 8, D], BF16)
            self.dma(WOUT[:], self.WOUTd[:, :, :], r=[self.WOUTd], w=[WOUT])
            PT = self.cast_load(st, "poolm", din["poolm"][:, :, :, :], [128, 5, 4, 128])

            def bct(name, src):
                t_ = self.sb(st, name, [128, D])
                self.dma(t_[:], src.partition_broadcast(128), r=[self.MOD], w=[t_])
                return t_
            g1t = bct("g1t", self.MOD[l, sq.row:sq.row + 1, 2048:3072])
            sc2 = bct("sc2", self.MOD[l, sq.row:sq.row + 1, 4096:5120])
            sh2 = bct("sh2", self.MOD[l, sq.row:sq.row + 1, 3072:4096])
            lng = bct("lng", din["ln1_g"][l, :, :])
            lnb = bct("lnb", din["ln1_b"][l, :, :])
            xt = [self.sb(st, "xt", [128, D]) for _ in range(4)]
            mx = [self.sb(st, "mx", [128, 768], BF16) for _ in range(4)]
            upw = [self.sb(st, "upw", [128, 3, 256], BF16) for _ in range(4)]
            nmr = [self.sb(st, "nmr", [128, 1]) for _ in range(2)]
            mixT = [self.sb(st, "mixT", [128, 8, 128], BF16) for _ in range(2)]
            v = [self.sb(st, "v", [128, D]) for _ in range(2)]
            xm = [self.sb(st, "xm", [128, D]) for _ in range(3)]
            hn = [self.sb(st, "hn", [128, D]) for _ in range(2)]
            h2 = [self.sb(st, "h2", [128, D], BF16) for _ in range(2)]
            h2T = [self.sb(st, "h2T", [128, 8, 128], BF16) for _ in range(2)]
            small = [(self.sb(st, "st6", [128, 2, 6]), self.sb(st, "mv", [128, 2]), self.sb(st, "rs", [128, 1])) for _ in range(4)]
            ptm = self.ps(st, "ptm", [128, 8, 128], BF16)
            ppl = self.ps(st, "ppl", [128, 2, 128])
            pout = [self.ps(st, "pout", [128, D]) for _ in range(2)]
            pth = self.ps(st, "pth", [128, 8, 128], BF16)
            h2v = scr["H2T"].t.rearrange("(c p) n -> p c n", p=128)

            def L0(t):
                k4 = t % 4
                self.dma(xt[k4][:], xin.t[t * 128:(t + 1) * 128, :], r=[xin], w=[xt[k4]])
                self.dma(mx[k4][:], scr["MIX"][t * 128:(t + 1) * 128, 0:768], r=[scr["MIX"]], w=[mx[k4]])
                for r_, tn in enumerate((t - 1, t, t + 1)):
                    if 0 <= tn < sq.nt:
                        self.dma(upw[k4][:, r_, :], scr["MIX"][tn * 128:(tn + 1) * 128, 768:1024], r=[scr["MIX"]], w=[upw[k4]])

            def S1(t):
                k = t % 2
                k4 = t % 4
                rels = [r_ for r_, tn in enumerate((t - 1, t, t + 1)) if 0 <= tn < sq.nt]
                for c in range(6):
                    self.P(lambda h, k4=k4, c=c: h.transpose(out=ptm[:, c, :], in_=mx[k4][:, c * 128:(c + 1) * 128], identity=self.identb[:]),
                           [mx[k4], self.identb], [ptm])
                self.A(lambda h, k=k: h.copy(out=mixT[k][:, 0:6, :], in_=ptm[:, 0:6, :]), [ptm], [mixT[k]])
                for g in range(4):
                    for i, r_ in enumerate(rels):
                        ridx = r_
                        if r_ == 1 and t == 0:
                            ridx = 3
                        if r_ == 1 and t == sq.nt - 1:
                            ridx = 4
                        self.mm(ppl, ppl[(g % 2) * 64:(g % 2) * 64 + 64, g // 2, :], upw[k4][:, r_, g * 64:(g + 1) * 64], PT[:, ridx, g, :],
                                i == 0, i == len(rels) - 1, [upw[k4], PT])
                self.V(lambda h, k=k: h.tensor_copy(out=mixT[k][:, 6:8, :], in_=ppl[:]), [ppl], [mixT[k]])

            def S2a(t):
                k = t % 2
                k3 = t % 3
                po = pout[k]
                for half in range(2):
                    for c in range(8):
                        self.mm(po, po[:, half * 512:(half + 1) * 512], mixT[k][:, c, :], WOUT[:, c, half * 512:(half + 1) * 512],
                                c == 0, c == 7, [mixT[k], WOUT])
                self.V(lambda h, k=k, po=po: h.tensor_tensor(out=v[k][:], in0=po[:], in1=g1t[:], op=ALU.mult), [po, g1t], [v[k]])
                x_ = xt[t % 4]
                self.V(lambda h, k=k, x_=x_: h.scalar_tensor_tensor(out=v[k][:], in0=x_[:], scalar=ALPHA, in1=v[k][:], op0=ALU.mult, op1=ALU.add),
                       [x_, v[k]], [v[k]])
                self.layernorm_affine(v[k], v[k][:], v[k], v[k][:], xm[k3], xm[k3][:], lng, lnb, small[k])
                self.dma(scr["XMID"][t * 128:(t + 1) * 128, :], xm[k3][:], r=[xm[k3]], w=[scr["XMID"]])

            def S2b(t):
                k = t % 2
                k3 = t % 3
                st6, mv, rs = small[2 + k]
                x3 = xm[k3]
                for hh in range(2):
                    self.V(lambda h, hh=hh, x3=x3, st6=st6: h.bn_stats(out=st6[:, hh, :], in_=x3[:, hh * 512:(hh + 1) * 512]), [x3], [st6])
                self.V(lambda h, st6=st6, mv=mv: h.bn_aggr(out=mv[:], in_=st6[:].rearrange("p a b -> p (a b)")), [st6], [mv])
                self.rstd(mv[:, 1:2], rs, rs[:], [mv])
                self.V(lambda h, k=k, mv=mv, rs=rs: h.scalar_tensor_tensor(out=nmr[k][:], in0=mv[:, 0:1], scalar=-1.0, in1=rs[:, 0:1],
                                                                        op0=ALU.mult, op1=ALU.mult), [mv, rs], [nmr[k]])
                self.A(lambda h, k=k, x3=x3, rs=rs: h.activation(out=hn[k][:], in_=x3[:], func=AF.Identity, scale=rs[:, 0:1], bias=nmr[k][:]),
                       [x3, rs, nmr[k]], [hn[k]])
                self.G(lambda h, k=k: h.tensor_tensor(out=hn[k][:], in0=hn[k][:], in1=sc2[:], op=ALU.mult), [hn[k], sc2], [hn[k]])
                self.G(lambda h, k=k: h.tensor_tensor(out=h2[k][:], in0=hn[k][:], in1=sh2[:], op=ALU.add), [hn[k], sh2], [h2[k]])

            def S3(t):
                k = t % 2
                for kc in range(8):
                    self.P(lambda h, k=k, kc=kc: h.transpose(out=pth[:, kc, :], in_=h2[k][:, kc * 128:(kc + 1) * 128], identity=self.identb[:]),
                           [h2[k], self.identb], [pth])
                self.A(lambda h, k=k: h.copy(out=h2T[k][:], in_=pth[:]), [pth], [h2T[k]])
                self.dma(h2v[:, :, t * 128:(t + 1) * 128], h2T[k][:], r=[h2T[k]], w=[scr["H2T"]])

            for step in range(-2, sq.nt + 3):
                if 0 <= step + 2 < sq.nt:
                    L0(step + 2)
                if 0 <= step < sq.nt:
                    S1(step)
                if 0 <= step - 1 < sq.nt:
                    S2a(step - 1)
                if 0 <= step - 2 < sq.nt:
                    S2b(step - 2)
                if 0 <= step - 3 < sq.nt:
                    S3(step - 3)
            self.S.barrier()

    def phase_M(self, l, sq, xout):
        din = self.din
        scr = sq.scr
        is_ctx = sq.name == "ctx"
        if is_ctx:
            rows, W, RB = 1, CTXL, 1
            taps = [(0, dx, 3 + (dx + 1)) for dx in (-1, 0, 1)]
        else:
            rows, W, RB = SEQ // GRID_W, GRID_W, 16
            taps = [(dy, dx, (dy + 1) * 3 + (dx + 1)) for dy in (-1, 0, 1) for dx in (-1, 0, 1)]
        nq = max(1, (RB * W) // 512)
        qrows = RB // nq if not is_ctx else 1
        qn = qrows * W
        ntb = (RB * W) // 128
        with contextlib.ExitStack() as st:
            WDN = self.sb(st, "WDN", [128, 22, D], BF16)
            self.dma(WDN[:], self.WDNd[:, :, :], r=[self.WDNd], w=[WDN])
            fcw = self.sb(st, "fcw", [128, 44, 9])
            fcb = self.sb(st, "fcb", [128, 44])
            self.dma(fcw[:], din["ffn_cw"][l, :, :, :], w=[fcw])
            self.dma(fcb[:], din["ffn_cb"][l, :, :], w=[fcb])

            def bct(name, src):
                t_ = self.sb(st, name, [128, D])
                self.dma(t_[:], src.partition_broadcast(128), r=[self.MOD], w=[t_])
                return t_
            g2t = bct("g2t", self.MOD[l, sq.row:sq.row + 1, 5120:6144])
            lng = bct("lng2", din["ln2_g"][l, :, :])
            lnb = bct("lnb2", din["ln2_b"][l, :, :])
            nhr = RB + 2
            hb = [self.sb(st, "hbk", [128, 8, nhr * W], BF16)]
            Abuf = [self.sb(st, "Abuf", [128, nhr, W + 2], BF16) for _ in range(4)]
            for a_ in Abuf:
                self.G(lambda h, a_=a_: h.memset(a_[:], 0.0), [], [a_])
            actT = self.sb(st, "actT", [128, 22, RB * W], BF16)
            wu = [self.sb(st, "wu", [128, 2, 8, 128], BF16) for _ in range(3)]
            dgt = [self.sb(st, "dgt", [128, len(taps), 128], BF16) for _ in range(3)]
            gl = [self.sb(st, "gl", [128, qn]) for _ in range(2)]
            npu = (nhr * W + 511) // 512
            pu = [self.ps(st, "pu", [128, npu * 512]) for _ in range(2)]
            pcv = [self.ps(st, "pcv", [128, 512]) for _ in range(2)]
            xmt = [self.sb(st, "xmt", [128, D]) for _ in range(2)]
            v = [self.sb(st, "vf", [128, D]) for _ in range(2)]
            xo = xmt
            small = [(self.sb(st, "st6", [128, 2, 6]), self.sb(st, "mv", [128, 2]), self.sb(st, "rs", [128, 1])) for _ in range(2)]
            h2v = scr["H2T"].t.rearrange("(c p) n -> p c n", p=128)
            nblk = rows // RB
            iu = 0
            icv = 0
            idg = 0
            for bi in range(nblk):
                r0, r1 = bi * RB, (bi + 1) * RB
                hr0, hr1 = max(r0 - 1, 0), min(r1 + 1, rows)
                nh = hr1 - hr0
                ar0 = hr0 - (r0 - 1)
                hb_ = hb[0]
                self.dma(hb_[:, :, 0:nh * W], h2v[:, :, hr0 * W:hr1 * W], r=[scr["H2T"]], w=[hb_])
                if bi == nblk - 1 and nblk > 1:
                    for a_ in Abuf:
                        self.G(lambda h, a_=a_: h.memset(a_[:, nhr - 1, :], 0.0), [], [a_])
                units = []
                for pr in range(22):
                    for vg in (1, 0):
                        units.append((pr, vg))
                state = {}

                def up_stage(pr, vg, hb_=hb_, nh=nh, ar0=ar0):
                    nonlocal iu, idg
                    wu_ = wu[pr % 3]
                    if vg == 1:
                        self.dma(wu_[:], self.WUPd[pr, :, :, :, :], r=[self.WUPd], w=[wu_])
                    cc = vg * 22 + pr
                    pu_ = pu[iu % 2]
                    iu += 1
                    A_ = Abuf[vg * 2 + (pr % 2)]
                    ntok = nh * W
                    for nb in range((ntok + 511) // 512):
                        n0, n1 = nb * 512, min(ntok, (nb + 1) * 512)
                        for kc in range(8):
                            self.mm(pu_, pu_[:, n0:n1], wu_[:, vg, kc, :], hb_[:, kc, n0:n1], kc == 0, kc == 7, [wu_, hb_])
                    self.A(lambda h, pu_=pu_, A_=A_, ntok=ntok, nh=nh, ar0=ar0: h.copy(
                        out=A_[:, ar0:ar0 + nh, 1:W + 1], in_=pu_[:, 0:ntok].rearrange("p (a b) -> p a b", b=W)), [pu_], [A_])
                    dg_ = dgt[idg % 3]
                    idg += 1
                    nt_ = len(taps)
                    w0 = taps[0][2]
                    self.V(lambda h, dg_=dg_, cc=cc, nt_=nt_, w0=w0: h.tensor_tensor(
                        out=dg_[:], in0=self.identf[:].unsqueeze(1).to_broadcast([128, nt_, 128]),
                        in1=fcw[:, cc, w0:w0 + nt_].unsqueeze(2).to_broadcast([128, nt_, 128]), op=ALU.mult),
                        [self.identf, fcw], [dg_])
                    state[(pr, vg)] = (A_, dg_, cc)

                def conv_stage(pr, vg):
                    nonlocal icv
                    A_, dg_, cc = state[(pr, vg)]
                    for q in range(nq):
                        pc_ = pcv[icv % 2]
                        icv += 1
                        for ti, (dy, dx, widx) in enumerate(taps):
                            ra = 1 + q * qrows + dy
                            self.mm(pc_, pc_[:, 0:qn], dg_[:, ti, :], A_[:, ra:ra + qrows, 1 + dx:1 + dx + W],
                                    ti == 0, ti == len(taps) - 1, [dg_, A_])
                        g_ = gl[q % 2]
                        if vg == 1:
                            self.A(lambda h, pc_=pc_, g_=g_, cc=cc: h.activation(out=g_[:], in_=pc_[:, 0:qn], func=AF.Gelu_apprx_tanh,
                                                                              bias=fcb[:, cc:cc + 1]), [pc_, fcb], [g_])
                        else:
                            self.V(lambda h, pc_=pc_, g_=g_, cc=cc, pr=pr, q=q: h.scalar_tensor_tensor(
                                out=actT[:, pr, q * qn:(q + 1) * qn], in0=pc_[:, 0:qn], scalar=fcb[:, cc:cc + 1], in1=g_[:],
                                op0=ALU.add, op1=ALU.mult), [pc_, fcb, g_], [actT])
                for ui in range(len(units) + 1):
                    if ui < len(units):
                        up_stage(*units[ui])
                    if ui >= 1:
                        conv_stage(*units[ui - 1])
                for tb in range(ntb):
                    t = bi * ntb + tb
                    k = t % 2
                    pd = pu[iu % 2]
                    iu += 1
                    self.dma(xmt[k][:], scr["XMID"][t * 128:(t + 1) * 128, :], r=[scr["XMID"]], w=[xmt[k]])
                    for half in range(2):
                        for fc in range(22):
                            self.mm(pd, pd[:, half * 512:(half + 1) * 512], actT[:, fc, tb * 128:(tb + 1) * 128],
                                    WDN[:, fc, half * 512:(half + 1) * 512], fc == 0, fc == 21, [actT, WDN])
                    self.V(lambda h, k=k, pd=pd: h.tensor_tensor(out=v[k][:], in0=pd[:, 0:D], in1=g2t[:], op=ALU.mult), [pd, g2t], [v[k]])
                    self.V(lambda h, k=k: h.scalar_tensor_tensor(out=v[k][:], in0=xmt[k][:], scalar=ALPHA, in1=v[k][:], op0=ALU.mult, op1=ALU.add),
                           [xmt[k], v[k]], [v[k]])
                    self.layernorm_affine(v[k], v[k][:], v[k], v[k][:], xo[k], xo[k][:], lng, lnb, small[k])
                    self.dma(xout.t[t * 128:(t + 1) * 128, :], xo[k][:], r=[xo[k]], w=[xout])
            self.S.barrier()


_PROG_CACHE = {}


def _get_prog():
    if "nc" not in _PROG_CACHE:
        p = Prog()
        _PROG_CACHE["nc"] = p.build()
    return _PROG_CACHE["nc"]


def kernel(**inputs):
    nc = _get_prog()
    in_maps = [_host_inputs(inputs, b) for b in range(NCORES)]
    res = run_bass_kernel_spmd(nc, in_maps, core_ids=list(range(NCORES)))
    out = np.stack([np.asarray(res.results[b]["y"], dtype=np.float32) for b in range(NCORES)], 0)
    return out
```

```python
import contextlib
import math
import numpy as np
import concourse.bass as bass
import concourse.mybir as mybir
from concourse.bass_utils import run_bass_kernel_spmd

F32 = mybir.dt.float32
BF16 = mybir.dt.bfloat16
AF = mybir.ActivationFunctionType
ALU = mybir.AluOpType

D = 1024
SEQ = 8192
CTXL = 256
DEPTH = 2
DFF = 2816
NIN = 2064
OFF_Z, OFF_XBC, OFF_DT, OFF_FNET, OFF_POOL = 0, 512, 1536, 1552, 1808
ALPHA = (2.0 * DEPTH) ** 0.25
EPS = 1e-6
GRID_W = 64
NCORES = 4
WCOLS = 2320
C_DT, C_POOL, C_FN = 1536, 1552, 1808
NEGBIG = -30000.0
EPOCH = 30000


class Buf:
    __slots__ = ("lw", "rd")

    def __init__(self):
        self.lw = None
        self.rd = []


class Sched:
    ENG = ["tensor", "vector", "scalar", "gpsimd", "sync"]

    def __init__(self, nc, stack):
        self.nc = nc
        self.stack = stack
        self.ops = {e: [] for e in self.ENG}
        self.cnt = {e: 0 for e in self.ENG}
        self.sems = {}
        self.waited = {e: {} for e in self.ENG}
        self.same = {"vector", "scalar", "gpsimd"}
        self.dcnt = {}
        self.dma_rr = {e: 0 for e in self.ENG}
        self.NDMA = 8
        self.last_ev = {}

    def sem(self, key):
        if key not in self.sems:
            self.sems[key] = self.stack.enter_context(self.nc.semaphore("s_%s_%s_%s" % key))
        return self.sems[key]

    def _waits(self, eng, reads, writes):
        deps = {}

        def add(ev):
            if ev is None:
                return
            k, v = ev
            if deps.get(k, 0) < v:
                deps[k] = v
        for b in reads:
            add(b.lw)
        for b in writes:
            add(b.lw)
            for r in b.rd:
                add(r)
        out = []
        for k, v in deps.items():
            if k[0] == eng and k[2] == "c" and eng not in self.same:
                continue
            if self.waited[eng].get(k, 0) >= v:
                continue
            self.waited[eng][k] = v
            out.append((self.sem(k), v))
        return out

    def _commit(self, ev, reads, writes):
        for b in reads:
            b.rd.append(ev)
            if len(b.rd) > 24:
                best = {}
                for k, v in b.rd:
                    if best.get(k, 0) < v:
                        best[k] = v
                b.rd = list(best.items())
        for b in writes:
            b.lw = ev
            b.rd = []
        self.last_ev[ev[0]] = ev[1]

    def op(self, eng, fn, reads=(), writes=()):
        waits = self._waits(eng, reads, writes)
        c = self.cnt[eng]
        self.cnt[eng] = c + 1
        key = (eng, c // EPOCH, "c")
        val = c % EPOCH + 1
        s = self.sem(key)

        def run(h, waits=waits, fn=fn, s=s):
            for (ws, wv) in waits:
                h.wait_ge(ws, wv)
            fn(h).then_inc(s, 1)
        self.ops[eng].append(run)
        self._commit((key, val), reads, writes)

    def dma(self, out, in_, reads=(), writes=(), eng="sync"):
        waits = self._waits(eng, reads, writes)
        i = self.dma_rr[eng]
        self.dma_rr[eng] = (i + 1) % self.NDMA
        key = (eng, i, "d")
        n = self.dcnt.get(key, 0) + 1
        self.dcnt[key] = n
        s = self.sem(key)
        prev = (n - 1) * 16
        if self.waited[eng].get(key, 0) < prev:
            self.waited[eng][key] = prev
        else:
            prev = 0

        def run(h, waits=waits, s=s, prev=prev, out=out, in_=in_):
            for (ws, wv) in waits:
                h.wait_ge(ws, wv)
            if prev > 0:
                h.wait_ge(s, prev)
            h.dma_start(out=out, in_=in_).then_inc(s, 16)
        self.ops[eng].append(run)
        self._commit((key, n * 16), reads, writes)

    def barrier(self):
        evs = dict(self.last_ev)
        for eng in self.ENG:
            waits = []
            for k, v in evs.items():
                if k[0] == eng and k[2] == "c":
                    continue
                if self.waited[eng].get(k, 0) >= v:
                    continue
                self.waited[eng][k] = v
                waits.append((self.sem(k), v))

            def run(h, waits=waits):
                for (ws, wv) in waits:
                    h.wait_ge(ws, wv)
            self.ops[eng].append(run)

    def finish(self, block):
        self.barrier()
        ops = self.ops

        @block.tensor
        def _(h):
            for f in ops["tensor"]:
                f(h)

        @block.vector
        def _(h):
            for f in ops["vector"]:
                f(h)

        @block.scalar
        def _(h):
            for f in ops["scalar"]:
                f(h)

        @block.gpsimd
        def _(h):
            for f in ops["gpsimd"]:
                f(h)

        @block.sync
        def _(h):
            for f in ops["sync"]:
                f(h)


class T:
    __slots__ = ("t", "b")

    def __init__(self, t):
        self.t = t
        self.b = Buf()

    def __getitem__(self, k):
        return self.t[k]


def _consts():
    c = {}
    c["ident"] = np.eye(128, dtype=np.float32)
    k = np.arange(128)
    U = (k[:, None] <= k[None, :]).astype(np.float32)
    Us = (k[:, None] < k[None, :]).astype(np.float32)
    c["tri"] = np.stack([U, -U, Us, np.ones((128, 128), np.float32)], 1).astype(np.float32)
    upat = np.zeros((128, 16, 128), np.float32)
    upat[:, 0:8, :] = U[:, None, :]
    upat[:, 8:16, :] = -Us[:, None, :]
    c["upat"] = upat
    neg = np.zeros((128, 16, 128), np.float32)
    neg[:, 0:8, :] = np.where(k[:, None] > k[None, :], NEGBIG, 0.0)[:, None, :]
    neg[:, 8:16, :] = np.where(k[:, None] < k[None, :], NEGBIG, 0.0)[:, None, :]
    c["negm"] = neg
    m = np.arange(64)
    ang = 2 * np.pi * np.outer(m, m) / 64.0
    nrm = 1.0 / math.sqrt(SEQ * 64.0)
    cc = np.cos(ang) * nrm
    sc = -np.sin(ang) * nrm
    c["chdft"] = np.stack([np.concatenate([cc, cc], 1), np.concatenate([sc, sc], 1)], 1).astype(np.float32)
    a128 = 2 * np.pi * np.outer(k, k) / 128.0
    C, S = np.cos(a128), np.sin(a128)
    c["dftA"] = np.stack([np.concatenate([C, -S], 1), np.concatenate([S, C], 1)], 1).astype(np.float32)
    l2 = np.arange(64)
    th = 2 * np.pi * np.outer(l2, k) / 8192.0
    c["twid"] = np.stack([np.cos(th), np.sin(th), -np.sin(th)], 1).astype(np.float32)
    a64 = 2 * np.pi * np.outer(l2, l2) / 64.0
    c["dftB"] = np.stack([np.cos(a64), np.sin(a64)], 1).astype(np.float32)
    kk = np.arange(256)
    a256 = 2 * np.pi * np.outer(kk, kk) / 256.0
    sc256 = math.sqrt(SEQ / CTXL)
    c256 = (np.cos(a256) * sc256).reshape(2, 128, 256).transpose(1, 0, 2)
    s256 = (np.sin(a256) * sc256).reshape(2, 128, 256).transpose(1, 0, 2)
    c["dftC"] = np.stack([c256, s256], 1).astype(np.float32)
    pt = np.zeros((128, 5, 4, 128), np.float32)
    for gi, win in enumerate((2, 4, 8, 16)):
        left = win // 2
        right = win - 1 - left
        for l in range(128):
            for rel, (off, first, last) in enumerate([(-128, False, False), (0, False, False), (128, False, False),
                                                      (0, True, False), (0, False, True)]):
                lo = l - left
                hi = l + right + 1
                if first:
                    lo = max(lo, 0)
                if last:
                    hi = min(hi, 128)
                cnt = hi - lo
                for s in range(lo, hi):
                    sl = s - off
                    if 0 <= sl < 128:
                        pt[sl, rel, gi, l] += 1.0 / cnt
                if off == 0:
                    pt[l, rel, gi, l] -= 1.0
    c["poolm"] = pt
    return c


_CONST = None


def _host_inputs(inp, b):
    global _CONST
    if _CONST is None:
        _CONST = _consts()
    f = np.float32
    A = np.ascontiguousarray
    d = dict(_CONST)
    d["x"] = A(inp["x"][b], dtype=f)
    d["ctx"] = A(inp["ctx"][b], dtype=f)
    cc = np.stack([np.asarray(inp["c"][b], f), np.asarray(inp["c_ctx"], f)], 1)
    d["cc"] = A(cc.reshape(8, 128, 2).transpose(1, 0, 2))
    d["w_ada"] = A(inp["w_ada"], dtype=f)
    d["b_ada"] = A(np.asarray(inp["b_ada"], f).reshape(DEPTH, 1, 6 * D))
    w_in = np.asarray(inp["w_in"], f)
    main = np.concatenate([w_in[:, :, OFF_Z:OFF_DT], w_in[:, :, OFF_DT:OFF_FNET]], 2)
    d["w_in_r"] = A(main.reshape(DEPTH, 8, 128, 1552).transpose(0, 2, 1, 3))
    wf = w_in[:, :, OFF_FNET:OFF_POOL]
    d["w_fT"] = A(wf.transpose(0, 2, 1).reshape(DEPTH, 2, 128, D).transpose(0, 2, 1, 3))
    wp = w_in[:, :, OFF_POOL:NIN]
    d["w_pT"] = A(wp.transpose(0, 2, 1).reshape(DEPTH, 2, 128, D).transpose(0, 2, 1, 3))
    d["fnet_w"] = A(np.asarray(inp["fnet_w"], f).transpose(0, 2, 1, 3))
    d["pool_w"] = A(inp["pool_w"], dtype=f)
    d["ssd_cw"] = A(np.asarray(inp["ssd_conv_w"], f).reshape(DEPTH, 3, 8, 128).transpose(0, 3, 2, 1))
    d["ssd_cb"] = A(np.asarray(inp["ssd_conv_b"], f).reshape(DEPTH, 8, 128).transpose(0, 2, 1))
    d["dt_bias"] = A(np.asarray(inp["ssd_dt_bias"], f).reshape(DEPTH, 1, 16))
    d["a_log"] = A(np.asarray(inp["ssd_a_log"], f).reshape(DEPTH, 1, 16))
    d["d_rep"] = A(np.repeat(np.asarray(inp["ssd_d"], f), 64, axis=1).reshape(DEPTH, 1, 512))
    rs = np.concatenate([np.asarray(inp["ssd_norm_w"], f), np.ones((DEPTH, 256), f),
                         np.asarray(inp["pool_scale"], f)], 1)
    d["rowscale"] = A(rs.reshape(DEPTH, 8, 128).transpose(0, 2, 1))
    d["w_out"] = A(np.asarray(inp["w_out"], f).reshape(DEPTH, 8, 128, D).transpose(0, 2, 1, 3))
    for n in ("ln1_g", "ln1_b", "ln2_g", "ln2_b"):
        d[n] = A(np.asarray(inp[n], f).reshape(DEPTH, 1, D))
    d["w_up"] = A(np.asarray(inp["ffn_w_up"], f).reshape(DEPTH, 8, 128, 2 * DFF).transpose(0, 2, 1, 3))
    d["w_down"] = A(np.asarray(inp["ffn_w_down"], f).reshape(DEPTH, 22, 128, D).transpose(0, 2, 1, 3))
    d["ffn_cw"] = A(np.asarray(inp["ffn_conv_w"], f).reshape(DEPTH, 9, 44, 128).transpose(0, 3, 2, 1))
    d["ffn_cb"] = A(np.asarray(inp["ffn_conv_b"], f).reshape(DEPTH, 44, 128).transpose(0, 2, 1))
    return d


_IN_SHAPES = {
    "x": [SEQ, D], "ctx": [CTXL, D], "cc": [128, 8, 2], "w_ada": [DEPTH, D, 6 * D], "b_ada": [DEPTH, 1, 6 * D],
    "w_in_r": [DEPTH, 128, 8, 1552], "w_fT": [DEPTH, 128, 2, D], "w_pT": [DEPTH, 128, 2, D],
    "fnet_w": [DEPTH, 64, 4, 64], "pool_w": [DEPTH, 4, 64, 64], "ssd_cw": [DEPTH, 128, 8, 3],
    "ssd_cb": [DEPTH, 128, 8], "dt_bias": [DEPTH, 1, 16], "a_log": [DEPTH, 1, 16], "d_rep": [DEPTH, 1, 512],
    "rowscale": [DEPTH, 128, 8], "w_out": [DEPTH, 128, 8, D], "ln1_g": [DEPTH, 1, D], "ln1_b": [DEPTH, 1, D],
    "ln2_g": [DEPTH, 1, D], "ln2_b": [DEPTH, 1, D], "w_up": [DEPTH, 128, 8, 2 * DFF],
    "w_down": [DEPTH, 128, 22, D], "ffn_cw": [DEPTH, 128, 44, 9], "ffn_cb": [DEPTH, 128, 44],
    "ident": [128, 128], "tri": [128, 4, 128], "upat": [128, 16, 128], "negm": [128, 16, 128],
    "chdft": [64, 2, 128], "dftA": [128, 2, 256], "twid": [64, 3, 128], "dftB": [64, 2, 64],
    "dftC": [128, 2, 2, 256], "poolm": [128, 5, 4, 128],
}


class Seq:
    def __init__(self, name, L, row, toff):
        self.name = name
        self.L = L
        self.nt = L // 128
        self.row = row
        self.toff = toff
        self.scr = {}


class Prog:
    def __init__(self, dbg=(), nlayers=DEPTH, stop=None):
        self.dbg = set(dbg)
        self.nlayers = nlayers
        self.stop = stop
        self.nc = bass.Bass("TRN2", target_bir_lowering=False)
        nc = self.nc
        self.din = {n: nc.dram_tensor(n, s, F32, kind="ExternalInput").ap() for n, s in _IN_SHAPES.items()}
        self.out = nc.dram_tensor("y", [SEQ, D], F32, kind="ExternalOutput").ap()
        self.dbg_outs = {}

    def dram(self, name, shape, dt):
        kind = "ExternalOutput" if name in self.dbg else "Internal"
        t = self.nc.dram_tensor(name, shape, dt, kind=kind)
        if name in self.dbg:
            self.dbg_outs[name] = (shape, dt)
        return T(t.ap())

    def sb(self, st, name, shape, dt=F32):
        self._n += 1
        return T(st.enter_context(self.nc.sbuf_tensor("%s_%d" % (name, self._n), shape, dt)))

    def ps(self, st, name, shape, dt=F32):
        self._n += 1
        return T(st.enter_context(self.nc.psum_tensor("%s_%d" % (name, self._n), shape, dt)))

    def V(self, fn, r=(), w=()):
        self.S.op("vector", fn, [t.b for t in r], [t.b for t in w])

    def G(self, fn, r=(), w=()):
        self.S.op("gpsimd", fn, [t.b for t in r], [t.b for t in w])

    def A(self, fn, r=(), w=()):
        self.S.op("scalar", fn, [t.b for t in r], [t.b for t in w])

    def P(self, fn, r=(), w=()):
        self.S.op("tensor", fn, [t.b for t in r], [t.b for t in w])

    def dma(self, out, in_, r=(), w=()):
        self.S.dma(out, in_, [t.b for t in r], [t.b for t in w])

    def mm(self, out_t, out_ap, lhsT, rhs, start, stop, r):
        self.P(lambda h: h.matmul(out_ap, lhsT=lhsT, rhs=rhs, start=start, stop=stop), r, [out_t])

    def cast_load(self, st, name, src_ap, shape, dt=BF16, eng="V"):
        tmp = self.sb(st, name + "_f", shape, F32)
        dst = self.sb(st, name, shape, dt)
        self.dma(tmp[:], src_ap, w=[tmp])
        (self.V if eng == "V" else self.G)(lambda h: h.tensor_copy(out=dst[:], in_=tmp[:]), [tmp], [dst])
        return dst

    def rstd(self, var_ap, out_t, out_ap, r, scale=1.0):
        self.A(lambda h: h.activation(out=out_ap, in_=var_ap, func=AF.Ln, bias=self.epsT[:], scale=scale),
               list(r) + [self.epsT], [out_t])
        self.A(lambda h: h.activation(out=out_ap, in_=out_ap, func=AF.Exp, scale=-0.5), [out_t], [out_t])

    def layernorm(self, st, src_t, src_ap, dst_t, dst_ap, tag, small):
        st6, mv, rs = small
        for hh in range(2):
            self.V(lambda h, hh=hh: h.bn_stats(out=st6[:, hh, :], in_=src_ap[:, hh * 512:(hh + 1) * 512]), [src_t], [st6])
        self.V(lambda h: h.bn_aggr(out=mv[:], in_=st6[:].rearrange("p a b -> p (a b)")), [st6], [mv])
        self.rstd(mv[:, 1:2], rs, rs[:], [mv])
        self.V(lambda h: h.tensor_scalar(out=dst_ap, in0=src_ap, scalar1=mv[:, 0:1], scalar2=rs[:, 0:1],
                                         op0=ALU.subtract, op1=ALU.mult), [src_t, mv, rs], [dst_t])

    def layernorm_affine(self, src_t, src_ap, tmp_t, tmp_ap, dst_t, dst_ap, gain_t, bias_t, small):
        st6, mv, rs = small
        for hh in range(2):
            self.V(lambda h, hh=hh: h.bn_stats(out=st6[:, hh, :], in_=src_ap[:, hh * 512:(hh + 1) * 512]), [src_t], [st6])
        self.V(lambda h: h.bn_aggr(out=mv[:], in_=st6[:].rearrange("p a b -> p (a b)")), [st6], [mv])
        self.rstd(mv[:, 1:2], rs, rs[:], [mv])
        self.V(lambda h: h.scalar_tensor_tensor(out=tmp_ap, in0=src_ap, scalar=mv[:, 0:1], in1=gain_t[:], op0=ALU.subtract, op1=ALU.mult),
               [src_t, mv, gain_t], [tmp_t])
        self.V(lambda h: h.scalar_tensor_tensor(out=dst_ap, in0=tmp_ap, scalar=rs[:, 0:1], in1=bias_t[:], op0=ALU.mult, op1=ALU.add),
               [tmp_t, rs, bias_t], [dst_t])

    def build(self):
        nc = self.nc
        self._n = 0
        with contextlib.ExitStack() as gst:
            self.S = Sched(nc, gst)
            S = self.S
            din = self.din
            lat = Seq("lat", SEQ, 0, 0)
            ctx = Seq("ctx", CTXL, 1, SEQ // 128)
            for sq in (lat, ctx):
                L = sq.L
                n = sq.name
                sq.scr = {
                    "SZ": self.dram("SZ_" + n, [L, 512], BF16),
                    "XBCT": self.dram("XBCT_" + n, [D, L], BF16),
                    "UF": self.dram("UF_" + n, [L, 512], BF16),
                    "MIX": self.dram("MIX_" + n, [L, 1024], BF16),
                    "YP": self.dram("YP_" + n, [L, 512], BF16),
                    "XMID": self.dram("XMID_" + n, [L, D], F32),
                    "H2T": self.dram("H2T_" + n, [D, L], BF16),
                    "X1": self.dram("X1_" + n, [L, D], F32),
                }
            self.MOD = self.dram("MOD", [DEPTH, 2, 6 * D], F32)
            self.WINd = self.dram("WINd", [128, 8, WCOLS], BF16)
            self.WOUTd = self.dram("WOUTd", [128, 8, D], BF16)
            self.WUPd = self.dram("WUPd", [22, 128, 2, 8, 128], BF16)
            self.WDNd = self.dram("WDNd", [128, 22, D], BF16)
            self.identf = self.sb(gst, "identf", [128, 128], F32)
            self.identb = self.sb(gst, "identb", [128, 128], BF16)
            self.dma(self.identf[:], din["ident"][:, :], w=[self.identf])
            self.V(lambda h: h.tensor_copy(out=self.identb[:], in_=self.identf[:]), [self.identf], [self.identb])
            self.epsT = self.sb(gst, "epsT", [128, 1], F32)
            self.V(lambda h: h.memset(self.epsT[:], EPS), [], [self.epsT])
            self.Sf = self.sb(gst, "Sf", [128, 512], F32)
            self.Sb = self.sb(gst, "Sb", [128, 512], F32)
            self.SfB = self.sb(gst, "SfB", [128, 512], BF16)
            self.SbB = self.sb(gst, "SbB", [128, 512], BF16)
            NTT = SEQ // 128 + CTXL // 128
            self.DT = self.sb(gst, "DT", [128, NTT, 16], F32)
            self.dAb = self.sb(gst, "dAb", [128, NTT, 16], BF16)

            for l in range(self.nlayers):
                last = (l == DEPTH - 1)
                xin_lat = T(din["x"]) if l == 0 else lat.scr["X1"]
                xin_ctx = T(din["ctx"]) if l == 0 else ctx.scr["X1"]
                xout_lat = T(self.out) if last else lat.scr["X1"]
                if l == 0:
                    self._xin0 = (xin_lat, xin_ctx)
                self.phase_mod(l)
                if self.stop == "mod":
                    break
                self.phase_weights(l)
                if self.stop == "weights":
                    break
                self.phase_A(l, ctx, xin_ctx)
                self.phase_A(l, lat, xin_lat)
                if "DTd" in self.dbg and l == 0:
                    dtd = self.dram("DTd", [128, SEQ // 128 + CTXL // 128, 16], F32)
                    self.dma(dtd[:, :, :], self.DT[:], r=[self.DT], w=[dtd])
                if self.stop == "A":
                    break
                self.V(lambda h: h.memset(self.Sf[:], 0.0), [], [self.Sf])
                self.V(lambda h: h.memset(self.Sb[:], 0.0), [], [self.Sb])
                self.V(lambda h: h.memset(self.SfB[:], 0.0), [], [self.SfB])
                self.V(lambda h: h.memset(self.SbB[:], 0.0), [], [self.SbB])
                self.phase_sweep(l, ctx, fwd=True)
                self.phase_sweep(l, ctx, fwd=False, full=not last)
                if self.stop == "Sctx":
                    break
                self.phase_sweep(l, lat, fwd=True)
                if self.stop == "Sfwd":
                    break
                if not last:
                    self.phase_F_ctx(l, ctx)
                self.phase_F_lat(l, lat)
                if self.stop == "F":
                    break
                self.phase_sweep(l, lat, fwd=False, full=True)
                if self.stop == "Sbwd":
                    break
                if not last:
                    self.phase_C(l, ctx, xin_ctx)
                self.phase_C(l, lat, xin_lat)
                if self.stop == "C":
                    break
                if not last:
                    self.phase_M(l, ctx, ctx.scr["X1"])
                self.phase_M(l, lat, xout_lat)
            with nc.Block() as block:
                S.finish(block)
        return nc

    def phase_mod(self, l):
        din = self.din
        with contextlib.ExitStack() as st:
            cct = self.sb(st, "cct", [128, 8, 2])
            scs = self.sb(st, "scs", [128, 8, 2])
            self.dma(cct[:], din["cc"][:, :, :], w=[cct])
            self.A(lambda h: h.activation(out=scs[:], in_=cct[:], func=AF.Silu), [cct], [scs])
            brow = self.sb(st, "brow", [1, 6 * D])
            self.dma(brow[:], din["b_ada"][l, :, :], w=[brow])
            ones2 = self.sb(st, "ones2", [1, 2])
            self.V(lambda h: h.memset(ones2[:], 1.0), [], [ones2])
            modsb = self.sb(st, "modsb", [2, 6 * D])
            wa = [self.sb(st, "wa", [128, 8, 512]) for _ in range(2)]
            pm = [self.ps(st, "pm", [2, 512]) for _ in range(2)]
            wsrc = din["w_ada"][l].rearrange("(kc p) n -> p kc n", p=128)
            for cb in range(12):
                w_ = wa[cb % 2]
                p_ = pm[cb % 2]
                self.dma(w_[:], wsrc[:, :, cb * 512:(cb + 1) * 512], w=[w_])
                for kc in range(8):
                    self.mm(p_, p_[:], scs[:, kc, :], w_[:, kc, :], kc == 0, False, [scs, w_])
                self.mm(p_, p_[:], ones2[:], brow[:, cb * 512:(cb + 1) * 512], False, True, [ones2, brow])
                add = 1.0 if cb in (2, 3, 8, 9) else 0.0
                self.V(lambda h, p_=p_, cb=cb, add=add: h.tensor_scalar(
                    out=modsb[:, cb * 512:(cb + 1) * 512], in0=p_[:], scalar1=add, scalar2=None, op0=ALU.add),
                    [p_], [modsb])
            self.dma(self.MOD[l], modsb[:], r=[modsb], w=[self.MOD])
            self.S.barrier()

    def phase_weights(self, l):
        din = self.din
        with contextlib.ExitStack() as st:
            for half in range(2):
                wf32 = self.sb(st, "wi32", [128, 4, 1552])
                wb16 = self.sb(st, "wi16", [128, 4, 1552], BF16)
                self.dma(wf32[:], din["w_in_r"][l, :, half * 4:(half + 1) * 4, :], w=[wf32])
                for kc in range(4):
                    eng = self.V if kc % 2 == 0 else self.G
                    eng(lambda h, kc=kc, wf32=wf32, wb16=wb16: h.tensor_copy(out=wb16[:, kc, :], in_=wf32[:, kc, :]),
                        [wf32], [wb16])
                self.dma(self.WINd[:, half * 4:(half + 1) * 4, 0:1552], wb16[:], r=[wb16], w=[self.WINd])
            chd = self.sb(st, "chd", [64, 2, 128])
            self.dma(chd[:], din["chdft"][:, :, :], w=[chd])
            fnw = self.sb(st, "fnw", [64, 4, 64])
            self.dma(fnw[:], din["fnet_w"][l, :, :, :], w=[fnw])
            pab = self.ps(st, "pab", [128, 4, 128])
            for g in range(4):
                for ri in range(2):
                    self.mm(pab, pab[:, g, ri * 64:(ri + 1) * 64], chd[:, ri, :], fnw[:, g, :], True, True, [chd, fnw])
            wfT = self.sb(st, "wfT", [128, 2, D])
            wpT = self.sb(st, "wpT", [128, 2, D])
            self.dma(wfT[:], din["w_fT"][l, :, :, :], w=[wfT])
            self.dma(wpT[:], din["w_pT"][l, :, :, :], w=[wpT])
            wfold = self.sb(st, "wfold", [128, 8, 768], BF16)
            pf = [self.ps(st, "pfold", [128, 512]) for _ in range(2)]
            bdf = self.sb(st, "bdf", [128, 2, 256])
            bdp = self.sb(st, "bdp", [128, 2, 128])
            self.V(lambda h: h.memset(bdf[:], 0.0), [], [bdf])
            self.V(lambda h: h.memset(bdp[:], 0.0), [], [bdp])
            for j in range(2):
                for gp in range(2):
                    g = 2 * j + gp
                    self.V(lambda h, j=j, gp=gp, g=g: h.tensor_copy(
                        out=bdf[gp * 64:(gp + 1) * 64, j, gp * 128:(gp + 1) * 128],
                        in_=pab[gp * 64:(gp + 1) * 64, g, :]), [pab], [bdf])
                    self.dma(bdp[gp * 64:(gp + 1) * 64, j, gp * 64:(gp + 1) * 64], din["pool_w"][l, g, :, :], w=[bdp])
            i = 0
            for kc in range(8):
                p_ = pf[i % 2]
                i += 1
                for j in range(2):
                    self.mm(p_, p_[:, j * 256:(j + 1) * 256], wfT[:, j, kc * 128:(kc + 1) * 128], bdf[:, j, :],
                            True, True, [wfT, bdf])
                self.A(lambda h, p_=p_, kc=kc: h.copy(out=wfold[:, kc, 256:768], in_=p_[:]), [p_], [wfold])
                p_ = pf[i % 2]
                i += 1
                for j in range(2):
                    self.mm(p_, p_[:, j * 128:(j + 1) * 128], wpT[:, j, kc * 128:(kc + 1) * 128], bdp[:, j, :],
                            True, True, [wpT, bdp])
                self.V(lambda h, p_=p_, kc=kc: h.tensor_copy(out=wfold[:, kc, 0:256], in_=p_[:, 0:256]), [p_], [wfold])
            self.dma(self.WINd[:, :, 1552:WCOLS], wfold[:], r=[wfold], w=[self.WINd])
            rsc = self.sb(st, "rsc", [128, 8])
            self.dma(rsc[:], din["rowscale"][l, :, :], w=[rsc])
            for half in range(2):
                wo32 = self.sb(st, "wo32", [128, 4, D])
                wo16 = self.sb(st, "wo16", [128, 4, D], BF16)
                self.dma(wo32[:], din["w_out"][l, :, half * 4:(half + 1) * 4, :], w=[wo32])
                for kc in range(4):
                    c = half * 4 + kc
                    self.V(lambda h, kc=kc, c=c, wo32=wo32, wo16=wo16: h.tensor_scalar(
                        out=wo16[:, kc, :], in0=wo32[:, kc, :], scalar1=rsc[:, c:c + 1], scalar2=None, op0=ALU.mult),
                        [wo32, rsc], [wo16])
                self.dma(self.WOUTd[:, half * 4:(half + 1) * 4, :], wo16[:], r=[wo16], w=[self.WOUTd])
            self.S.barrier()
        with contextlib.ExitStack() as st:
            u32 = [self.sb(st, "u32", [128, 8, 256]) for _ in range(3)]
            u16 = [self.sb(st, "u16", [128, 2, 8, 128], BF16) for _ in range(3)]
            i = 0
            for vg in range(2):
                for pb in range(11):
                    a, b_ = u32[i % 3], u16[i % 3]
                    c0 = vg * DFF + pb * 256
                    self.dma(a[:], din["w_up"][l, :, :, c0:c0 + 256], w=[a])
                    eng = (self.V, self.G, self.A)[i % 3]
                    if i % 3 == 2:
                        eng(lambda h, a=a, b_=b_: h.copy(out=b_[:], in_=a[:].rearrange("p k (q c) -> p q k c", q=2)), [a], [b_])
                    else:
                        eng(lambda h, a=a, b_=b_: h.tensor_copy(out=b_[:], in_=a[:].rearrange("p k (q c) -> p q k c", q=2)), [a], [b_])
                    for q in range(2):
                        self.dma(self.WUPd[2 * pb + q, :, vg, :, :], b_[:, q, :, :], r=[b_], w=[self.WUPd])
                    i += 1
            d32 = [self.sb(st, "d32", [128, 2, D]) for _ in range(2)]
            d16 = [self.sb(st, "d16", [128, 2, D], BF16) for _ in range(2)]
            for i in range(11):
                a, b_ = d32[i % 2], d16[i % 2]
                self.dma(a[:], din["w_down"][l, :, 2 * i:2 * i + 2, :], w=[a])
                eng = self.V if i % 2 == 0 else self.G
                eng(lambda h, a=a, b_=b_: h.tensor_copy(out=b_[:], in_=a[:]), [a], [b_])
                self.dma(self.WDNd[:, 2 * i:2 * i + 2, :], b_[:], r=[b_], w=[self.WDNd])
            self.S.barrier()

    def phase_A(self, l, sq, xin):
        din = self.din
        scr = sq.scr
        with contextlib.ExitStack() as st:
            WIN = self.sb(st, "WIN", [128, 8, WCOLS], BF16)
            self.dma(WIN[:], self.WINd[:, :, :], r=[self.WINd], w=[WIN])
            scp = self.sb(st, "scp", [128, D])
            sh = self.sb(st, "sh", [128, D])
            self.dma(scp[:], self.MOD[l, sq.row:sq.row + 1, 1024:2048].partition_broadcast(128), r=[self.MOD], w=[scp])
            self.dma(sh[:], self.MOD[l, sq.row:sq.row + 1, 0:1024].partition_broadcast(128), r=[self.MOD], w=[sh])
            dtb = self.sb(st, "dtb", [128, 16])
            nega = self.sb(st, "nega", [128, 16])
            self.dma(dtb[:], din["dt_bias"][l, :, :].partition_broadcast(128), w=[dtb])
            self.dma(nega[:], din["a_log"][l, :, :].partition_broadcast(128), w=[nega])
            self.A(lambda h: h.activation(out=nega[:], in_=nega[:], func=AF.Exp), [nega], [nega])
            self.V(lambda h: h.tensor_scalar(out=nega[:], in0=nega[:], scalar1=-1.0, scalar2=None, op0=ALU.mult), [nega], [nega])
            xt = [self.sb(st, "xt", [128, D]) for _ in range(4)]
            xn = [self.sb(st, "xn", [128, D]) for _ in range(2)]
            hb = [self.sb(st, "hb", [128, D], BF16) for _ in range(3)]
            small = [(self.sb(st, "st6", [128, 2, 6]), self.sb(st, "mv", [128, 2]), self.sb(st, "rs", [128, 1])) for _ in range(3)]
            stw = min(4, sq.nt)
            hT = [self.sb(st, "hT", [128, 8, stw * 128], BF16) for _ in range(2)]
            ptr = [self.ps(st, "ptr", [128, 8, 128], BF16) for _ in range(2)]
            pz = self.ps(st, "pz", [128, 512])
            pfn = self.ps(st, "pfn", [128, 512])
            ppd = self.ps(st, "ppd", [128, 512])
            pxb = [self.ps(st, "pxb", [128, 512]) for _ in range(2)]
            szt = [self.sb(st, "szt", [128, 512], BF16) for _ in range(2)]
            uft = [self.sb(st, "uft", [128, 512], BF16) for _ in range(2)]
            upt = [self.sb(st, "upt", [128, 256], BF16) for _ in range(2)]
            dtr = [self.sb(st, "dtr", [128, 16]) for _ in range(2)]
            xbst = [self.sb(st, "xbst", [128, 8, stw * 128], BF16) for _ in range(2)]
            xbd = scr["XBCT"].t.rearrange("(c p) n -> p c n", p=128)

            def L0(t):
                self.dma(xt[t % 4][:], xin.t[t * 128:(t + 1) * 128, :], r=[xin], w=[xt[t % 4]])

            def S1(t):
                k = t % 3
                x_ = xt[t % 4]
                n_ = xn[t % 2]
                self.layernorm_affine(x_, x_[:], n_, n_[:], hb[k], hb[k][:], scp, sh, small[k])

            def S2(t):
                k = t % 3
                p = t % 2
                sti, ti = divmod(t, stw)
                hT_ = hT[sti % 2]
                for kc in range(8):
                    self.P(lambda h, k=k, p=p, kc=kc: h.transpose(out=ptr[p][:, kc, :], in_=hb[k][:, kc * 128:(kc + 1) * 128],
                                                                 identity=self.identb[:]), [hb[k], self.identb], [ptr[p]])
                self.A(lambda h, p=p, ti=ti, hT_=hT_: h.copy(out=hT_[:, :, ti * 128:(ti + 1) * 128], in_=ptr[p][:]), [ptr[p]], [hT_])

            def S3(t):
                k = t % 2
                tt = sq.toff + t
                sti, ti = divmod(t, stw)
                hT_ = hT[sti % 2]
                for kc in range(8):
                    self.mm(ppd, ppd[:, 0:272], hT_[:, kc, ti * 128:(ti + 1) * 128], WIN[:, kc, C_DT:C_FN], kc == 0, kc == 7, [hT_, WIN])
                for kc in range(8):
                    self.mm(pz, pz[:], hT_[:, kc, ti * 128:(ti + 1) * 128], WIN[:, kc, 0:512], kc == 0, kc == 7, [hT_, WIN])
                for kc in range(8):
                    self.mm(pfn, pfn[:], hT_[:, kc, ti * 128:(ti + 1) * 128], WIN[:, kc, C_FN:WCOLS], kc == 0, kc == 7, [hT_, WIN])
                self.V(lambda h, k=k: h.tensor_tensor(out=dtr[k][:], in0=ppd[:, 0:16], in1=dtb[:], op=ALU.add), [ppd, dtb], [dtr[k]])
                self.V(lambda h, k=k: h.tensor_copy(out=upt[k][:], in_=ppd[:, 16:272]), [ppd], [upt[k]])
                self.A(lambda h, k=k: h.activation(out=dtr[k][:], in_=dtr[k][:], func=AF.Exp), [dtr[k]], [dtr[k]])
                self.A(lambda h, k=k, tt=tt: h.activation(out=self.DT[:, tt, :], in_=dtr[k][:], func=AF.Ln, bias=1.0), [dtr[k]], [self.DT])
                self.A(lambda h, k=k: h.activation(out=szt[k][:], in_=pz[:], func=AF.Silu), [pz], [szt[k]])
                self.V(lambda h, k=k: h.tensor_copy(out=uft[k][:], in_=pfn[:]), [pfn], [uft[k]])
                self.V(lambda h, tt=tt: h.tensor_tensor(out=self.dAb[:, tt, :], in0=self.DT[:, tt, :], in1=nega[:], op=ALU.mult),
                       [self.DT, nega], [self.dAb])
                self.dma(scr["MIX"][t * 128:(t + 1) * 128, 768:1024], upt[k][:], r=[upt[k]], w=[scr["MIX"]])
                self.dma(scr["SZ"][t * 128:(t + 1) * 128, :], szt[k][:], r=[szt[k]], w=[scr["SZ"]])
                self.dma(scr["UF"][t * 128:(t + 1) * 128, :], uft[k][:], r=[uft[k]], w=[scr["UF"]])
                if ti == stw - 1:
                    xb_ = xbst[sti % 2]
                    for ch in range(8):
                        p_ = pxb[ch % 2]
                        for kc in range(8):
                            self.mm(p_, p_[:, 0:stw * 128], WIN[:, kc, 512 + ch * 128:512 + (ch + 1) * 128], hT_[:, kc, :],
                                    kc == 0, kc == 7, [hT_, WIN])
                        if ch % 2 == 0:
                            self.A(lambda h, p_=p_, ch=ch, xb_=xb_: h.copy(out=xb_[:, ch, :], in_=p_[:, 0:stw * 128]), [p_], [xb_])
                        else:
                            self.V(lambda h, p_=p_, ch=ch, xb_=xb_: h.tensor_copy(out=xb_[:, ch, :], in_=p_[:, 0:stw * 128]), [p_], [xb_])
                    c0 = sti * stw * 128
                    self.dma(xbd[:, :, c0:c0 + stw * 128], xb_[:], r=[xb_], w=[scr["XBCT"]])

            for step in range(-2, sq.nt + 2):
                if 0 <= step + 2 < sq.nt:
                    L0(step + 2)
                if 0 <= step < sq.nt:
                    S1(step)
                if 0 <= step - 1 < sq.nt:
                    S2(step - 1)
                if 0 <= step - 2 < sq.nt:
                    S3(step - 2)
            self.S.barrier()

    def phase_sweep(self, l, sq, fwd, full=True):
        din = self.din
        scr = sq.scr
        with contextlib.ExitStack() as st:
            cw = self.sb(st, "cw", [128, 8, 3])
            cb = self.sb(st, "cb", [128, 8])
            self.dma(cw[:], din["ssd_cw"][l, :, :, :], w=[cw])
            self.dma(cb[:], din["ssd_cb"][l, :, :], w=[cb])
            DG = self.sb(st, "DG", [128, 8, 3, 128], BF16)
            for ch in range(8):
                for tp in range(3):
                    self.V(lambda h, ch=ch, tp=tp: h.tensor_scalar(out=DG[:, ch, tp, :], in0=self.identf[:],
                                                                   scalar1=cw[:, ch, tp:tp + 1], scalar2=None, op0=ALU.mult),
                           [self.identf, cw], [DG])
            tri = self.cast_load(st, "tri", din["tri"][:, :, :], [128, 4, 128])
            if fwd:
                upat = self.cast_load(st, "upat", din["upat"][:, :, :], [128, 16, 128], eng="G")
                negm = self.cast_load(st, "negm", din["negm"][:, :, :], [128, 16, 128], eng="G")
                dtile = self.sb(st, "dtile", [128, 512])
                self.dma(dtile[:], din["d_rep"][l, :, :].partition_broadcast(128), w=[dtile])
            U, NU, Us, ONES = (tri[:, i, :] for i in range(4))
            Sx = self.Sf if fwd else self.Sb
            SxB = self.SfB if fwd else self.SbB
            xin = [self.sb(st, "xin", [128, 8, 130], BF16) for _ in range(4)]
            xc = [self.sb(st, "xc", [128, 8, 128], BF16) for _ in range(2)]
            XB = [self.sb(st, "XB", [128, 768], BF16) for _ in range(2)]
            pc0 = self.ps(st, "pc0", [128, 4, 128])
            pc1 = self.ps(st, "pc1", [128, 4, 128])
            ptr = self.ps(st, "ptr", [128, 1024], BF16)
            psm = self.ps(st, "psm", [128, 512])
            py = self.ps(st, "py", [128, 512])
            arg = self.sb(st, "arg", [128, 24])
            ex = [self.sb(st, "ex", [128, 24]) for _ in range(2)]
            wv = self.sb(st, "wv", [128, 8])
            Xw = [self.sb(st, "Xw", [128, 512], BF16) for _ in range(2)]
            yo = self.sb(st, "yo", [128, 512])
            if fwd:
                pseg = [self.ps(st, "pseg", [128, 4, 128]) for _ in range(3)]
                rhs1 = self.sb(st, "rhs1", [128, 16, 128], BF16)
                LT = self.sb(st, "LT", [128, 16, 128], BF16)
                MT = [self.sb(st, "MT", [128, 16, 128], BF16) for _ in range(2)]
                Xf = [self.sb(st, "Xf", [128, 512], BF16) for _ in range(2)]
                Xb = [self.sb(st, "Xb", [128, 512], BF16) for _ in range(2)]
                XD = [self.sb(st, "XD", [128, 512], BF16) for _ in range(2)]
                ypo = [self.sb(st, "ypo", [128, 512], BF16) for _ in range(2)]
            else:
                ypi = [self.sb(st, "ypi", [128, 512], BF16) for _ in range(4)]
                szi = [self.sb(st, "szi", [128, 512], BF16) for _ in range(4)]
                yz = self.sb(st, "yz", [128, 512])
                sqj = self.sb(st, "sqj", [128, 512])
                ss = self.sb(st, "ss", [128, 2])
                rg = self.sb(st, "rg", [128, 2])
                yn = [self.sb(st, "yn", [128, 512], BF16) for _ in range(2)]
            xbd = scr["XBCT"].t.rearrange("(c p) n -> p c n", p=128)
            order = list(range(sq.nt)) if fwd else list(range(sq.nt - 1, -1, -1))

            def bc8(ap):
                return ap.unsqueeze(2).to_broadcast([128, 8, 64])

            def Lst(i):
                t = order[i]
                xi = xin[i % 4]
                lo_, hi_ = max(t * 128 - 1, 0), min(t * 128 + 129, sq.L)
                o_ = lo_ - (t * 128 - 1)
                if o_ > 0:
                    self.G(lambda h, xi=xi: h.memset(xi[:, :, 0:1], 0.0), [], [xi])
                if hi_ < t * 128 + 129:
                    self.G(lambda h, xi=xi: h.memset(xi[:, :, 129:130], 0.0), [], [xi])
                self.dma(xi[:, :, o_:o_ + hi_ - lo_], xbd[:, :, lo_:hi_], r=[scr["XBCT"]], w=[xi])
                if (not fwd) and full:
                    self.dma(ypi[i % 4][:], scr["YP"][t * 128:(t + 1) * 128, :], r=[scr["YP"]], w=[ypi[i % 4]])
                    self.dma(szi[i % 4][:], scr["SZ"][t * 128:(t + 1) * 128, :], r=[scr["SZ"]], w=[szi[i % 4]])

            def Pst(i, part):
                t = order[i]
                k = i % 2
                tt = sq.toff + t
                xi, xc_, XB_, ex_, Xw_ = xin[i % 4], xc[k], XB[k], ex[k], Xw[k]
                if part == 1:
                    if fwd:
                        MT_ = MT[k]
                        for g in range(2):
                            self.mm(psm, psm[:, 128 + g * 128:256 + g * 128], xc_[:, 4 + g, :], xc_[:, 6 + g, :], True, True, [xc_])
                        for q in range(4):
                            pq = pseg[q % 3]
                            lt2 = NU if q < 2 else Us
                            self.mm(pq, pq[:], ONES, rhs1[:, 4 * q:4 * q + 4, :], True, False, [tri, rhs1])
                            self.mm(pq, pq[:], lt2, self.dAb[:, tt, 4 * q:4 * q + 4].unsqueeze(2).to_broadcast([128, 4, 128]), False, False, [tri, self.dAb])
                            self.mm(pq, pq[:], self.identb[:], negm[:, 4 * q:4 * q + 4, :], False, True, [self.identb, negm])
                            self.A(lambda h, pq=pq, q=q: h.activation(out=LT[:, 4 * q:4 * q + 4, :], in_=pq[:], func=AF.Exp), [pq], [LT])
                            g = q % 2
                            self.V(lambda h, q=q, g=g, MT_=MT_: h.tensor_tensor(
                                out=MT_[:, 4 * q:4 * q + 4, :], in0=LT[:, 4 * q:4 * q + 4, :],
                                in1=psm[:, 128 + g * 128:256 + g * 128].unsqueeze(1).to_broadcast([128, 4, 128]), op=ALU.mult),
                                [LT, psm], [MT_])
                    return
                for ch in range(8):
                    pc = pc0 if ch < 4 else pc1
                    for tp in range(3):
                        self.mm(pc, pc[:, ch % 4, :], DG[:, ch, tp, :], xi[:, ch, tp:tp + 128], tp == 0, tp == 2, [DG, xi])
                for ch in range(8):
                    pc = pc0 if ch < 4 else pc1
                    self.A(lambda h, pc=pc, ch=ch, xc_=xc_: h.activation(out=xc_[:, ch, :], in_=pc[:, ch % 4, :], func=AF.Silu,
                                                                        bias=cb[:, ch:ch + 1]), [pc, cb], [xc_])
                dA_t = self.dAb[:, tt, :]
                self.mm(psm, psm[:, 0:16], U if fwd else Us, dA_t, True, True, [tri, self.dAb])
                self.mm(psm, psm[:, 16:32], ONES, dA_t, True, True, [tri, self.dAb])
                for ch in range(6):
                    self.P(lambda h, ch=ch, xc_=xc_: h.transpose(out=ptr[:, ch * 128:(ch + 1) * 128], in_=xc_[:, ch, :],
                                                                identity=self.identb[:]), [xc_, self.identb], [ptr])
                o = 0 if fwd else 8
                self.V(lambda h, o=o: h.tensor_copy(out=arg[:, 0:8], in_=psm[:, o:o + 8]), [psm], [arg])
                self.V(lambda h, o=o: h.tensor_copy(out=arg[:, 8:16], in_=psm[:, 16 + o:24 + o]), [psm], [arg])
                self.V(lambda h, o=o: h.tensor_tensor(out=arg[:, 16:24], in0=psm[:, 16 + o:24 + o], in1=arg[:, 0:8],
                                                      op=ALU.subtract), [psm, arg], [arg])
                self.A(lambda h, XB_=XB_: h.copy(out=XB_[:], in_=ptr[:, 0:768]), [ptr], [XB_])
                self.A(lambda h, ex_=ex_: h.activation(out=ex_[:], in_=arg[:], func=AF.Exp), [arg], [ex_])
                dt_d = self.DT[:, tt, o:o + 8]
                if fwd:
                    self.V(lambda h, dt_d=dt_d, ex_=ex_: h.tensor_tensor(out=wv[:], in0=ex_[:, 16:24], in1=dt_d, op=ALU.mult), [ex_, self.DT], [wv])
                else:
                    self.V(lambda h, dt_d=dt_d, ex_=ex_: h.tensor_tensor(out=wv[:], in0=ex_[:, 0:8], in1=dt_d, op=ALU.mult), [ex_, self.DT], [wv])
                X3 = XB_[:, 0:512].rearrange("p (a b) -> p a b", a=8)
                self.V(lambda h, X3=X3, Xw_=Xw_: h.tensor_tensor(out=Xw_[:].rearrange("p (a b) -> p a b", a=8), in0=X3, in1=bc8(wv[:]), op=ALU.mult),
                       [XB_, wv], [Xw_])
                if fwd:
                    MT_, Xf_, Xb_, XD_ = MT[k], Xf[k], Xb[k], XD[k]
                    dtf = self.DT[:, tt, 0:8]
                    dtb_ = self.DT[:, tt, 8:16]
                    self.G(lambda h, tt=tt: h.tensor_tensor(out=rhs1[:], in0=upat[:], in1=self.dAb[:, tt, :].unsqueeze(2).to_broadcast([128, 16, 128]),
                                                            op=ALU.mult), [upat, self.dAb], [rhs1])
                    self.V(lambda h, X3=X3, dtf=dtf, Xf_=Xf_: h.tensor_tensor(out=Xf_[:].rearrange("p (a b) -> p a b", a=8), in0=X3, in1=bc8(dtf), op=ALU.mult),
                           [XB_, self.DT], [Xf_])
                    self.G(lambda h, X3=X3, dtb_=dtb_, Xb_=Xb_: h.tensor_tensor(out=Xb_[:].rearrange("p (a b) -> p a b", a=8), in0=X3, in1=bc8(dtb_), op=ALU.mult),
                           [XB_, self.DT], [Xb_])
                    self.G(lambda h, XB_=XB_, XD_=XD_: h.tensor_tensor(out=XD_[:], in0=XB_[:, 0:512], in1=dtile[:], op=ALU.mult), [XB_, dtile], [XD_])

            def Qst(i):
                t = order[i]
                k = i % 2
                xc_, XB_, ex_, Xw_ = xc[k], XB[k], ex[k], Xw[k]
                ysc = ex_[:, 0:8] if fwd else ex_[:, 16:24]
                if fwd:
                    MT_, Xf_, Xb_, XD_ = MT[k], Xf[k], Xb[k], XD[k]
                    self.mm(py, py[:], self.identb[:], XD_[:], True, False, [self.identb, XD_])
                    for hd in range(8):
                        self.mm(py, py[:, hd * 64:(hd + 1) * 64], MT_[:, hd, :], Xf_[:, hd * 64:(hd + 1) * 64], False, False, [MT_, Xf_])
                        self.mm(py, py[:, hd * 64:(hd + 1) * 64], MT_[:, 8 + hd, :], Xb_[:, hd * 64:(hd + 1) * 64], False, True, [MT_, Xb_])
                if fwd or full:
                    po = pc0
                    for g in range(2):
                        self.mm(po, po[:].rearrange("p a b -> p (a b)")[:, g * 256:(g + 1) * 256], xc_[:, 6 + g, :],
                                SxB[:, g * 256:(g + 1) * 256], True, True, [xc_, SxB])
                pd = pc1
                for g in range(2):
                    self.mm(pd, pd[:].rearrange("p a b -> p (a b)")[:, g * 256:(g + 1) * 256], XB_[:, 512 + g * 128:640 + g * 128],
                            Xw_[:, g * 256:(g + 1) * 256], True, True, [XB_, Xw_])
                if fwd or full:
                    self.V(lambda h, po=po, ysc=ysc: h.tensor_tensor(out=yo[:].rearrange("p (a b) -> p a b", a=8),
                                                                     in0=po[:].rearrange("p a (c b) -> p (a c) b", b=64),
                                                                     in1=bc8(ysc), op=ALU.mult), [po, ex_], [yo])
                self.V(lambda h, ex_=ex_: h.tensor_tensor(out=Sx[:].rearrange("p (a b) -> p a b", a=8), in0=Sx[:].rearrange("p (a b) -> p a b", a=8),
                                                          in1=bc8(ex_[:, 8:16]), op=ALU.mult), [Sx, ex_], [Sx])
                self.V(lambda h, pd=pd: h.tensor_tensor(out=Sx[:], in0=Sx[:], in1=pd[:].rearrange("p a b -> p (a b)"), op=ALU.add), [Sx, pd], [Sx])
                self.A(lambda h: h.copy(out=SxB[:], in_=Sx[:]), [Sx], [SxB])
                if fwd:
                    self.V(lambda h, k=k: h.tensor_tensor(out=ypo[k][:], in0=py[:], in1=yo[:], op=ALU.add), [py, yo], [ypo[k]])
                    self.dma(scr["YP"][t * 128:(t + 1) * 128, :], ypo[k][:], r=[ypo[k]], w=[scr["YP"]])
                elif full:
                    k4 = i % 4
                    self.V(lambda h, k4=k4: h.tensor_tensor(out=yz[:], in0=yo[:], in1=ypi[k4][:], op=ALU.add), [yo, ypi[k4]], [yz])
                    self.V(lambda h, k4=k4: h.tensor_tensor(out=yz[:], in0=yz[:], in1=szi[k4][:], op=ALU.mult), [yz, szi[k4]], [yz])
                    for g in range(2):
                        self.A(lambda h, g=g: h.activation(out=sqj[:, g * 256:(g + 1) * 256], in_=yz[:, g * 256:(g + 1) * 256],
                                                           func=AF.Square, accum_out=ss[:, g:g + 1]), [yz], [sqj, ss])
                    self.rstd(ss[:], rg, rg[:], [ss], scale=1.0 / 256.0)
                    for g in range(2):
                        self.V(lambda h, g=g, k=k: h.tensor_scalar(out=yn[k][:, g * 256:(g + 1) * 256], in0=yz[:, g * 256:(g + 1) * 256],
                                                                  scalar1=rg[:, g:g + 1], scalar2=None, op0=ALU.mult), [yz, rg], [yn[k]])
                    self.dma(scr["MIX"][t * 128:(t + 1) * 128, 0:512], yn[k][:], r=[yn[k]], w=[scr["MIX"]])

            n_ = len(order)
            for step in range(-2, n_ + 1):
                if 0 <= step + 2 < n_:
                    Lst(step + 2)
                if 0 <= step < n_:
                    Pst(step, 0)
                if 1 <= step:
                    Qst(step - 1)
                if 0 <= step < n_:
                    Pst(step, 1)
            self.S.barrier()

    def phase_F_lat(self, l, sq):
        din = self.din
        scr = sq.scr
        with contextlib.ExitStack() as st:
            dA_ = self.cast_load(st, "dftA", din["dftA"][:, :, :], [128, 2, 256])
            dB_ = self.cast_load(st, "dftB", din["dftB"][:, :, :], [64, 2, 64])
            tw = self.sb(st, "tw", [64, 3, 128])
            self.dma(tw[:], din["twid"][:, :, :], w=[tw])
            Gt = [self.sb(st, "Gt", [128, 64, 128], BF16) for _ in range(2)]
            YF = self.sb(st, "YF", [128, 64, 256], BF16)
            pa = [self.ps(st, "pa", [64, 2, 2, 128]) for _ in range(3)]
            pb = [self.ps(st, "pb", [128, 8, 64]) for _ in range(2)]
            At = [self.sb(st, "At", [64, 2, 2, 128]) for _ in range(2)]
            Bt = [self.sb(st, "Bt", [64, 2, 2, 128]) for _ in range(2)]
            Yp = [self.sb(st, "Yp", [64, 2, 2, 128], BF16) for _ in range(3)]
            ufv = scr["UF"].t.rearrange("(a b) c -> a b c", b=64)

            def FA(n):
                g, cp = divmod(n, 32)
                G_ = Gt[g % 2]
                if cp == 0:
                    self.dma(G_[:], ufv[:, :, g * 128:(g + 1) * 128], r=[scr["UF"]], w=[G_])
                pa_ = pa[n % 3]
                for ch in range(2):
                    d_ = 2 * cp + ch
                    self.mm(pa_, pa_[:, ch, :, :].rearrange("p a b -> p (a b)"), G_[:, :, d_], dA_[:, 0, :], True, False, [G_, dA_])
                    self.mm(pa_, pa_[:, ch, :, :].rearrange("p a b -> p (a b)"), G_[:, :, 64 + d_], dA_[:, 1, :], False, True, [G_, dA_])

            def FT(n):
                pa_, At_, Bt_, Yp_ = pa[n % 3], At[n % 2], Bt[n % 2], Yp[n % 3]
                trb = tw[:, 0, :].unsqueeze(1).unsqueeze(1).to_broadcast([64, 2, 2, 128])
                self.V(lambda h, pa_=pa_, At_=At_, trb=trb: h.tensor_tensor(out=At_[:], in0=pa_[:], in1=trb, op=ALU.mult), [pa_, tw], [At_])
                self.V(lambda h, pa_=pa_, Bt_=Bt_: h.tensor_tensor(out=Bt_[:, :, 0, :], in0=pa_[:, :, 1, :],
                                                                  in1=tw[:, 1, :].unsqueeze(1).to_broadcast([64, 2, 128]), op=ALU.mult), [pa_, tw], [Bt_])
                self.V(lambda h, pa_=pa_, Bt_=Bt_: h.tensor_tensor(out=Bt_[:, :, 1, :], in0=pa_[:, :, 0, :],
                                                                  in1=tw[:, 2, :].unsqueeze(1).to_broadcast([64, 2, 128]), op=ALU.mult), [pa_, tw], [Bt_])
                self.G(lambda h, At_=At_, Bt_=Bt_, Yp_=Yp_: h.tensor_tensor(out=Yp_[:], in0=At_[:], in1=Bt_[:], op=ALU.add), [At_, Bt_], [Yp_])

            def FB(n):
                g, cp = divmod(n, 32)
                Yp_ = Yp[n % 3]
                pb_ = pb[(n // 4) % 2]
                for ch in range(2):
                    c8 = (cp % 4) * 2 + ch
                    self.mm(pb_, pb_[:, c8, :], Yp_[:, ch, 0, :], dB_[:, 0, :], True, False, [Yp_, dB_])
                    self.mm(pb_, pb_[:, c8, :], Yp_[:, ch, 1, :], dB_[:, 1, :], False, True, [Yp_, dB_])
                if cp % 4 == 3:
                    c0 = g * 64 + (cp // 4) * 8
                    self.A(lambda h, pb_=pb_, c0=c0: h.copy(out=YF[:, :, c0:c0 + 8].rearrange("p k c -> p c k"), in_=pb_[:]), [pb_], [YF])

            for step in range(128 + 2):
                if step < 128:
                    FA(step)
                if 0 <= step - 1 < 128:
                    FT(step - 1)
                if 0 <= step - 2 < 128:
                    FB(step - 2)
            self.dma(scr["MIX"].t.rearrange("(a b) c -> b a c", b=128)[:, :, 512:768], YF[:], r=[YF], w=[scr["MIX"]])
            self.S.barrier()

    def phase_F_ctx(self, l, sq):
        din = self.din
        scr = sq.scr
        with contextlib.ExitStack() as st:
            dC = self.cast_load(st, "dftC", din["dftC"][:, :, :, :], [128, 2, 2, 256])
            gc = self.sb(st, "gc", [128, 2, 512], BF16)
            self.dma(gc[:], scr["UF"].t.rearrange("(c p) n -> p c n", p=128), r=[scr["UF"]], w=[gc])
            pf = [self.ps(st, "pfc", [128, 256]) for _ in range(2)]
            yf = [self.sb(st, "yfc", [128, 256], BF16) for _ in range(2)]
            for kt in range(2):
                for g in range(4):
                    i = 0
                    for lc in range(2):
                        for ri in range(2):
                            self.mm(pf[kt], pf[kt][:, g * 64:(g + 1) * 64], dC[:, ri, lc, kt * 128:(kt + 1) * 128],
                                    gc[:, lc, g * 128 + ri * 64:g * 128 + ri * 64 + 64], i == 0, i == 3, [dC, gc])
                            i += 1
                self.V(lambda h, kt=kt: h.tensor_copy(out=yf[kt][:], in_=pf[kt][:]), [pf[kt]], [yf[kt]])
                self.dma(scr["MIX"][kt * 128:(kt + 1) * 128, 512:768], yf[kt][:], r=[yf[kt]], w=[scr["MIX"]])
            self.S.barrier()

    def phase_C(self, l, sq, xin):
        din = self.din
        scr = sq.scr
        with contextlib.ExitStack() as st:
            WOUT = self.sb(st, "WOUT", [128, 8, D], BF16)
            self.dma(WOUT[:], self.WOUTd[:, :, :], r=[self.WOUTd], w=[WOUT])
            PT = self.cast_load(st, "poolm", din["poolm"][:, :, :, :], [128, 5, 4, 128])

            def bct(name, src):
                t_ = self.sb(st, name, [128, D])
                self.dma(t_[:], src.partition_broadcast(128), r=[self.MOD], w=[t_])
                return t_
            g1t = bct("g1t", self.MOD[l, sq.row:sq.row + 1, 2048:3072])
            sc2 = bct("sc2", self.MOD[l, sq.row:sq.row + 1, 4096:5120])
            sh2 = bct("sh2", self.MOD[l, sq.row:sq.row + 1, 3072:4096])
            lng = bct("lng", din["ln1_g"][l, :, :])
            lnb = bct("lnb", din["ln1_b"][l, :, :])
            xt = [self.sb(st, "xt", [128, D]) for _ in range(4)]
            mx = [self.sb(st, "mx", [128, 768], BF16) for _ in range(4)]
            upw = [self.sb(st, "upw", [128, 3, 256], BF16) for _ in range(4)]
            nmr = [self.sb(st, "nmr", [128, 1]) for _ in range(2)]
            mixT = [self.sb(st, "mixT", [128, 8, 128], BF16) for _ in range(2)]
            v = [self.sb(st, "v", [128, D]) for _ in range(2)]
            xm = [self.sb(st, "xm", [128, D]) for _ in range(3)]
            hn = [self.sb(st, "hn", [128, D]) for _ in range(2)]
            h2 = [self.sb(st, "h2", [128, D], BF16) for _ in range(2)]
            h2T = [self.sb(st, "h2T", [128, 8, 128], BF16) for _ in range(2)]
            small = [(self.sb(st, "st6", [128, 2, 6]), self.sb(st, "mv", [128, 2]), self.sb(st, "rs", [128, 1])) for _ in range(4)]
            ptm = self.ps(st, "ptm", [128, 8, 128], BF16)
            ppl = self.ps(st, "ppl", [128, 2, 128])
            pout = [self.ps(st, "pout", [128, D]) for _ in range(2)]
            pth = self.ps(st, "pth", [128, 8, 128], BF16)
            h2v = scr["H2T"].t.rearrange("(c p) n -> p c n", p=128)

            def L0(t):
                k4 = t % 4
                self.dma(xt[k4][:], xin.t[t * 128:(t + 1) * 128, :], r=[xin], w=[xt[k4]])
                self.dma(mx[k4][:], scr["MIX"][t * 128:(t + 1) * 128, 0:768], r=[scr["MIX"]], w=[mx[k4]])
                for r_, tn in enumerate((t - 1, t, t + 1)):
                    if 0 <= tn < sq.nt:
                        self.dma(upw[k4][:, r_, :], scr["MIX"][tn * 128:(tn + 1) * 128, 768:1024], r=[scr["MIX"]], w=[upw[k4]])

            def S1(t):
                k = t % 2
                k4 = t % 4
                rels = [r_ for r_, tn in enumerate((t - 1, t, t + 1)) if 0 <= tn < sq.nt]
                for c in range(6):
                    self.P(lambda h, k4=k4, c=c: h.transpose(out=ptm[:, c, :], in_=mx[k4][:, c * 128:(c + 1) * 128], identity=self.identb[:]),
                           [mx[k4], self.identb], [ptm])
                self.A(lambda h, k=k: h.copy(out=mixT[k][:, 0:6, :], in_=ptm[:, 0:6, :]), [ptm], [mixT[k]])
                for g in range(4):
                    for i, r_ in enumerate(rels):
                        ridx = r_
                        if r_ == 1 and t == 0:
                            ridx = 3
                        if r_ == 1 and t == sq.nt - 1:
                            ridx = 4
                        self.mm(ppl, ppl[(g % 2) * 64:(g % 2) * 64 + 64, g // 2, :], upw[k4][:, r_, g * 64:(g + 1) * 64], PT[:, ridx, g, :],
                                i == 0, i == len(rels) - 1, [upw[k4], PT])
                self.V(lambda h, k=k: h.tensor_copy(out=mixT[k][:, 6:8, :], in_=ppl[:]), [ppl], [mixT[k]])

            def S2a(t):
                k = t % 2
                k3 = t % 3
                po = pout[k]
                for half in range(2):
                    for c in range(8):
                        self.mm(po, po[:, half * 512:(half + 1) * 512], mixT[k][:, c, :], WOUT[:, c, half * 512:(half + 1) * 512],
                                c == 0, c == 7, [mixT[k], WOUT])
                self.V(lambda h, k=k, po=po: h.tensor_tensor(out=v[k][:], in0=po[:], in1=g1t[:], op=ALU.mult), [po, g1t], [v[k]])
                x_ = xt[t % 4]
                self.V(lambda h, k=k, x_=x_: h.scalar_tensor_tensor(out=v[k][:], in0=x_[:], scalar=ALPHA, in1=v[k][:], op0=ALU.mult, op1=ALU.add),
                       [x_, v[k]], [v[k]])
                self.layernorm_affine(v[k], v[k][:], v[k], v[k][:], xm[k3], xm[k3][:], lng, lnb, small[k])
                self.dma(scr["XMID"][t * 128:(t + 1) * 128, :], xm[k3][:], r=[xm[k3]], w=[scr["XMID"]])

            def S2b(t):
                k = t % 2
                k3 = t % 3
                st6, mv, rs = small[2 + k]
                x3 = xm[k3]
                for hh in range(2):
                    self.V(lambda h, hh=hh, x3=x3, st6=st6: h.bn_stats(out=st6[:, hh, :], in_=x3[:, hh * 512:(hh + 1) * 512]), [x3], [st6])
                self.V(lambda h, st6=st6, mv=mv: h.bn_aggr(out=mv[:], in_=st6[:].rearrange("p a b -> p (a b)")), [st6], [mv])
                self.rstd(mv[:, 1:2], rs, rs[:], [mv])
                self.V(lambda h, k=k, mv=mv, rs=rs: h.scalar_tensor_tensor(out=nmr[k][:], in0=mv[:, 0:1], scalar=-1.0, in1=rs[:, 0:1],
                                                                        op0=ALU.mult, op1=ALU.mult), [mv, rs], [nmr[k]])
                self.A(lambda h, k=k, x3=x3, rs=rs: h.activation(out=hn[k][:], in_=x3[:], func=AF.Identity, scale=rs[:, 0:1], bias=nmr[k][:]),
                       [x3, rs, nmr[k]], [hn[k]])
                self.G(lambda h, k=k: h.tensor_tensor(out=hn[k][:], in0=hn[k][:], in1=sc2[:], op=ALU.mult), [hn[k], sc2], [hn[k]])
                self.G(lambda h, k=k: h.tensor_tensor(out=h2[k][:], in0=hn[k][:], in1=sh2[:], op=ALU.add), [hn[k], sh2], [h2[k]])

            def S3(t):
                k = t % 2
                for kc in range(8):
                    self.P(lambda h, k=k, kc=kc: h.transpose(out=pth[:, kc, :], in_=h2[k][:, kc * 128:(kc + 1) * 128], identity=self.identb[:]),
                           [h2[k], self.identb], [pth])
                self.A(lambda h, k=k: h.copy(out=h2T[k][:], in_=pth[:]), [pth], [h2T[k]])
                self.dma(h2v[:, :, t * 128:(t + 1) * 128], h2T[k][:], r=[h2T[k]], w=[scr["H2T"]])

            for step in range(-2, sq.nt + 3):
                if 0 <= step + 2 < sq.nt:
                    L0(step + 2)
                if 0 <= step < sq.nt:
                    S1(step)
                if 0 <= step - 1 < sq.nt:
                    S2a(step - 1)
                if 0 <= step - 2 < sq.nt:
                    S2b(step - 2)
                if 0 <= step - 3 < sq.nt:
                    S3(step - 3)
            self.S.barrier()

    def phase_M(self, l, sq, xout):
        din = self.din
        scr = sq.scr
        is_ctx = sq.name == "ctx"
        if is_ctx:
            rows, W, RB = 1, CTXL, 1
            taps = [(0, dx, 3 + (dx + 1)) for dx in (-1, 0, 1)]
        else:
            rows, W, RB = SEQ // GRID_W, GRID_W, 16
            taps = [(dy, dx, (dy + 1) * 3 + (dx + 1)) for dy in (-1, 0, 1) for dx in (-1, 0, 1)]
        nq = max(1, (RB * W) // 512)
        qrows = RB // nq if not is_ctx else 1
        qn = qrows * W
        ntb = (RB * W) // 128
        with contextlib.ExitStack() as st:
            WDN = self.sb(st, "WDN", [128, 22, D], BF16)
            self.dma(WDN[:], self.WDNd[:, :, :], r=[self.WDNd], w=[WDN])
            fcw = self.sb(st, "fcw", [128, 44, 9])
            fcb = self.sb(st, "fcb", [128, 44])
            self.dma(fcw[:], din["ffn_cw"][l, :, :, :], w=[fcw])
            self.dma(fcb[:], din["ffn_cb"][l, :, :], w=[fcb])

            def bct(name, src):
                t_ = self.sb(st, name, [128, D])
                self.dma(t_[:], src.partition_broadcast(128), r=[self.MOD], w=[t_])
                return t_
            g2t = bct("g2t", self.MOD[l, sq.row:sq.row + 1, 5120:6144])
            lng = bct("lng2", din["ln2_g"][l, :, :])
            lnb = bct("lnb2", din["ln2_b"][l, :, :])
            nhr = RB + 2
            hb = [self.sb(st, "hbk", [128, 8, nhr * W], BF16)]
            Abuf = [self.sb(st, "Abuf", [128, nhr, W + 2], BF16) for _ in range(4)]
            for a_ in Abuf:
                self.G(lambda h, a_=a_: h.memset(a_[:], 0.0), [], [a_])
            actT = self.sb(st, "actT", [128, 22, RB * W], BF16)
            wu = [self.sb(st, "wu", [128, 2, 8, 128], BF16) for _ in range(3)]
            dgt = [self.sb(st, "dgt", [128, len(taps), 128], BF16) for _ in range(3)]
            gl = [self.sb(st, "gl", [128, qn]) for _ in range(2)]
            npu = (nhr * W + 511) // 512
            pu = [self.ps(st, "pu", [128, npu * 512]) for _ in range(2)]
            pcv = [self.ps(st, "pcv", [128, 512]) for _ in range(2)]
            xmt = [self.sb(st, "xmt", [128, D]) for _ in range(2)]
            v = [self.sb(st, "vf", [128, D]) for _ in range(2)]
            xo = xmt
            small = [(self.sb(st, "st6", [128, 2, 6]), self.sb(st, "mv", [128, 2]), self.sb(st, "rs", [128, 1])) for _ in range(2)]
            h2v = scr["H2T"].t.rearrange("(c p) n -> p c n", p=128)
            nblk = rows // RB
            iu = 0
            icv = 0
            idg = 0
            for bi in range(nblk):
                r0, r1 = bi * RB, (bi + 1) * RB
                hr0, hr1 = max(r0 - 1, 0), min(r1 + 1, rows)
                nh = hr1 - hr0
                ar0 = hr0 - (r0 - 1)
                hb_ = hb[0]
                self.dma(hb_[:, :, 0:nh * W], h2v[:, :, hr0 * W:hr1 * W], r=[scr["H2T"]], w=[hb_])
                if bi == nblk - 1 and nblk > 1:
                    for a_ in Abuf:
                        self.G(lambda h, a_=a_: h.memset(a_[:, nhr - 1, :], 0.0), [], [a_])
                units = []
                for pr in range(22):
                    for vg in (1, 0):
                        units.append((pr, vg))
                state = {}

                def up_stage(pr, vg, hb_=hb_, nh=nh, ar0=ar0):
                    nonlocal iu, idg
                    wu_ = wu[pr % 3]
                    if vg == 1:
                        self.dma(wu_[:], self.WUPd[pr, :, :, :, :], r=[self.WUPd], w=[wu_])
                    cc = vg * 22 + pr
                    pu_ = pu[iu % 2]
                    iu += 1
                    A_ = Abuf[vg * 2 + (pr % 2)]
                    ntok = nh * W
                    for nb in range((ntok + 511) // 512):
                        n0, n1 = nb * 512, min(ntok, (nb + 1) * 512)
                        for kc in range(8):
                            self.mm(pu_, pu_[:, n0:n1], wu_[:, vg, kc, :], hb_[:, kc, n0:n1], kc == 0, kc == 7, [wu_, hb_])
                    self.A(lambda h, pu_=pu_, A_=A_, ntok=ntok, nh=nh, ar0=ar0: h.copy(
                        out=A_[:, ar0:ar0 + nh, 1:W + 1], in_=pu_[:, 0:ntok].rearrange("p (a b) -> p a b", b=W)), [pu_], [A_])
                    dg_ = dgt[idg % 3]
                    idg += 1
                    nt_ = len(taps)
                    w0 = taps[0][2]
                    self.V(lambda h, dg_=dg_, cc=cc, nt_=nt_, w0=w0: h.tensor_tensor(
                        out=dg_[:], in0=self.identf[:].unsqueeze(1).to_broadcast([128, nt_, 128]),
                        in1=fcw[:, cc, w0:w0 + nt_].unsqueeze(2).to_broadcast([128, nt_, 128]), op=ALU.mult),
                        [self.identf, fcw], [dg_])
                    state[(pr, vg)] = (A_, dg_, cc)

                def conv_stage(pr, vg):
                    nonlocal icv
                    A_, dg_, cc = state[(pr, vg)]
                    for q in range(nq):
                        pc_ = pcv[icv % 2]
                        icv += 1
                        for ti, (dy, dx, widx) in enumerate(taps):
                            ra = 1 + q * qrows + dy
                            self.mm(pc_, pc_[:, 0:qn], dg_[:, ti, :], A_[:, ra:ra + qrows, 1 + dx:1 + dx + W],
                                    ti == 0, ti == len(taps) - 1, [dg_, A_])
                        g_ = gl[q % 2]
                        if vg == 1:
                            self.A(lambda h, pc_=pc_, g_=g_, cc=cc: h.activation(out=g_[:], in_=pc_[:, 0:qn], func=AF.Gelu_apprx_tanh,
                                                                              bias=fcb[:, cc:cc + 1]), [pc_, fcb], [g_])
                        else:
                            self.V(lambda h, pc_=pc_, g_=g_, cc=cc, pr=pr, q=q: h.scalar_tensor_tensor(
                                out=actT[:, pr, q * qn:(q + 1) * qn], in0=pc_[:, 0:qn], scalar=fcb[:, cc:cc + 1], in1=g_[:],
                                op0=ALU.add, op1=ALU.mult), [pc_, fcb, g_], [actT])
                for ui in range(len(units) + 1):
                    if ui < len(units):
                        up_stage(*units[ui])
                    if ui >= 1:
                        conv_stage(*units[ui - 1])
                for tb in range(ntb):
                    t = bi * ntb + tb
                    k = t % 2
                    pd = pu[iu % 2]
                    iu += 1
                    self.dma(xmt[k][:], scr["XMID"][t * 128:(t + 1) * 128, :], r=[scr["XMID"]], w=[xmt[k]])
                    for half in range(2):
                        for fc in range(22):
                            self.mm(pd, pd[:, half * 512:(half + 1) * 512], actT[:, fc, tb * 128:(tb + 1) * 128],
                                    WDN[:, fc, half * 512:(half + 1) * 512], fc == 0, fc == 21, [actT, WDN])
                    self.V(lambda h, k=k, pd=pd: h.tensor_tensor(out=v[k][:], in0=pd[:, 0:D], in1=g2t[:], op=ALU.mult), [pd, g2t], [v[k]])
                    self.V(lambda h, k=k: h.scalar_tensor_tensor(out=v[k][:], in0=xmt[k][:], scalar=ALPHA, in1=v[k][:], op0=ALU.mult, op1=ALU.add),
                           [xmt[k], v[k]], [v[k]])
                    self.layernorm_affine(v[k], v[k][:], v[k], v[k][:], xo[k], xo[k][:], lng, lnb, small[k])
                    self.dma(xout.t[t * 128:(t + 1) * 128, :], xo[k][:], r=[xo[k]], w=[xout])
            self.S.barrier()


_PROG_CACHE = {}


def _get_prog():
    if "nc" not in _PROG_CACHE:
        p = Prog()
        _PROG_CACHE["nc"] = p.build()
    return _PROG_CACHE["nc"]


def kernel(**inputs):
    nc = _get_prog()
    in_maps = [_host_inputs(inputs, b) for b in range(NCORES)]
    res = run_bass_kernel_spmd(nc, in_maps, core_ids=list(range(NCORES)))
    out = np.stack([np.asarray(res.results[b]["y"], dtype=np.float32) for b in range(NCORES)], 0)
    return out
```
